# Optimizing a Trainium2 kernel written in Bass

```python
import math
import jax, jax.numpy as jnp
from jax import lax
import numpy as np

D_MODEL = 1024
BATCH = 16
SEQ = 256
DEPTH = 2
DEC_BATCH = 2
DEC_SEQ = 4096
PAST_LEN = 256

GRID_W = 64
HEAD_DIM = 64
SSM_WIDTH = D_MODEL // 4
SSM_GROUP_CH = 16
SSM_GROUPS = SSM_WIDTH // SSM_GROUP_CH
SSM_STATE = 64
NA_HEADS = (D_MODEL // 4) // HEAD_DIM
NA_WIDTH = NA_HEADS * HEAD_DIM
NA_WIN_R = 8
NA_WIN_C = 16
NA_QCB = 16
NA_KCB = NA_QCB + NA_WIN_C
NA_NCB = GRID_W // NA_QCB
GQA_HEADS = (D_MODEL // 2) // HEAD_DIM
GQA_KV_HEADS = 2
GQA_REP = GQA_HEADS // GQA_KV_HEADS
GQA_WIDTH = GQA_HEADS * HEAD_DIM
GQA_KV_WIDTH = GQA_KV_HEADS * HEAD_DIM
MIX_WIDTH = SSM_WIDTH + NA_WIDTH + GQA_WIDTH
IN_WIDTH = SSM_WIDTH + 3 * NA_WIDTH + GQA_WIDTH + 2 * GQA_KV_WIDTH
D_FF = ((8 * D_MODEL + 3 * 256 - 1) // (3 * 256)) * 256
ROPE_THETA = 10000.0
Q_BLOCK = 128
LN_EPS = 1e-6
RMS_EPS = 1e-6
DEEPNORM_ALPHA = (2 * DEPTH) ** 0.25
DEEPNORM_BETA = (8 * DEPTH) ** -0.25

kernel_name = 'hybrid_s5_natten_gqa_diffusion_step'


def layer_norm(x, g=None, b=None):
    xf = x.astype(jnp.float32)
    mu = jnp.mean(xf, axis=-1, keepdims=True)
    var = jnp.mean(jnp.square(xf - mu), axis=-1, keepdims=True)
    y = (xf - mu) * lax.rsqrt(var + LN_EPS)
    if g is not None:
        y = y * g.astype(jnp.float32) + b.astype(jnp.float32)
    return y.astype(x.dtype)


def rms_norm_heads(x, g):
    xf = x.astype(jnp.float32)
    y = xf * lax.rsqrt(jnp.mean(jnp.square(xf), axis=-1, keepdims=True) + RMS_EPS) * g.astype(jnp.float32)
    return y.astype(x.dtype)


def rope_2d(x):
    n = x.shape[1]
    nf = HEAD_DIM // 4
    t = jnp.arange(n)
    inv = ROPE_THETA ** (-jnp.arange(nf, dtype=jnp.float32) / nf)
    rows = (t // GRID_W).astype(jnp.float32)
    cols = (t % GRID_W).astype(jnp.float32)
    ang = jnp.stack([rows[:, None] * inv, cols[:, None] * inv], axis=1)
    ang = ang.reshape((1, n) + (1,) * (x.ndim - 3) + (2, nf))
    cos, sin = jnp.cos(ang), jnp.sin(ang)
    xr = x.astype(jnp.float32).reshape(x.shape[:-1] + (2, 2, nf))
    x1, x2 = xr[..., 0, :], xr[..., 1, :]
    out = jnp.stack([x1 * cos - x2 * sin, x2 * cos + x1 * sin], axis=-2)
    return out.reshape(x.shape).astype(x.dtype)


def attend_dense(q, k, v):
    b, lq = q.shape[0], q.shape[1]
    qb = min(Q_BLOCK, lq)
    nb = lq // qb
    qs = jnp.moveaxis(q.reshape((b, nb, qb) + q.shape[2:]), 1, 0)
    scale = HEAD_DIM ** -0.5

    def block(qi):
        s = jnp.einsum('bqgrd,bkgd->bgrqk', qi, k).astype(jnp.float32) * scale
        p = jax.nn.softmax(s, axis=-1).astype(v.dtype)
        return jnp.einsum('bgrqk,bkgd->bqgrd', p, v)

    o = lax.map(block, qs)
    return jnp.moveaxis(o, 0, 1).reshape(q.shape)


def na_indices(rows):
    wr = min(NA_WIN_R, rows)
    r = np.arange(rows)
    rs = np.clip(r - wr // 2, 0, rows - wr)
    key_rows = rs[:, None] + np.arange(wr)[None, :]
    cb = np.arange(NA_NCB)
    kcs = np.clip(cb * NA_QCB - NA_WIN_C // 2, 0, GRID_W - NA_KCB)
    key_cols = kcs[:, None] + np.arange(NA_KCB)[None, :]
    q_cols = cb[:, None] * NA_QCB + np.arange(NA_QCB)[None, :]
    cs = np.clip(q_cols - NA_WIN_C // 2, 0, GRID_W - NA_WIN_C)
    col_valid = (key_cols[:, None, :] >= cs[:, :, None]) & (key_cols[:, None, :] < cs[:, :, None] + NA_WIN_C)
    kidx = key_rows[:, None, :, None] * GRID_W + key_cols[None, :, None, :]
    dr_idx = key_rows - r[:, None] + (NA_WIN_R - 1)
    dc_idx = np.clip(key_cols[:, None, :] - q_cols[:, :, None], -(NA_WIN_C - 1), NA_WIN_C - 1) + (NA_WIN_C - 1)
    return wr, kidx, col_valid, dr_idx, dc_idx


def na_latent(q, k, v, ck, cv, bias_tab):
    b, n, h, dh = q.shape
    rows = n // GRID_W
    wr, kidx, col_valid, dr_idx, dc_idx = na_indices(rows)
    scale = HEAD_DIM ** -0.5
    kg = k[:, kidx]
    vg = v[:, kidx]
    qg = q.reshape(b, rows, NA_NCB, NA_QCB, h, dh)
    s_loc = jnp.einsum('brnqhd,brnwkhd->bhrnqwk', qg, kg).astype(jnp.float32) * scale
    bias = bias_tab.astype(jnp.float32)[:, dr_idx[:, None, None, :, None], dc_idx[None, :, :, None, :]]
    s_loc = jnp.where(col_valid[None, None, None, :, :, None, :], s_loc + bias[None], -jnp.inf)
    n_loc = wr * NA_KCB
    s_loc = s_loc.reshape(b, h, rows, NA_NCB, NA_QCB, n_loc)
    s_ctx = jnp.einsum('brnqhd,bchd->bhrnqc', qg, ck).astype(jnp.float32) * scale
    p = jax.nn.softmax(jnp.concatenate([s_loc, s_ctx], axis=-1), axis=-1).astype(v.dtype)
    p_loc = p[..., :n_loc].reshape(b, h, rows, NA_NCB, NA_QCB, wr, NA_KCB)
    p_ctx = p[..., n_loc:]
    o = jnp.einsum('bhrnqwk,brnwkhd->brnqhd', p_loc, vg) + jnp.einsum('bhrnqc,bchd->brnqhd', p_ctx, cv)
    return o.reshape(b, n, h * dh)


def ssm_discretize(lam_re, lam_im, log_dt, b_re, b_im):
    lam = lax.complex(jnp.minimum(lam_re.astype(jnp.float32), -1e-4), lam_im.astype(jnp.float32))
    dt = jnp.exp(log_dt.astype(jnp.float32))[:, None]
    lam_bar = jnp.exp(lam * dt)
    b_bar = ((lam_bar - 1.0) / lam)[..., None] * lax.complex(b_re.astype(jnp.float32), b_im.astype(jnp.float32))
    return lam_bar, b_bar


def _ssm_combine(e1, e2):
    a1, b1 = e1
    a2, b2 = e2
    return a1 * a2, a2 * b1 + b2


def ssm_scan(u, lam_bar, b_bar, h0, reverse):
    bu = jnp.einsum('blgc,gpc->blgp', u.astype(jnp.complex64), b_bar)
    first = -1 if reverse else 0
    bu = bu.at[:, first].add(lam_bar * h0)
    a = jnp.broadcast_to(lam_bar, bu.shape)
    _, hs = lax.associative_scan(_ssm_combine, (a, bu), reverse=reverse, axis=1)
    return hs


def ssm_mixer(u, lp, h0_f, h0_b):
    b, n, _ = u.shape
    uf = u.astype(jnp.float32).reshape(b, n, SSM_GROUPS, SSM_GROUP_CH)
    y = uf * lp['ssm_d'].astype(jnp.float32).reshape(SSM_GROUPS, SSM_GROUP_CH)
    finals = []
    for d, h0, rev in ((0, h0_f, False), (1, h0_b, True)):
        lam_bar, b_bar = ssm_discretize(lp['ssm_lam_re'][d], lp['ssm_lam_im'][d], lp['ssm_log_dt'][d],
                                        lp['ssm_b_re'][d], lp['ssm_b_im'][d])
        hs = ssm_scan(uf, lam_bar, b_bar, h0, rev)
        c_mat = lax.complex(lp['ssm_c_re'][d].astype(jnp.float32), lp['ssm_c_im'][d].astype(jnp.float32))
        y = y + jnp.einsum('blgp,gcp->blgc', hs, c_mat).real
        finals.append(hs[:, 0] if rev else hs[:, -1])
    g = jax.nn.gelu(y.reshape(b, n, SSM_WIDTH))
    out = g * jax.nn.sigmoid(g @ lp['w_ssm_glu'].astype(jnp.float32))
    return out.astype(u.dtype), jnp.stack(finals, axis=1)


def mixing(h, lp, ctx):
    b, n, _ = h.shape
    sizes = [SSM_WIDTH, NA_WIDTH, NA_WIDTH, NA_WIDTH, GQA_WIDTH, GQA_KV_WIDTH, GQA_KV_WIDTH]
    cuts = [int(s) for s in np.cumsum(sizes)[:-1]]
    z = h @ lp['w_in']
    u, q_na, k_na, v_na, q_g, k_g, v_g = jnp.split(z, cuts, axis=-1)
    q_na = q_na.reshape(b, n, NA_HEADS, HEAD_DIM)
    k_na = k_na.reshape(b, n, NA_HEADS, HEAD_DIM)
    v_na = v_na.reshape(b, n, NA_HEADS, HEAD_DIM)
    q_g = rms_norm_heads(q_g.reshape(b, n, GQA_KV_HEADS, GQA_REP, HEAD_DIM), lp['q_norm_g'])
    k_g = rms_norm_heads(k_g.reshape(b, n, GQA_KV_HEADS, HEAD_DIM), lp['k_norm_g'])
    v_g = v_g.reshape(b, n, GQA_KV_HEADS, HEAD_DIM)
    if ctx is None:
        h0 = jnp.zeros((b, SSM_GROUPS, SSM_STATE), jnp.complex64)
        ssm_out, st = ssm_mixer(u, lp, h0, h0)
        na_out = attend_dense(q_na[:, :, :, None, :], k_na, v_na).reshape(b, n, NA_WIDTH)
        gqa_out = attend_dense(q_g, k_g, v_g).reshape(b, n, GQA_WIDTH)
        new_ctx = (k_na, v_na, k_g, v_g, st.real, st.imag)
    else:
        ck_na, cv_na, ck_g, cv_g, st_re, st_im = ctx
        st = lax.complex(st_re.astype(jnp.float32), st_im.astype(jnp.float32))
        ssm_out, _ = ssm_mixer(u, lp, st[:, 0], st[:, 1])
        na_out = na_latent(q_na, k_na, v_na, ck_na, cv_na, lp['na_bias'])
        q_g = rope_2d(q_g)
        k_g = rope_2d(k_g)
        k_all = jnp.concatenate([ck_g.astype(k_g.dtype), k_g], axis=1)
        v_all = jnp.concatenate([cv_g.astype(v_g.dtype), v_g], axis=1)
        gqa_out = attend_dense(q_g, k_all, v_all).reshape(b, n, GQA_WIDTH)
        new_ctx = None
    o = jnp.concatenate([ssm_out, na_out.astype(h.dtype), gqa_out.astype(h.dtype)], axis=-1) @ lp['w_out']
    return o, new_ctx


def trunk_layer(x, cond, lp, ctx):
    m = jax.nn.silu(cond.astype(jnp.float32)) @ lp['w_ada'].astype(jnp.float32) + lp['b_ada'].astype(jnp.float32)
    m = m.astype(x.dtype)[:, None, :]
    sh1, sc1, g1, sh2, sc2, g2 = jnp.split(m, 6, axis=-1)
    h = layer_norm(x) * (1.0 + sc1) + sh1
    o, new_ctx = mixing(h, lp, ctx)
    x = layer_norm(DEEPNORM_ALPHA * x + g1 * o, lp['ln1_g'], lp['ln1_b'])
    h = layer_norm(x) * (1.0 + sc2) + sh2
    a, gate = jnp.split(h @ lp['w_ffn_in'], 2, axis=-1)
    f = (jax.nn.silu(a) * gate) @ lp['w_ffn_out']
    x = layer_norm(DEEPNORM_ALPHA * x + g2 * f, lp['ln2_g'], lp['ln2_b'])
    return x, new_ctx


def setup_inputs(seed: int = 0) -> dict:
    key = jax.random.key(seed)
    ks = jax.random.split(key, 40)
    f32 = jnp.float32
    nrm = lambda k, s, sc: jax.random.normal(k, s, f32) * sc
    lam_im = jnp.pi * jnp.arange(SSM_STATE, dtype=f32)
    return {
        'x_prompt': nrm(ks[0], (BATCH, SEQ, D_MODEL), 1.0),
        'x_sample': nrm(ks[1], (DEC_BATCH, DEC_SEQ, D_MODEL), 1.0),
        'c': nrm(ks[2], (DEC_BATCH, D_MODEL), 1.0),
        'cache_na_k': nrm(ks[3], (DEC_BATCH, DEPTH, PAST_LEN, NA_HEADS, HEAD_DIM), 1.0),
        'cache_na_v': nrm(ks[4], (DEC_BATCH, DEPTH, PAST_LEN, NA_HEADS, HEAD_DIM), 1.0),
        'cache_gqa_k': nrm(ks[5], (DEC_BATCH, DEPTH, PAST_LEN, GQA_KV_HEADS, HEAD_DIM), 1.0),
        'cache_gqa_v': nrm(ks[6], (DEC_BATCH, DEPTH, PAST_LEN, GQA_KV_HEADS, HEAD_DIM), 1.0),
        'state_ssm_re': nrm(ks[7], (DEC_BATCH, DEPTH, 2, SSM_GROUPS, SSM_STATE), 0.1),
        'state_ssm_im': nrm(ks[8], (DEC_BATCH, DEPTH, 2, SSM_GROUPS, SSM_STATE), 0.1),
        'c_ctx': nrm(ks[9], (D_MODEL,), 1.0),
        'w_ada': nrm(ks[10], (DEPTH, D_MODEL, 6 * D_MODEL), 0.5 * D_MODEL ** -0.5),
        'b_ada': nrm(ks[11], (DEPTH, 6 * D_MODEL), 0.01),
        'w_in': nrm(ks[12], (DEPTH, D_MODEL, IN_WIDTH), D_MODEL ** -0.5),
        'w_out': nrm(ks[13], (DEPTH, MIX_WIDTH, D_MODEL), DEEPNORM_BETA * MIX_WIDTH ** -0.5),
        'q_norm_g': 1.0 + nrm(ks[14], (DEPTH, HEAD_DIM), 0.02),
        'k_norm_g': 1.0 + nrm(ks[15], (DEPTH, HEAD_DIM), 0.02),
        'na_bias': nrm(ks[16], (DEPTH, NA_HEADS, 2 * NA_WIN_R - 1, 2 * NA_WIN_C - 1), 0.1),
        'ssm_lam_re': -0.5 + nrm(ks[17], (DEPTH, 2, SSM_GROUPS, SSM_STATE), 0.01),
        'ssm_lam_im': lam_im + nrm(ks[18], (DEPTH, 2, SSM_GROUPS, SSM_STATE), 0.01),
        'ssm_log_dt': jax.random.uniform(ks[19], (DEPTH, 2, SSM_GROUPS), f32, math.log(1e-3), math.log(1e-1)),
        'ssm_b_re': nrm(ks[20], (DEPTH, 2, SSM_GROUPS, SSM_STATE, SSM_GROUP_CH), (2 * SSM_GROUP_CH) ** -0.5),
        'ssm_b_im': nrm(ks[21], (DEPTH, 2, SSM_GROUPS, SSM_STATE, SSM_GROUP_CH), (2 * SSM_GROUP_CH) ** -0.5),
        'ssm_c_re': nrm(ks[22], (DEPTH, 2, SSM_GROUPS, SSM_GROUP_CH, SSM_STATE), SSM_STATE ** -0.5),
        'ssm_c_im': nrm(ks[23], (DEPTH, 2, SSM_GROUPS, SSM_GROUP_CH, SSM_STATE), SSM_STATE ** -0.5),
        'ssm_d': nrm(ks[24], (DEPTH, SSM_WIDTH), 1.0),
        'w_ssm_glu': nrm(ks[25], (DEPTH, SSM_WIDTH, SSM_WIDTH), SSM_WIDTH ** -0.5),
        'ln1_g': 1.0 + nrm(ks[26], (DEPTH, D_MODEL), 0.02),
        'ln1_b': nrm(ks[27], (DEPTH, D_MODEL), 0.02),
        'ln2_g': 1.0 + nrm(ks[28], (DEPTH, D_MODEL), 0.02),
        'ln2_b': nrm(ks[29], (DEPTH, D_MODEL), 0.02),
        'w_ffn_in': nrm(ks[30], (DEPTH, D_MODEL, 2 * D_FF), D_MODEL ** -0.5),
        'w_ffn_out': nrm(ks[31], (DEPTH, D_FF, D_MODEL), DEEPNORM_BETA * D_FF ** -0.5),
    }


def reference(x_prompt, x_sample, c, cache_na_k, cache_na_v, cache_gqa_k, cache_gqa_v, state_ssm_re, state_ssm_im,
              c_ctx, w_ada, b_ada, w_in, w_out, q_norm_g, k_norm_g, na_bias, ssm_lam_re, ssm_lam_im, ssm_log_dt,
              ssm_b_re, ssm_b_im, ssm_c_re, ssm_c_im, ssm_d, w_ssm_glu, ln1_g, ln1_b, ln2_g, ln2_b,
              w_ffn_in, w_ffn_out):
    y_prompt = x_prompt
    y_sample = x_sample
    na_k, na_v, g_k, g_v, st_re, st_im = [], [], [], [], [], []
    for l in range(DEPTH):
        lp = dict(w_ada=w_ada[l], b_ada=b_ada[l], w_in=w_in[l], w_out=w_out[l], q_norm_g=q_norm_g[l],
                  k_norm_g=k_norm_g[l], na_bias=na_bias[l], ssm_lam_re=ssm_lam_re[l], ssm_lam_im=ssm_lam_im[l],
                  ssm_log_dt=ssm_log_dt[l], ssm_b_re=ssm_b_re[l], ssm_b_im=ssm_b_im[l], ssm_c_re=ssm_c_re[l],
                  ssm_c_im=ssm_c_im[l], ssm_d=ssm_d[l], w_ssm_glu=w_ssm_glu[l], ln1_g=ln1_g[l], ln1_b=ln1_b[l],
                  ln2_g=ln2_g[l], ln2_b=ln2_b[l], w_ffn_in=w_ffn_in[l], w_ffn_out=w_ffn_out[l])
        y_prompt, ctx_out = trunk_layer(y_prompt, c_ctx[None, :], lp, None)
        na_k.append(ctx_out[0]); na_v.append(ctx_out[1]); g_k.append(ctx_out[2]); g_v.append(ctx_out[3])
        st_re.append(ctx_out[4]); st_im.append(ctx_out[5])
        ctx_l = (cache_na_k[:, l], cache_na_v[:, l], cache_gqa_k[:, l], cache_gqa_v[:, l],
                 state_ssm_re[:, l], state_ssm_im[:, l])
        y_sample, _ = trunk_layer(y_sample, c, lp, ctx_l)
    new_na_k = jnp.stack(na_k, axis=1)
    new_na_v = jnp.stack(na_v, axis=1)
    new_gqa_k = jnp.stack(g_k, axis=1)
    new_gqa_v = jnp.stack(g_v, axis=1)
    new_state_re = jnp.stack(st_re, axis=1)
    new_state_im = jnp.stack(st_im, axis=1)
    return (y_prompt, y_sample, new_na_k, new_na_v, new_gqa_k, new_gqa_v, new_state_re, new_state_im)
```

```python
import math
import numpy as np
import concourse.bass as bass
import concourse.mybir as mybir
from concourse.bass_utils import run_bass_kernel_spmd
from contextlib import ExitStack

F32 = mybir.dt.float32
BF16 = mybir.dt.bfloat16
AF = mybir.ActivationFunctionType
ALU = mybir.AluOpType
AX = mybir.AxisListType

D = 1024
DEPTH = 2
SEQ = 256
NPT = 512
IN_W = 1792
DFF = 2816
ALPHA = (2 * DEPTH) ** 0.25
LN_EPS = 1e-6
RMS_EPS = 1e-6
TWO_PI = 2.0 * math.pi
MAGIC = 12582912.0
N_DMA_SEMS = 40
N_SW_SEMS = 12
SAME_ENGINE_SYNC = True


class Res:
    __slots__ = ("name", "w", "r")

    def __init__(self, name):
        self.name = name
        self.w = None
        self.r = []


class Sched:
    def __init__(self, nc, es):
        self.nc = nc
        self.eng = {"pe": nc.tensor, "dve": nc.vector, "act": nc.scalar, "pool": nc.gpsimd, "sp": nc.sync}
        self.sem = {}
        self.cnt = {}
        self.prog = {k: [] for k in self.eng}
        self.seen = {k: {} for k in self.eng}
        for k in self.eng:
            self.sem[k] = es.enter_context(nc.semaphore("s_" + k))
            self.cnt[k] = 0
        self.dsem = [es.enter_context(nc.semaphore("d%d" % i)) for i in range(N_DMA_SEMS)]
        self.dcnt = [0] * N_DMA_SEMS
        self.csem = [es.enter_context(nc.semaphore("c%d" % i)) for i in range(10)]
        self.cnext = 0
        self.dnext = 0
        self.dnext_sw = 0
        self.rs = {}

    def R(self, name):
        r = self.rs.get(name)
        if r is None:
            r = Res(name)
            self.rs[name] = r
        return r

    def _semobj(self, key):
        if isinstance(key, str):
            return self.sem[key]
        if key >= 1000:
            return self.csem[key - 1000]
        return self.dsem[key]

    def coll(self, fn, reads=(), writes=()):
        reads = [self.R(x) if isinstance(x, str) else x for x in reads]
        writes = [self.R(x) if isinstance(x, str) else x for x in writes]
        waits = self._deps("pool", reads, writes)
        k = 1000 + self.cnext
        self.cnext += 1
        tok = (k, 1)
        self.prog["pool"].append((waits, fn, k))
        self._commit(tok, reads, writes)

    def _deps(self, e, reads, writes):
        need = {}

        def add(tok):
            if tok is None:
                return
            k, v = tok
            if k == e and (e == "pe" or not SAME_ENGINE_SYNC):
                return
            if need.get(k, 0) < v:
                need[k] = v
        for r in reads:
            add(r.w)
        for w in writes:
            add(w.w)
            for t in w.r:
                add(t)
        waits = []
        seen = self.seen[e]
        for k, v in need.items():
            if seen.get(k, 0) >= v:
                continue
            seen[k] = v
            waits.append((k, v))
        return waits

    def _commit(self, tok, reads, writes):
        for r in reads:
            if r not in writes:
                r.r.append(tok)
        for w in writes:
            w.w = tok
            w.r = []

    def op(self, e, fn, reads=(), writes=(), sig=True):
        reads = [self.R(x) if isinstance(x, str) else x for x in reads]
        writes = [self.R(x) if isinstance(x, str) else x for x in writes]
        for r in reads:
            if r.name.startswith("psb") and r not in writes:
                writes.append(r)
        waits = self._deps(e, reads, writes)
        if sig:
            self.cnt[e] += 1
            tok = (e, self.cnt[e])
            self.prog[e].append((waits, fn, None))
        else:
            tok = (e, self.cnt[e] + 1)
            self.prog[e].append((waits, fn, "nosig"))
        self._commit(tok, reads, writes)

    def dma(self, q, fn, reads=(), writes=()):
        reads = [self.R(x) if isinstance(x, str) else x for x in reads]
        writes = [self.R(x) if isinstance(x, str) else x for x in writes]
        if q == "pool":
            k = self.dnext_sw
            self.dnext_sw = (self.dnext_sw + 1) % N_SW_SEMS
        else:
            k = N_SW_SEMS + self.dnext
            self.dnext = (self.dnext + 1) % (N_DMA_SEMS - N_SW_SEMS)
        waits = self._deps(q, reads, writes)
        prev = self.dcnt[k]
        if prev > 0 and self.seen[q].get(k, 0) < prev:
            self.seen[q][k] = prev
            waits.append((k, prev))
        self.dcnt[k] += 16
        tok = (k, self.dcnt[k])
        self.prog[q].append((waits, fn, k))
        self._commit(tok, reads, writes)

    def barrier(self):
        for e in self.eng:
            waits = []
            for k in self.eng:
                if k != e and self.cnt[k] > self.seen[e].get(k, 0):
                    self.seen[e][k] = self.cnt[k]
                    waits.append((k, self.cnt[k]))
            for k in range(N_DMA_SEMS):
                if self.dcnt[k] > self.seen[e].get(k, 0):
                    self.seen[e][k] = self.dcnt[k]
                    waits.append((k, self.dcnt[k]))
            self.prog[e].append((waits, None, None))

    def wait_all(self, e, resources):
        resources = [self.R(x) if isinstance(x, str) else x for x in resources]
        waits = self._deps(e, resources, ())
        self.prog[e].append((waits, None, None))

    def run(self, block):
        sch = self

        def mk(ename):
            def body(engine):
                for waits, fn, dk in sch.prog[ename]:
                    for k, v in waits:
                        engine.wait_ge(sch._semobj(k), v)
                    if fn is None:
                        continue
                    inst = fn(engine)
                    if dk is None:
                        inst.then_inc(sch.sem[ename], 1)
                    elif dk == "nosig":
                        pass
                    elif dk >= 1000:
                        inst.then_inc(sch.csem[dk - 1000], 1)
                    else:
                        inst.then_inc(sch.dsem[dk], 16)
            return body
        block.tensor(mk("pe"))
        block.vector(mk("dve"))
        block.scalar(mk("act"))
        block.gpsimd(mk("pool"))
        block.sync(mk("sp"))


def fap(t, off, dims):
    return bass.AP(t, off, [list(d) for d in dims])


class _Stop(Exception):
    pass


NT = 12
NTOK = 1536
BROWS = 896
NEG = -30000.0
NABLEN = 128 + 4 * 15 * 31 + 256


def build_program(kstop=99):
    nc = bass.Bass("TRN2", target_bir_lowering=False)
    stage = {"n": 0}

    def checkpoint():
        stage["n"] += 1
        if stage["n"] >= kstop:
            raise _Stop()

    def din(name, shape, dt=F32):
        return nc.dram_tensor(name, list(shape), dt, kind="ExternalInput").ap()

    def dout(name, shape):
        return nc.dram_tensor(name, list(shape), F32, kind="ExternalOutput").ap()

    xp = din("xp", [NPT, D])
    xs = din("xs", [1024, D])
    condT = din("condT", [128, 8, 2])
    w_ada = din("w_ada", [DEPTH, D, 6 * D])
    b_adaT = din("b_adaT", [DEPTH, 128, 48])
    b_ada = din("b_ada", [DEPTH, 6 * D])
    w_in = din("w_in", [DEPTH, D, IN_W])
    w_out = din("w_out", [DEPTH, D, D])
    g10 = din("g10", [DEPTH, 640])
    lamre = din("lamre", [DEPTH, 128, 32])
    lamim = din("lamim", [DEPTH, 128, 32])
    logdt = din("logdt", [DEPTH, 32])
    bre = din("bre", [DEPTH, 64, 512])
    bim = din("bim", [DEPTH, 64, 512])
    cre = din("cre", [DEPTH, 128, 4, 64])
    cim = din("cim", [DEPTH, 128, 4, 64])
    ssmd = din("ssmd", [DEPTH, 128, 2])
    wglu = din("wglu", [DEPTH, 256, 256])
    ln1g = din("ln1g", [DEPTH, D])
    ln1b = din("ln1b", [DEPTH, D])
    ln2g = din("ln2g", [DEPTH, D])
    ln2b = din("ln2b", [DEPTH, D])
    w_f1 = din("w_f1", [DEPTH, D, 2 * DFF])
    w_f2 = din("w_f2", [DEPTH, DFF, D])
    c_ident = din("c_ident", [128, 128])
    c_sp1 = din("c_sp1", [128, 1024])
    c_rowmask = din("c_rowmask", [128, 8])
    c_colmask = din("c_colmask", [128, 8, 128])
    c_swap = din("c_swap", [128, 128])
    c_sgn = din("c_sgn", [128, 1])
    ropec = din("ropec", [1024, 640])
    ropes = din("ropes", [1024, 640])
    namask_d = din("namask", [128, 16, 8])
    nacol_d = din("nacol", [128, 64])
    jmat_d = din("jmat", [64, 64])
    oh_d = din("oh", [128, 16])
    h0_d = din("h0", [DEPTH, 128, 32])
    cnak = din("cnak", [DEPTH, 256, 256])
    cnav = din("cnav", [DEPTH, 256, 256])
    cgk = din("cgk", [DEPTH, 256, 128])
    cgv = din("cgv", [DEPTH, 256, 128])
    nab = din("nab", [DEPTH, NABLEN])

    yp = dout("yp", [NPT, D])
    ys = dout("ys", [1024, D])
    o_nak = dout("o_nak", [2, DEPTH, SEQ, 256])
    o_nav = dout("o_nav", [2, DEPTH, SEQ, 256])
    o_gk = dout("o_gk", [2, DEPTH, SEQ, 128])
    o_gv = dout("o_gv", [2, DEPTH, SEQ, 128])
    o_sre = dout("o_sre", [2, DEPTH, 32, 64])
    o_sim = dout("o_sim", [2, DEPTH, 32, 64])

    kna_in_t = [nc.dram_tensor("kna_in%d" % l, [256, 1024], BF16) for l in range(DEPTH)]
    kna_out_t = [nc.dram_tensor("kna_out%d" % l, [1024, 1024], BF16) for l in range(DEPTH)]
    kg_in_t = [nc.dram_tensor("kg_in%d" % l, [256, 1024], BF16) for l in range(DEPTH)]
    kg_out_t = [nc.dram_tensor("kg_out%d" % l, [1024, 1024], BF16) for l in range(DEPTH)]
    v_in_t = [nc.dram_tensor("v_in%d" % l, [1024, 384], BF16) for l in range(DEPTH)]
    v_out_t = [nc.dram_tensor("v_out%d" % l, [4096, 384], BF16) for l in range(DEPTH)]
    sgin_t = [nc.dram_tensor("sg_in%d" % l, [128, 32], F32) for l in range(DEPTH)]
    sgout_t = [nc.dram_tensor("sg_out%d" % l, [512, 32], F32) for l in range(DEPTH)]

    es = ExitStack()
    S = Sched(nc, es)
    cur = [es]
    uniq = {"n": 0}

    def sb(name, shape, dt=F32):
        uniq["n"] += 1
        return cur[-1].enter_context(nc.sbuf_tensor("%s_%d" % (name, uniq["n"]), list(shape), dt))

    def open_scope():
        sc = ExitStack()
        cur.append(sc)
        return sc

    def close_scope(sc):
        assert cur[-1] is sc
        S.barrier()
        cur.pop()
        sc.close()

    def mm(out, lhsT, rhs, start, stop, rd, wr, sig=None):
        if sig is None:
            sig = bool(stop)
        S.op("pe", lambda e: e.matmul(out, lhsT=lhsT, rhs=rhs, start=start, stop=stop), rd, wr, sig=sig)

    def tr(out, in_, ident, rd, wr):
        S.op("pe", lambda e: e.transpose(out, in_, ident), rd, wr)

    def act(out, in_, func, rd, wr, scale=1.0, bias=None):
        if bias is None:
            S.op("act", lambda e: e.activation(out=out, in_=in_, func=func, scale=scale), rd, wr)
        else:
            S.op("act", lambda e: e.activation(out=out, in_=in_, func=func, scale=scale, bias=bias), rd, wr)

    def tt(eng, out, in0, in1, op, rd, wr):
        S.op(eng, lambda e: e.tensor_tensor(out=out, in0=in0, in1=in1, op=op), rd, wr)

    def ts(eng, out, in0, s1, s2, op0, op1, rd, wr):
        if op1 is None:
            S.op(eng, lambda e: e.tensor_scalar(out=out, in0=in0, scalar1=s1, scalar2=None, op0=op0), rd, wr)
        else:
            S.op(eng, lambda e: e.tensor_scalar(out=out, in0=in0, scalar1=s1, scalar2=s2, op0=op0, op1=op1), rd, wr)

    def stt(eng, out, in0, scalar, in1, op0, op1, rd, wr):
        S.op("dve", lambda e: e.scalar_tensor_tensor(out=out, in0=in0, scalar=scalar, in1=in1, op0=op0, op1=op1), rd, wr)

    def cp(eng, out, in_, rd, wr):
        if eng == "act":
            act(out, in_, AF.Identity, rd, wr)
        else:
            S.op(eng, lambda e: e.tensor_copy(out=out, in_=in_), rd, wr)

    def treduce(out, in_, rd, wr):
        S.op("dve", lambda e: e.tensor_reduce(out=out, in_=in_, op=ALU.add, axis=AX.X), rd, wr)

    def recip(out, in_, rd, wr):
        S.op("dve", lambda e: e.reciprocal(out=out, in_=in_), rd, wr)

    def scan(out, d0, d1, init, rd, wr):
        S.op("dve", lambda e: e.tensor_tensor_scan(out=out, data0=d0, data1=d1, initial=init, op0=ALU.mult, op1=ALU.add), rd, wr)

    def dma(q, out, in_, rd, wr):
        S.dma(q, lambda e: e.dma_start(out=out, in_=in_), rd, wr)

    outs = []

    def oname():
        n = "out%d" % len(outs)
        outs.append(n)
        return n

    def memset(eng, out, val, wr):
        S.op(eng, lambda e: e.memset(out, val), (), wr)

    ps_all = es.enter_context(nc.psum_tensor("ps_all", [128, 4096], F32))

    def bank(b):
        return ps_all[:, b * 512:(b + 1) * 512]

    def bank_bf(b):
        return ps_all[:, b * 512:(b + 1) * 512].bitcast(BF16)

    PB = ["psb%d" % b for b in range(8)]
    rot = {"i": 0}
    reserved = set()

    def nextb():
        while True:
            b = rot["i"]
            rot["i"] = (b + 1) % 8
            if b not in reserved:
                return b

    ident_f = sb("ident_f", [128, 128])
    ident_b = sb("ident_b", [128, 128], BF16)
    ones_f = sb("ones_f", [128, 128])
    sp1 = sb("sp1", [128, 1024])
    rowmask = sb("rowmask", [128, 8])
    nrowmask = sb("nrowmask", [128, 8])
    colmask = sb("colmask", [128, 8, 128], BF16)
    swapm = sb("swapm", [128, 128])
    sgn = sb("sgn", [128, 1])
    halfpi = sb("halfpi", [128, 1])
    oh = sb("oh", [128, 16])
    dma("sp", ident_f[:], c_ident, [], ["ident_f"])
    cp("dve", ident_b[:], ident_f[:], ["ident_f"], ["ident_b"])
    memset("dve", ones_f[:], 1.0, ["ones_f"])
    memset("dve", halfpi[:], math.pi / 2.0, ["halfpi"])
    dma("sp", sp1[:], c_sp1, [], ["sp1"])
    dma("sp", rowmask[:], c_rowmask, [], ["rowmask"])
    ts("dve", nrowmask[:], rowmask[:], -1.0, None, ALU.mult, None, ["rowmask"], ["nrowmask"])
    _sc0 = ExitStack()
    cur.append(_sc0)
    colmask_f = sb("colmask_f", [128, 8, 128])
    dma("sp", colmask_f[:], c_colmask, [], ["colmask_f"])
    cp("dve", colmask[:], colmask_f[:], ["colmask_f"], ["colmask"])
    S.barrier()
    cur.pop()
    _sc0.close()
    dma("sp", swapm[:], c_swap, [], ["swapm"])
    dma("sp", sgn[:], c_sgn, [], ["sgn"])
    dma("sp", oh[:], oh_d, [], ["oh"])

    xres = sb("xres", [128, NT, D])
    for ti in range(NT):
        src = xp[ti * 128:(ti + 1) * 128, :] if ti < 4 else xs[(ti - 4) * 128:(ti - 3) * 128, :]
        dma("sp", xres[:, ti, :], src, [], ["xres%d" % ti])

    def cond_of(ti):
        return 0 if ti < 4 else 1

    sc_f = sb("sc_f", [128, 8, 2])
    sc_b = sb("sc_b", [128, 8, 2], BF16)
    mT = sb("mT", [128, 48, 2])
    scp1 = sb("scp1", [128, 16, 2])
    bT = sb("bT", [128, 48])
    gate = [[sb("gate%d_%d" % (i, c), [128, D]) for c in range(2)] for i in range(2)]
    xn = [sb("xn%d" % i, [128, D], BF16) for i in range(2)]
    stats = sb("stats", [128, 2, 6])
    mv = sb("mv", [128, 2])
    rstd = sb("rstd", [128, 1])
    nmr = sb("nmr", [128, 1])
    hold = {}
    actT = sb("actT", [128, 8, NTOK], BF16)
    pT = [sb("pT%d" % i, [128, 512], BF16) for i in range(3)]
    rinv = sb("rinv", [128, 512])
    bcs = sb("bcs", [128, 512])
    lre = sb("lre", [128, 32]); lim = sb("lim", [128, 32]); ldt = sb("ldt", [128, 32])
    s_a = sb("s_a", [128, 32]); s_rho = sb("s_rho", [128, 32]); s_turn = sb("s_turn", [128, 32])
    s_t1 = sb("s_t1", [128, 32]); s_t2 = sb("s_t2", [128, 32]); s_t3 = sb("s_t3", [128, 32])
    s_c1 = sb("s_c1", [128, 32]); s_s1 = sb("s_s1", [128, 32]); s_q1 = sb("s_q1", [128, 32]); s_q2 = sb("s_q2", [128, 32])
    s_cT = sb("s_cT", [128, 32]); s_sT = sb("s_sT", [128, 32])
    s_cK = sb("s_cK", [128, 32]); s_sK = sb("s_sK", [128, 32])
    Lrr = sb("Lrr", [128, 32]); Lii = sb("Lii", [128, 32])
    kre = sb("kre", [128, 32]); kim = sb("kim", [128, 32])
    GL = sb("GL", [128, 64]); SWt = sb("SWt", [128, 64]); Sfin = sb("Sfin", [128, 64]); SfT = sb("SfT", [64, 128])
    GLs = sb("GLs", [128, 32]); Sloc = sb("Sloc", [128, 32]); SWs = sb("SWs", [128, 32])
    Sg = sb("Sg", [128, 4, 32]); SJ = sb("SJ", [128, 4, 32]); hin = sb("hin", [128, 4, 32]); hsel = sb("hsel", [128, 32])
    ssmd_t = sb("ssmd_t", [128, 2])

    def sincos(turn, sin_out, cos_out, tmp_r, tmp_a, rd, wrn):
        ts("dve", tmp_r, turn, MAGIC, MAGIC, ALU.add, ALU.subtract, rd, ["sc_tmp_r"])
        tt("dve", tmp_r, turn, tmp_r, ALU.subtract, rd + ["sc_tmp_r"], ["sc_tmp_r"])
        stt("dve", tmp_a, tmp_r, -1.0, tmp_r, ALU.mult, ALU.max, ["sc_tmp_r"], ["sc_tmp_a"])
        act(sin_out, tmp_r, AF.Sin, ["sc_tmp_r"], [wrn + "_s"], scale=TWO_PI)
        act(cos_out, tmp_a, AF.Sin, ["sc_tmp_a", "halfpi"], [wrn + "_c"], scale=-TWO_PI, bias=halfpi[:])

    def layernorm_stats(x_ap, rd):
        S.op("dve", lambda e: e.bn_stats(out=stats[:, 0, :], in_=x_ap[:, 0:512]), rd, ["stats"])
        S.op("dve", lambda e: e.bn_stats(out=stats[:, 1, :], in_=x_ap[:, 512:1024]), rd + ["stats"], ["stats"])
        S.op("dve", lambda e: e.bn_aggr(out=mv[:], in_=stats[:].rearrange("p a b -> p (a b)")), ["stats"], ["mv"])
        ts("dve", rstd[:], mv[:, 1:2], LN_EPS, None, ALU.add, None, ["mv"], ["rstd"])
        act(rstd[:], rstd[:], AF.Sqrt, ["rstd"], ["rstd"])
        recip(rstd[:], rstd[:], ["rstd"], ["rstd"])
        stt("dve", nmr[:], mv[:, 0:1], -1.0, rstd[:], ALU.mult, ALU.mult, ["mv", "rstd"], ["nmr"])

    def ln_to_T(ti, sc_idx, sh_idx):
        cond = cond_of(ti)
        xr = "xres%d" % ti
        x_ap = xres[:, ti, :]
        layernorm_stats(x_ap, [xr])
        xb = xn[ti % 2]
        xbn = "xn%d" % (ti % 2)
        act(xb[:], x_ap, AF.Identity, [xr, "rstd", "nmr"], [xbn], scale=rstd[:], bias=nmr[:])
        b = nextb()
        for k in range(8):
            tr(bank_bf(b)[:, k * 128:(k + 1) * 128], xb[:, k * 128:(k + 1) * 128], ident_b[:], [xbn, "ident_b"], [PB[b]])
        for k in range(8):
            if k % 2 == 1:
                act(actT[:, k, ti * 128:(ti + 1) * 128], bank_bf(b)[:, k * 128:(k + 1) * 128], AF.Identity,
                    [PB[b], "scp1", "mT"], ["actT%d" % ti], scale=scp1[:, sc_idx * 8 + k, cond:cond + 1],
                    bias=mT[:, sh_idx * 8 + k, cond:cond + 1])
            else:
                ts("dve", actT[:, k, ti * 128:(ti + 1) * 128], bank_bf(b)[:, k * 128:(k + 1) * 128],
                   scp1[:, sc_idx * 8 + k, cond:cond + 1], mT[:, sh_idx * 8 + k, cond:cond + 1], ALU.mult, ALU.add,
                   [PB[b], "scp1", "mT"], ["actT%d" % ti])

    def attn_norm(po_b, base, ncols, dst_ap, sumrow, rd_extra, wr):
        S.op("dve", lambda e: e.reciprocal(out=rinv[sumrow:sumrow + 1, 0:ncols], in_=bank(po_b)[sumrow:sumrow + 1, 0:ncols]),
             [PB[po_b]], ["rinv"])
        bb = nextb()
        mm(bank(bb)[:, 0:ncols], ones_f[sumrow:sumrow + 1, 0:128], rinv[sumrow:sumrow + 1, 0:ncols], True, True,
           ["ones_f", "rinv"], [PB[bb]])
        cp("act", bcs[base:base + 64, 0:ncols], bank(bb)[base:base + 64, 0:ncols], [PB[bb]], ["bcs"])
        tt("dve", dst_ap, bank(po_b)[base:base + 64, 0:ncols], bcs[base:base + 64, 0:ncols], ALU.mult,
           [PB[po_b], "bcs"] + rd_extra, wr)

    ACTT = ["actT%d" % ti for ti in range(NT)]

    def post_ln(ti, gi, lnr0, lnr1, bl):
        xr = "xres%d" % ti
        cond = cond_of(ti)
        gt = gate[gi][cond]
        gn = "gate%d_%d" % (gi, cond)
        tmpf, xnf = hold["tmpf"], hold["xnf"]
        for half in range(2):
            tt("dve", tmpf[:, half * 512:(half + 1) * 512], bank(bl[half]), gt[:, half * 512:(half + 1) * 512], ALU.mult,
               [PB[bl[half]], gn], ["tmpf"])
        stt("dve", xres[:, ti, :], xres[:, ti, :], ALPHA, tmpf[:], ALU.mult, ALU.add, [xr, "tmpf"], [xr])
        layernorm_stats(xres[:, ti, :], [xr])
        act(xnf[:], xres[:, ti, :], AF.Identity, [xr, "rstd", "nmr"], ["xnf"], scale=rstd[:], bias=nmr[:])
        tt("pool", xnf[:], xnf[:], lnr0[:], ALU.mult, ["xnf", "lnrow0"], ["xnf"])
        tt("pool", xres[:, ti, :], xnf[:], lnr1[:], ALU.add, ["xnf", "lnrow1"], [xr])

    try:
      for l in range(DEPTH):
        kna_in, kna_out, kg_in, kg_out, v_in, v_out = kna_in_t[l], kna_out_t[l], kg_in_t[l], kg_out_t[l], v_in_t[l], v_out_t[l]
        binn = []

        def bname():
            n = "bin%d_%d" % (l, len(binn))
            binn.append(n)
            return n
        scW = open_scope()
        wada = [sb("wada%d" % i, [128, 8, 1024], BF16) for i in range(2)]
        sc_rep = sb("sc_rep", [128, 8, 2, 128], BF16)
        brow = sb("brow", [128, D])
        if l == 0:
            dma("sp", sc_f[:], condT, [], ["sc_f"])
            act(sc_f[:], sc_f[:], AF.Silu, ["sc_f"], ["sc_f"])
            cp("dve", sc_b[:], sc_f[:], ["sc_f"], ["sc_b"])
        cp("dve", sc_rep[:], fap(sc_f, 0, [[16, 128], [2, 8], [1, 2], [0, 128]]), ["sc_f"], ["sc_rep"])
        dma("sp", bT[:], b_adaT[l], [], ["bT"])
        for piece in range(6):
            wb = wada[piece % 2]
            wn = "wada%d" % (piece % 2)
            dma("pool", wb[:], w_ada[l, :, piece * 1024:(piece + 1) * 1024].rearrange("(k p) n -> p k n", p=128), [], [wn])
            if piece in (2, 5):
                gi = 0 if piece == 2 else 1
                dma("sp", brow[:], fap(b_ada.tensor, l * 6 * D + piece * 1024, [[0, 128], [1, 1024]]), [], ["brow"])
                for cond in range(2):
                    for half in range(2):
                        b = nextb()
                        for k in range(8):
                            mm(bank(b), sc_rep[:, k, cond, :], wb[:, k, half * 512:(half + 1) * 512], k == 0, k == 7,
                               ["sc_rep", wn], [PB[b]])
                        tt("dve", gate[gi][cond][:, half * 512:(half + 1) * 512], bank(b), brow[:, half * 512:(half + 1) * 512], ALU.add,
                           [PB[b], "brow"], ["gate%d_%d" % (gi, cond)])
            else:
                b = nextb()
                for oc in range(8):
                    for k in range(8):
                        mm(bank(b)[:, oc * 2:(oc + 1) * 2], wb[:, k, oc * 128:(oc + 1) * 128], sc_b[:, k, :], k == 0, k == 7,
                           ["sc_b", wn], [PB[b]])
                tt("dve", mT[:, piece * 8:(piece + 1) * 8, :], bank(b)[:, 0:16].rearrange("p (a b) -> p a b", b=2),
                   fap(bT, piece * 8, [[48, 128], [1, 8], [0, 2]]), ALU.add, [PB[b], "bT"], ["mT"])
        ts("dve", scp1[:, 0:8, :], mT[:, 8:16, :], 1.0, None, ALU.add, None, ["mT"], ["scp1"])
        ts("dve", scp1[:, 8:16, :], mT[:, 32:40, :], 1.0, None, ALU.add, None, ["mT"], ["scp1"])
        close_scope(scW)
        checkpoint()

        scPA = open_scope()
        uT = sb("uT", [128, 2, NTOK], BF16)
        qnaT = sb("qnaT", [128, 2, NTOK], BF16)
        QgT = sb("QgT", [128, 4, 1024], BF16)
        wglu_t = sb("wglu_t", [128, 2, 256], BF16)
        scP2 = open_scope()
        kqkT = sb("kqkT", [128, 8, NPT], BF16)
        v_na_e = sb("v_na_e", [128, 4, 2, 65], BF16)
        v_na_o = sb("v_na_o", [128, 4, 2, 128], BF16)
        v_g_e = sb("v_g_e", [128, 4, 2, 65], BF16)
        v_g_o = sb("v_g_o", [128, 4, 2, 128], BF16)
        memset("pool", v_na_e[:], 1.0, ["v_na_e"])
        memset("pool", v_g_e[:], 1.0, ["v_g_e"])
        memset("pool", v_na_o[:], 0.0, ["v_na_o"])
        memset("pool", v_g_o[:], 0.0, ["v_g_o"])
        memset("pool", v_na_o[:, :, :, 0:1], 1.0, ["v_na_o"])
        memset("pool", v_g_o[:, :, :, 0:1], 1.0, ["v_g_o"])
        dma("pool", wglu_t[:], wglu[l].rearrange("(k p) n -> p k n", p=128), [], ["wglu_t"])

        scA1 = open_scope()
        win = sb("win", [128, 8, IN_W], BF16)
        tz = sb("tz", [128, 1280])
        sqt = sb("sqt", [128, 640])
        ss10 = sb("ss10", [128, 10])
        g10t = sb("g10t", [128, 640])
        kqk_b = sb("kqk_b", [128, 1024], BF16)
        rc_t = sb("rc_t", [128, 640]); rs_t = sb("rs_t", [128, 640])
        ktmp = sb("ktmp", [128, 8, 128], BF16)
        vtok = sb("vtok", [128, 384], BF16)
        dma("pool", win[:], w_in[l].rearrange("(k p) n -> p k n", p=128), [], ["win"])
        dma("sp", g10t[:], fap(g10.tensor, l * 640, [[0, 128], [1, 640]]), [], ["g10t"])
        for ti in range(NT):
            ln_to_T(ti, 0, 0)
        checkpoint()
        for T in range(3):
            rdT = ACTT[T * 4:(T + 1) * 4]
            for oc in range(4):
                b = nextb()
                for k in range(8):
                    mm(bank(b), win[:, k, oc * 128:(oc + 1) * 128], actT[:, k, T * 512:(T + 1) * 512], k == 0, k == 7,
                       ["win"] + rdT, [PB[b]])
                if oc < 2:
                    cp("dve", uT[:, oc, T * 512:(T + 1) * 512], bank(b), [PB[b]], ["uT"])
                else:
                    cp("act", qnaT[:, oc - 2, T * 512:(T + 1) * 512], bank(b), [PB[b]], ["qnaT"])
        checkpoint()
        for ti in range(NT):
            is_s = ti >= 4
            bl = []
            for (c0, n) in [(512, 512), (1024, 512), (1536, 256)]:
                b = nextb()
                bl.append(b)
                for k in range(8):
                    mm(bank(b)[:, 0:n], actT[:, k, ti * 128:(ti + 1) * 128], win[:, k, c0:c0 + n], k == 0, k == 7,
                       ["win", ACTT[ti]], [PB[b]])
            cp("act", tz[:, 0:512], bank(bl[0]), [PB[bl[0]]], ["tz"])
            cp("dve", tz[:, 512:1024], bank(bl[1]), [PB[bl[1]]], ["tz"])
            cp("act", tz[:, 1024:1280], bank(bl[2])[:, 0:256], [PB[bl[2]]], ["tz"])
            tt("pool", sqt[:], tz[:, 512:1152], tz[:, 512:1152], ALU.mult, ["tz"], ["sqt"])
            treduce(ss10[:], sqt[:].rearrange("p (h d) -> p h d", d=64), ["sqt"], ["ss10"])
            ts("dve", ss10[:], ss10[:], 1.0 / 64.0, RMS_EPS, ALU.mult, ALU.add, ["ss10"], ["ss10"])
            act(ss10[:], ss10[:], AF.Sqrt, ["ss10"], ["ss10"])
            recip(ss10[:], ss10[:], ["ss10"], ["ss10"])
            tt("dve", tz[:, 512:1152].rearrange("p (h d) -> p h d", d=64), tz[:, 512:1152].rearrange("p (h d) -> p h d", d=64),
               fap(ss10, 0, [[10, 128], [1, 10], [0, 64]]), ALU.mult, ["tz", "ss10"], ["tz"])
            tt("pool", tz[:, 512:1152], tz[:, 512:1152], g10t[:], ALU.mult, ["tz", "g10t"], ["tz"])
            if not is_s:
                sub = ti
                bb_, t0 = sub // 2, (sub % 2) * 128
                dma("sp", o_nak[bb_, l, t0:t0 + 128, :], tz[:, 0:256], ["tz"], [oname()])
                dma("sp", o_nav[bb_, l, t0:t0 + 128, :], tz[:, 256:512], ["tz"], [oname()])
                dma("sp", o_gk[bb_, l, t0:t0 + 128, :], tz[:, 1024:1152], ["tz"], [oname()])
                dma("sp", o_gv[bb_, l, t0:t0 + 128, :], tz[:, 1152:1280], ["tz"], [oname()])
            else:
                ts0 = (ti - 4) * 128
                dma("sp", rc_t[:], ropec[ts0:ts0 + 128, :], [], ["rc_t"])
                dma("sp", rs_t[:], ropes[ts0:ts0 + 128, :], [], ["rs_t"])
                xsw = fap(tz, 512 + 16, [[1280, 128], [32, 20], [-16, 2], [1, 16]])
                tt("pool", sqt[:].rearrange("p (a b c) -> p a b c", b=2, c=16), xsw,
                   rs_t[:].rearrange("p (a b c) -> p a b c", b=2, c=16), ALU.mult, ["tz", "rs_t"], ["sqt"])
                tt("pool", tz[:, 512:1152], tz[:, 512:1152], rc_t[:], ALU.mult, ["tz", "rc_t"], ["tz"])
                tt("pool", tz[:, 512:1152], tz[:, 512:1152], sqt[:], ALU.add, ["tz", "sqt"], ["tz"])
            cp("act", kqk_b[:, 0:256], tz[:, 0:256], ["tz"], ["kqk_b"])
            cp("pool", kqk_b[:, 256:768], tz[:, 512:1024], ["tz"], ["kqk_b"])
            cp("pool", kqk_b[:, 768:1024].rearrange("p (kv r d) -> p kv r d", kv=2, r=2),
               fap(tz, 1024, [[1280, 128], [64, 2], [0, 2], [1, 64]]), ["tz"], ["kqk_b"])
            b = nextb()
            for c in range(8):
                tr(bank_bf(b)[:, c * 128:(c + 1) * 128], kqk_b[:, c * 128:(c + 1) * 128], ident_b[:], ["kqk_b", "ident_b"], [PB[b]])
            if not is_s:
                sub = ti
                cp("act", v_na_e[:, sub, :, 0:64], fap(tz, 256, [[1280, 128], [128, 2], [1, 64]]), ["tz"], ["v_na_e"])
                cp("act", v_na_o[:, sub, :, 64:128], fap(tz, 256 + 64, [[1280, 128], [128, 2], [1, 64]]), ["tz"], ["v_na_o"])
                cp("pool", v_g_e[:, sub, :, 0:64], fap(tz, 1152, [[1280, 128], [64, 2], [1, 64]]), ["tz"], ["v_g_e"])
                cp("pool", v_g_o[:, sub, :, 64:128], fap(tz, 1152, [[1280, 128], [64, 2], [1, 64]]), ["tz"], ["v_g_o"])
                cp("dve", kqkT[:, :, sub * 128:(sub + 1) * 128], bank_bf(b).rearrange("p (c t) -> p c t", t=128), [PB[b]], ["kqkT"])
            else:
                ts0 = (ti - 4) * 128
                cp("dve", ktmp[:], bank_bf(b).rearrange("p (c t) -> p c t", t=128), [PB[b]], ["ktmp"])
                cp("pool", QgT[:, :, ts0:ts0 + 128], ktmp[:, 2:6, :], ["ktmp"], ["QgT"])
                dma("sp", fap(kna_in, ts0, [[1024, 128], [128 * 1024, 2], [1, 128]]), ktmp[:, 0:2, :], ["ktmp"], [bname()])
                dma("sp", fap(kg_in, ts0, [[1024, 128], [128 * 1024, 2], [1, 128]]), ktmp[:, 6:8, :], ["ktmp"], [bname()])
                cp("act", vtok[:, 0:256], tz[:, 256:512], ["tz"], ["vtok"])
                cp("act", vtok[:, 256:384], tz[:, 1152:1280], ["tz"], ["vtok"])
                dma("sp", fap(v_in, ts0 * 384, [[384, 128], [1, 384]]), vtok[:], ["vtok"], [bname()])
        close_scope(scA1)
        checkpoint()

        for (a_, b__) in [(kna_in, kna_out), (kg_in, kg_out), (v_in, v_out)]:
            S.coll((lambda a_=a_, b__=b__: lambda e: e.collective_compute(
                "AllGather", ALU.bypass, replica_groups=[[0, 1, 2, 3], [4, 5, 6, 7]], ins=[a_.ap()], outs=[b__.ap()]))(),
                list(binn), ["bout_%s" % a_.name])
        mixT = actT
        MIXN = ["mix"]

        dma("sp", lre[:], lamre[l], [], ["lre"])
        dma("sp", lim[:], lamim[l], [], ["lim"])
        dma("sp", ldt[:], fap(logdt.tensor, l * 32, [[0, 128], [1, 32]]), [], ["ldt"])
        dma("sp", ssmd_t[:], ssmd[l], [], ["ssmd_t"])
        ts("dve", lre[:], lre[:], -1e-4, None, ALU.min, None, ["lre"], ["lre"])
        act(ldt[:], ldt[:], AF.Exp, ["ldt"], ["ldt"])
        tt("dve", s_a[:], lre[:], ldt[:], ALU.mult, ["lre", "ldt"], ["s_a"])
        act(s_rho[:], s_a[:], AF.Exp, ["s_a"], ["s_rho"])
        tt("dve", s_turn[:], lim[:], ldt[:], ALU.mult, ["lim", "ldt"], ["s_turn"])
        ts("dve", s_turn[:], s_turn[:], 1.0 / TWO_PI, None, ALU.mult, None, ["s_turn"], ["s_turn"])
        sincos(s_turn[:], s_s1[:], s_c1[:], s_q1[:], s_q2[:], ["s_turn"], "sc1")
        ts("dve", s_t3[:], s_turn[:], float(SEQ), None, ALU.mult, None, ["s_turn"], ["s_t3"])
        sincos(s_t3[:], s_sT[:], s_cT[:], s_q1[:], s_q2[:], ["s_t3"], "scT")
        ts("dve", s_t3[:], s_turn[:], 1024.0, None, ALU.mult, None, ["s_turn", "s_t3"], ["s_t3"])
        sincos(s_t3[:], s_sK[:], s_cK[:], s_q1[:], s_q2[:], ["s_t3"], "scK")
        act(s_t1[:], s_a[:], AF.Exp, ["s_a"], ["s_t1"], scale=1024.0)
        tt("dve", Lrr[:], s_t1[:], s_cK[:], ALU.mult, ["s_t1", "scK_c"], ["Lrr"])
        tt("dve", Lii[:], s_t1[:], s_sK[:], ALU.mult, ["s_t1", "scK_s"], ["Lii"])
        ts("dve", Lii[:], Lii[:], sgn[:, 0:1], None, ALU.mult, None, ["Lii", "sgn"], ["Lii"])
        tt("dve", s_t1[:], s_rho[:], s_c1[:], ALU.mult, ["s_rho", "sc1_c", "Lrr", "Lii"], ["s_t1"])
        ts("dve", s_t1[:], s_t1[:], -1.0, None, ALU.add, None, ["s_t1"], ["s_t1"])
        tt("dve", s_t2[:], s_rho[:], s_s1[:], ALU.mult, ["s_rho", "sc1_s"], ["s_t2"])
        tt("dve", s_t3[:], lre[:], lre[:], ALU.mult, ["lre", "scK_s", "scK_c", "sc_tmp_r"], ["s_t3"])
        tt("dve", kre[:], lim[:], lim[:], ALU.mult, ["lim"], ["kre"])
        tt("dve", s_t3[:], s_t3[:], kre[:], ALU.add, ["s_t3", "kre"], ["s_t3"])
        recip(s_t3[:], s_t3[:], ["s_t3"], ["s_t3"])
        tt("dve", kre[:], s_t1[:], lre[:], ALU.mult, ["s_t1", "lre"], ["kre"])
        tt("dve", kim[:], s_t2[:], lim[:], ALU.mult, ["s_t2", "lim"], ["kim"])
        tt("dve", kre[:], kre[:], kim[:], ALU.add, ["kre", "kim"], ["kre"])
        tt("dve", kre[:], kre[:], s_t3[:], ALU.mult, ["kre", "s_t3"], ["kre"])
        tt("dve", kim[:], s_t2[:], lre[:], ALU.mult, ["s_t2", "lre"], ["kim"])
        tt("dve", s_t2[:], s_t1[:], lim[:], ALU.mult, ["s_t1", "lim", "kim"], ["s_t2"])
        tt("dve", kim[:], kim[:], s_t2[:], ALU.subtract, ["kim", "s_t2"], ["kim"])
        tt("dve", kim[:], kim[:], s_t3[:], ALU.mult, ["kim", "s_t3"], ["kim"])

        def ssm_pass(mode, co=None):
            scS = open_scope()
            bb_re = sb("bb_re", [64, 32, 16]); bb_im = sb("bb_im", [64, 32, 16])
            scB = open_scope()
            b_re_t = sb("b_re_t", [64, 32, 16]); b_im_t = sb("b_im_t", [64, 32, 16]); bb_t = sb("bb_t", [64, 32, 16])
            dma("sp", b_re_t[:].rearrange("p a c -> p (a c)"), bre[l], [], ["b_re_t"])
            dma("sp", b_im_t[:].rearrange("p a c -> p (a c)"), bim[l], [], ["b_im_t"])
            kre_b = fap(kre, 0, [[32, 64], [1, 32], [0, 16]])
            kim_b = fap(kim, 0, [[32, 64], [1, 32], [0, 16]])
            tt("pool", bb_re[:], b_re_t[:], kre_b, ALU.mult, ["b_re_t", "kre"], ["bb_re"])
            tt("pool", bb_t[:], b_im_t[:], kim_b, ALU.mult, ["b_im_t", "kim"], ["bb_t"])
            tt("pool", bb_re[:], bb_re[:], bb_t[:], ALU.subtract, ["bb_re", "bb_t"], ["bb_re"])
            tt("pool", bb_im[:], b_im_t[:], kre_b, ALU.mult, ["b_im_t", "kre"], ["bb_im"])
            tt("pool", bb_t[:], b_re_t[:], kim_b, ALU.mult, ["b_re_t", "kim", "bb_re"], ["bb_t"])
            tt("pool", bb_im[:], bb_im[:], bb_t[:], ALU.add, ["bb_im", "bb_t"], ["bb_im"])
            close_scope(scB)
            c_re_t = sb("c_re_t", [128, 4, 64]); c_im_t = sb("c_im_t", [128, 4, 64])
            dma("sp", c_re_t[:], cre[l], [], ["c_re_t"])
            dma("sp", c_im_t[:], cim[l], [], ["c_im_t"])
            csrc1 = sb("csrc1", [128, 128]); csrc2 = sb("csrc2", [128, 128])
            bpad = sb("bpad", [128, 8, 128], BF16); bsw = sb("bsw", [128, 8, 128], BF16)
            c1pad = sb("c1pad", [128, 8, 128], BF16); c2pad = sb("c2pad", [128, 8, 128], BF16)
            ntab = 2 if mode == 1 else 1
            tcoss = [sb("tcos%d" % i, [128, 1024]) for i in range(ntab)]; tsins = [sb("tsin%d" % i, [128, 1024]) for i in range(ntab)]
            r1s = [sb("r1_%d" % i, [128, 512]) for i in range(2)]; r2s = [sb("r2_%d" % i, [128, 512]) for i in range(2)]
            btls = [sb("btl_%d" % i, [128, 512]) for i in range(2)]; gscs = [sb("gsc_%d" % i, [128, 512]) for i in range(2)]
            G1s = [sb("G1_%d" % i, [128, 512], BF16) for i in range(2)]; G2s = [sb("G2_%d" % i, [128, 512], BF16) for i in range(2)]
            r1, r2, btl = r1s[0], r2s[0], btls[0]
            ucnt = {"n": 0}
            pend = {"f": None}
            gl_b = sb("gl_b", [128, 2, 1024], BF16)
            nlen = 1024 if True else 256

            for gc in range(2):
                if mode == 1:
                    accs = {"p": 7}
                else:
                    accs = {"s1": 6, "s2": 7}
                for a in accs.values():
                    reserved.add(a)
                first = {k: True for k in accs}
                for d in range(2):
                    dg0 = d * 16 + gc * 8
                    b = nextb()
                    tr(bank(b)[:, 0:64], bb_re[:, dg0:dg0 + 8, :].rearrange("p a c -> p (a c)"), ident_f[0:64, 0:64], ["bb_re", "ident_f"], [PB[b]])
                    tr(bank(b)[:, 64:128], bb_im[:, dg0:dg0 + 8, :].rearrange("p a c -> p (a c)"), ident_f[0:64, 0:64], ["bb_im", "ident_f"], [PB[b]])
                    full = fap(ps_all, b * 512, [[4096, 128], [0, 8], [1, 128]])
                    tt("dve", bpad[:], full, fap(rowmask, 0, [[8, 128], [1, 8], [0, 128]]), ALU.mult, [PB[b], "rowmask"], ["bpad"])
                    full_im = fap(ps_all, b * 512 + 64, [[4096, 128], [0, 8], [1, 64]])
                    full_re = fap(ps_all, b * 512, [[4096, 128], [0, 8], [1, 64]])
                    tt("dve", bsw[:, :, 0:64], full_im, fap(rowmask, 0, [[8, 128], [1, 8], [0, 64]]), ALU.mult, [PB[b], "rowmask"], ["bsw"])
                    tt("dve", bsw[:, :, 64:128], full_re, fap(nrowmask, 0, [[8, 128], [1, 8], [0, 64]]), ALU.mult, [PB[b], "nrowmask"], ["bsw"])
                    ci = d * 2 + gc
                    cp("pool", csrc1[:, 0:64], c_re_t[:, ci, :], ["c_re_t"], ["csrc1"])
                    ts("pool", csrc1[:, 64:128], c_im_t[:, ci, :], -1.0, None, ALU.mult, None, ["c_im_t"], ["csrc1"])
                    ts("pool", csrc2[:, 0:64], c_im_t[:, ci, :], -1.0, None, ALU.mult, None, ["c_im_t"], ["csrc2"])
                    ts("pool", csrc2[:, 64:128], c_re_t[:, ci, :], -1.0, None, ALU.mult, None, ["c_re_t"], ["csrc2"])
                    b1 = nextb()
                    tr(bank(b1)[:, 0:128], csrc1[:], ident_f[:], ["csrc1", "ident_f"], [PB[b1]])
                    tr(bank(b1)[:, 128:256], csrc2[:], ident_f[:], ["csrc2", "ident_f"], [PB[b1]])
                    tt("dve", c1pad[:], fap(ps_all, b1 * 512, [[4096, 128], [0, 8], [1, 128]]), colmask[:], ALU.mult, [PB[b1], "colmask"], ["c1pad"])
                    tt("dve", c2pad[:], fap(ps_all, b1 * 512 + 128, [[4096, 128], [0, 8], [1, 128]]), colmask[:], ALU.mult, [PB[b1], "colmask"], ["c2pad"])
                    for g8 in range(8):
                        dg = dg0 + g8
                        tb = dg % ntab
                        if ntab == 1 and pend["f"] is not None:
                            pend["f"]()
                            pend["f"] = None
                        tcos, tsin = tcoss[tb], tsins[tb]
                        nTC, nTS = "tcos%d" % tb, "tsin%d" % tb
                        ts("dve", tsin[:, 0:nlen], sp1[:, 0:nlen], s_turn[:, dg:dg + 1], None, ALU.mult, None, ["sp1", "s_turn"], [nTS])
                        ts("dve", tcos[:, 0:nlen], tsin[:, 0:nlen], MAGIC, MAGIC, ALU.add, ALU.subtract, [nTS], [nTC])
                        tt("pool", tsin[:, 0:nlen], tsin[:, 0:nlen], tcos[:, 0:nlen], ALU.subtract, [nTS, nTC], [nTS])
                        stt("dve", tcos[:, 0:nlen], tsin[:, 0:nlen], -1.0, tsin[:, 0:nlen], ALU.mult, ALU.max, [nTS], [nTC])
                        act(tsin[:, 0:nlen], tsin[:, 0:nlen], AF.Sin, [nTS], [nTS], scale=TWO_PI)
                        act(tcos[:, 0:nlen], tcos[:, 0:nlen], AF.Sin, [nTC, "halfpi"], [nTC], scale=-TWO_PI, bias=halfpi[:])
                        if mode == 1:
                            units = [("s", 512, [(0, 512, 0)], None), ("s", 1024, [(0, 512, 512)], None),
                                     ("p", 0, [(0, 256, 0), (256, 256, 0)], "p")]
                        else:
                            units = [("s", 512, [(0, 512, 0)], "s1"), ("s", 1024, [(0, 512, 512)], "s2")]
                        if d == 1:
                            if mode == 1:
                                units = [("s", 1024, [(0, 512, 0)], None), ("s", 512, [(0, 512, 512)], None),
                                         ("p", 0, [(0, 256, 0), (256, 256, 0)], "p")]
                            else:
                                units = [("s", 1024, [(0, 512, 0)], "s2"), ("s", 512, [(0, 512, 512)], "s1")]
                        prev_last = None
                        for (kind, c0, segs, acck) in units:
                            ub = ucnt["n"] % 2
                            ucnt["n"] += 1
                            r1, r2, btl, gsc, G1, G2 = r1s[ub], r2s[ub], btls[ub], gscs[ub], G1s[ub], G2s[ub]
                            nR1, nR2, nBT, nGS, nG1, nG2 = "r1_%d" % ub, "r2_%d" % ub, "btl_%d" % ub, "gsc_%d" % ub, "G1_%d" % ub, "G2_%d" % ub
                            bu = nextb()
                            bw = nextb()
                            mm(bank(bu), bpad[:, g8, :], uT[:, gc, c0:c0 + 512], True, True, ["bpad", "uT"], [PB[bu]])
                            mm(bank(bw), bsw[:, g8, :], uT[:, gc, c0:c0 + 512], True, True, ["bsw", "uT"], [PB[bw]])
                            for (off, ln_, toff) in segs:
                                if d == 0:
                                    bu_v = fap(ps_all, bu * 512 + off, [[4096, 128], [1, ln_]])
                                    bw_v = fap(ps_all, bw * 512 + off, [[4096, 128], [1, ln_]])
                                else:
                                    bu_v = fap(ps_all, bu * 512 + off + ln_ - 1, [[4096, 128], [-1, ln_]])
                                    bw_v = fap(ps_all, bw * 512 + off + ln_ - 1, [[4096, 128], [-1, ln_]])
                                tt("dve", r1[:, off:off + ln_], bu_v, tcos[:, toff:toff + ln_], ALU.mult, [PB[bu], nTC], [nR1])
                                tt("dve", r2[:, off:off + ln_], bw_v, tsin[:, toff:toff + ln_], ALU.mult, [PB[bw], nTS], [nR2])
                            tt("dve", btl[:], r1[:], r2[:], ALU.add, [nR1, nR2], [nBT])

                            def stageB(kind=kind, segs=segs, acck=acck, btl=btl, gsc=gsc, G1=G1, G2=G2, nBT=nBT, nGS=nGS, nG1=nG1, nG2=nG2,
                                       dg=dg, d=d, g8=g8, tcos=tcos, tsin=tsin, nTC=nTC, nTS=nTS):
                                for si, (off, ln_, toff) in enumerate(segs):
                                    if kind == "p":
                                        init = 0.0
                                        rdi = []
                                    elif toff == 0:
                                        if mode == 1:
                                            init = 0.0
                                            rdi = []
                                        else:
                                            init = hsel[:, dg:dg + 1]
                                            rdi = ["hsel"]
                                    else:
                                        init = GLs[:, dg:dg + 1]
                                        rdi = ["GLs"]
                                    scan(gsc[:, off:off + ln_], fap(s_rho, dg, [[32, 128], [0, ln_]]), btl[:, off:off + ln_], init,
                                         [nBT, "s_rho"] + rdi, [nGS])
                                if kind == "p":
                                    cp("act", GL[:, dg * 2:dg * 2 + 2], fap(gsc, 255, [[512, 128], [256, 2]]), [nGS], ["GL"])
                                else:
                                    cp("act", GLs[:, dg:dg + 1], gsc[:, 511:512], [nGS], ["GLs"])
                                if acck is not None:
                                    for (off, ln_, toff) in segs:
                                        if d == 0:
                                            g1o = G1[:, off:off + ln_]
                                            g2o = G2[:, off:off + ln_]
                                        else:
                                            g1o = fap(G1, off + ln_ - 1, [[512, 128], [-1, ln_]])
                                            g2o = fap(G2, off + ln_ - 1, [[512, 128], [-1, ln_]])
                                        tt("dve", g1o, gsc[:, off:off + ln_], tcos[:, toff:toff + ln_], ALU.mult, [nGS, nTC], [nG1])
                                        tt("pool", g2o, gsc[:, off:off + ln_], tsin[:, toff:toff + ln_], ALU.mult, [nGS, nTS], [nG2])
                                    acc = accs[acck]
                                    lastmm = (d == 1 and g8 == 7)
                                    mm(bank(acc), c1pad[:, g8, :], G1[:], first[acck], False, ["c1pad", nG1], [PB[acc]])
                                    first[acck] = False
                                    mm(bank(acc), c2pad[:, g8, :], G2[:], False, lastmm, ["c2pad", nG2], [PB[acc]])
                            if pend["f"] is not None:
                                pend["f"]()
                            pend["f"] = stageB
                            if co is not None:
                                for _ in range(9):
                                    next(co, None)
                    if pend["f"] is not None:
                        pend["f"]()
                        pend["f"] = None
                for acck, acc in accs.items():
                    if acck == "p":
                        c0, gcol = 0, 0
                    elif acck == "s1":
                        c0, gcol = 512, 0
                    else:
                        c0, gcol = 1024, 512
                    yt, y2, y3 = r1s[0], r2s[0], btls[0]
                    stt("dve", yt[:], uT[:, gc, c0:c0 + 512], ssmd_t[:, gc:gc + 1], bank(acc), ALU.mult, ALU.add, [PB[acc], "uT", "ssmd_t"], ["r1_0"])
                    tt("pool", y2[:], yt[:], yt[:], ALU.mult, ["r1_0"], ["r2_0"])
                    ts("pool", y2[:], y2[:], 0.044715, 1.0, ALU.mult, ALU.add, ["r2_0"], ["r2_0"])
                    tt("pool", y2[:], y2[:], yt[:], ALU.mult, ["r2_0", "r1_0"], ["r2_0"])
                    act(y3[:], y2[:], AF.Tanh, ["r2_0"], ["btl_0"], scale=math.sqrt(2.0 / math.pi))
                    ts("pool", y3[:], y3[:], 1.0, 0.5, ALU.add, ALU.mult, ["btl_0"], ["btl_0"])
                    tt("pool", gl_b[:, gc, gcol:gcol + 512], y3[:], yt[:], ALU.mult, ["btl_0", "r1_0"], ["gl_b"])
                for a in accs.values():
                    reserved.discard(a)
            if co is not None:
                for _ in co:
                    pass
            cols = [(0, 0)] if mode == 1 else [(512, 0), (1024, 512)]
            for (c0, gcol) in cols:
                for oc in range(2):
                    b = nextb()
                    for k in range(2):
                        mm(bank(b), wglu_t[:, k, oc * 128:(oc + 1) * 128], gl_b[:, k, gcol:gcol + 512], k == 0, k == 1, ["wglu_t", "gl_b"], [PB[b]])
                    act(r1s[0][:], bank(b), AF.Sigmoid, [PB[b]], ["r1_0"])
                    tt("dve", mixT[:, oc, c0:c0 + 512], r1s[0][:], gl_b[:, oc, gcol:gcol + 512], ALU.mult, ["r1_0", "gl_b"], MIXN)
            close_scope(scS)

        ssm_pass(1)
        b = nextb()
        mm(bank(b)[:, 0:64], swapm[:], GL[:], True, True, ["swapm", "GL"], [PB[b]])
        cp("act", SWt[:], bank(b)[:, 0:64], [PB[b]], ["SWt"])
        ts("dve", s_t1[:], s_sT[:], sgn[:, 0:1], None, ALU.mult, None, ["scT_s", "sgn"], ["s_t1"])
        tt("dve", Sfin[:].rearrange("p (a b) -> p a b", b=2), GL[:].rearrange("p (a b) -> p a b", b=2),
           fap(s_cT, 0, [[32, 128], [1, 32], [0, 2]]), ALU.mult, ["GL", "scT_c"], ["Sfin"])
        tt("dve", SWt[:].rearrange("p (a b) -> p a b", b=2), SWt[:].rearrange("p (a b) -> p a b", b=2),
           fap(s_t1, 0, [[32, 128], [1, 32], [0, 2]]), ALU.mult, ["SWt", "s_t1"], ["SWt"])
        tt("dve", Sfin[:], Sfin[:], SWt[:], ALU.add, ["Sfin", "SWt"], ["Sfin"])
        b = nextb()
        tr(bank(b)[0:64, 0:128], Sfin[:], ident_f[:], ["Sfin", "ident_f"], [PB[b]])
        cp("act", SfT[:], bank(b)[0:64, 0:128], [PB[b]], ["SfT"])
        for bq in range(2):
            dma("sp", o_sre[bq, l], fap(SfT, bq * 128, [[256, 32], [1, 64]]), ["SfT"], [oname()])
            dma("sp", o_sim[bq, l], fap(SfT, bq * 128 + 64, [[256, 32], [1, 64]]), ["SfT"], [oname()])
        b = nextb()
        mm(bank(b)[:, 0:32], swapm[:], GLs[:], True, True, ["swapm", "GLs"], [PB[b]])
        cp("act", SWs[:], bank(b)[:, 0:32], [PB[b]], ["SWs"])
        ts("dve", s_t2[:], s_sK[:], sgn[:, 0:1], None, ALU.mult, None, ["scK_s", "sgn"], ["s_t2"])
        tt("dve", Sloc[:], GLs[:], s_cK[:], ALU.mult, ["GLs", "scK_c"], ["Sloc"])
        tt("dve", SWs[:], SWs[:], s_t2[:], ALU.mult, ["SWs", "s_t2"], ["SWs"])
        tt("dve", Sloc[:], Sloc[:], SWs[:], ALU.add, ["Sloc", "SWs"], ["Sloc"])
        dma("sp", sgin_t[l].ap(), Sloc[:], ["Sloc"], ["sgin"])
        checkpoint()

        S.coll(lambda e, a=sgin_t[l], b_=sgout_t[l]: e.collective_compute(
            "AllGather", ALU.bypass, replica_groups=[[0, 1, 2, 3], [4, 5, 6, 7]], ins=[a.ap()], outs=[b_.ap()]),
            ["sgin"], ["sgout"])
        checkpoint()

        for bq in range(2):
            tok0 = bq * 256
            for h in range(4):
                base = (h % 2) * 64
                b = nextb()
                for j in range(2):
                    mm(bank(b)[:, j * 256:(j + 1) * 256], kqkT[base:base + 64, h // 2, tok0 + j * 128: tok0 + (j + 1) * 128],
                       qnaT[base:base + 64, h // 2, tok0:tok0 + 256], True, True, ["kqkT", "qnaT"], [PB[b]])
                pt = pT[h % 2]
                act(pt[:], bank(b), AF.Exp, [PB[b]], ["pT%d" % (h % 2)], scale=0.125)
                po = nextb()
                for j in range(2):
                    if h % 2 == 0:
                        mm(bank(po)[0:65, 0:256], v_na_e[:, bq * 2 + j, h // 2, :], pt[:, j * 256:(j + 1) * 256], j == 0, j == 1,
                           ["v_na_e", "pT%d" % (h % 2)], [PB[po]])
                    else:
                        mm(bank(po)[:, 0:256], v_na_o[:, bq * 2 + j, h // 2, :], pt[:, j * 256:(j + 1) * 256], j == 0, j == 1,
                           ["v_na_o", "pT%d" % (h % 2)], [PB[po]])
                attn_norm(po, base, 256, mixT[base:base + 64, 2 + h // 2, tok0:tok0 + 256], 64 if h % 2 == 0 else 0, [], MIXN)
            for hq in range(8):
                base = (hq % 2) * 64
                kv = hq // 4
                b = nextb()
                for j in range(2):
                    mm(bank(b)[:, j * 256:(j + 1) * 256], kqkT[base:base + 64, 6 + kv, tok0 + j * 128: tok0 + (j + 1) * 128],
                       kqkT[base:base + 64, 2 + hq // 2, tok0:tok0 + 256], True, True, ["kqkT"], [PB[b]])
                pt = pT[hq % 2]
                act(pt[:], bank(b), AF.Exp, [PB[b]], ["pT%d" % (hq % 2)], scale=0.125)
                po = nextb()
                for j in range(2):
                    if hq % 2 == 0:
                        mm(bank(po)[0:65, 0:256], v_g_e[:, bq * 2 + j, kv, :], pt[:, j * 256:(j + 1) * 256], j == 0, j == 1,
                           ["v_g_e", "pT%d" % (hq % 2)], [PB[po]])
                    else:
                        mm(bank(po)[:, 0:256], v_g_o[:, bq * 2 + j, kv, :], pt[:, j * 256:(j + 1) * 256], j == 0, j == 1,
                           ["v_g_o", "pT%d" % (hq % 2)], [PB[po]])
                attn_norm(po, base, 256, mixT[base:base + 64, 4 + hq // 2, tok0:tok0 + 256], 64 if hq % 2 == 0 else 0, [], MIXN)
        checkpoint()


        close_scope(scP2)
        dma("sp", Sg[:], sgout_t[l].ap().rearrange("(r p) c -> p r c", p=128), ["sgout"], ["Sg"])
        for j in range(4):
            cp("dve", SJ[:, j, 0:16], Sg[:, j, 0:16], ["Sg"], ["SJ"])
            cp("dve", SJ[:, j, 16:32], Sg[:, 3 - j, 16:32], ["Sg"], ["SJ"])
        dma("sp", hin[:, 0, :], h0_d[l], [], ["hin"])
        for j in range(3):
            b = nextb()
            mm(bank(b)[:, 0:32], swapm[:], hin[:, j, :], True, True, ["swapm", "hin"], [PB[b]])
            tt("dve", SWs[:], bank(b)[:, 0:32], Lii[:], ALU.mult, [PB[b], "Lii"], ["SWs"])
            tt("dve", hin[:, j + 1, :], hin[:, j, :], Lrr[:], ALU.mult, ["hin", "Lrr"], ["hin"])
            tt("dve", hin[:, j + 1, :], hin[:, j + 1, :], SWs[:], ALU.add, ["hin", "SWs"], ["hin"])
            tt("dve", hin[:, j + 1, :], hin[:, j + 1, :], SJ[:, j, :], ALU.add, ["hin", "SJ"], ["hin"])
        for (c0, oo) in [(0, 0), (16, 4)]:
            ts("dve", hsel[:, c0:c0 + 16], hin[:, 0, c0:c0 + 16], oh[:, oo:oo + 1], None, ALU.mult, None, ["hin", "oh"], ["hsel"])
            for j in range(1, 4):
                stt("dve", hsel[:, c0:c0 + 16], hin[:, j, c0:c0 + 16], oh[:, oo + j:oo + j + 1], hsel[:, c0:c0 + 16], ALU.mult, ALU.add,
                    ["hin", "oh", "hsel"], ["hsel"])
        def gqa_gen():
            for kv in range(2):
                scG = open_scope()
                KgT = sb("KgT", [128, 4352], BF16)
                vge = sb("vge", [128, 34, 65], BF16)
                vgo = sb("vgo", [128, 34, 128], BF16)
                ckf = sb("ckf", [128, 2, 128])
                ckb = sb("ckb", [128, 2, 128], BF16)
                memset("pool", vge[:], 1.0, ["vge"])
                memset("pool", vgo[:], 0.0, ["vgo"])
                memset("pool", vgo[:, :, 0:1], 1.0, ["vgo"])
                for r in range(4):
                    dma("sp", KgT[:, r * 1024:(r + 1) * 1024], kg_out.ap()[r * 256 + kv * 128: r * 256 + (kv + 1) * 128, :], ["bout_%s" % kg_in.name], ["KgT"])
                    vsrc = fap(v_out, r * 1024 * 384 + 256 + kv * 64, [[384, 128], [128 * 384, 8], [1, 64]])
                    dma("sp", vge[:, r * 8:(r + 1) * 8, 0:64], vsrc, ["bout_%s" % v_in.name, "vge"], ["vge"])
                    dma("sp", vgo[:, r * 8:(r + 1) * 8, 64:128], vsrc, ["bout_%s" % v_in.name, "vgo"], ["vgo"])
                for j in range(2):
                    for rep in range(2):
                        dma("sp", ckf[:, j, rep * 64:(rep + 1) * 64], fap(cgk.tensor, l * 256 * 128 + j * 128 * 128 + kv * 64, [[128, 128], [1, 64]]), [], ["ckf"])
                cp("dve", ckb[:], ckf[:], ["ckf"], ["ckb"])
                b = nextb()
                for j in range(2):
                    tr(bank_bf(b)[:, j * 128:(j + 1) * 128], ckb[:, j, :], ident_b[:], ["ckb", "ident_b"], [PB[b]])
                cp("dve", KgT[:, 4096:4352], bank_bf(b)[:, 0:256], [PB[b]], ["KgT"])
                csrc = fap(cgv.tensor, l * 256 * 128 + kv * 64, [[128, 128], [128 * 128, 2], [1, 64]])
                dma("pool", vge[:, 32:34, 0:64], csrc, ["vge"], ["vge"])
                dma("pool", vgo[:, 32:34, 64:128], csrc, ["vgo"], ["vgo"])
                for T in range(2):
                    for hh in range(4):
                        hq = kv * 4 + hh
                        base = (hq % 2) * 64
                        po = nextb()
                        reserved.add(po)
                        def s_mm(c):
                            b = nextb()
                            mm(bank(b), KgT[base:base + 64, c * 128:(c + 1) * 128], QgT[base:base + 64, hq // 2, T * 512:(T + 1) * 512],
                               True, True, ["KgT", "QgT"], [PB[b]])
                            reserved.add(b)
                            return b
                        bcur = s_mm(0)
                        for c in range(34):
                            bnext = s_mm(c + 1) if c < 33 else None
                            pt = pT[c % 3]
                            ptn = "pT%d" % (c % 3)
                            act(pt[:], bank(bcur), AF.Exp, [PB[bcur]], [ptn], scale=0.125)
                            reserved.discard(bcur)
                            if hq % 2 == 0:
                                mm(bank(po)[0:65, :], vge[:, c, :], pt[:], c == 0, c == 33, ["vge", ptn], [PB[po]])
                            else:
                                mm(bank(po)[:, :], vgo[:, c, :], pt[:], c == 0, c == 33, ["vgo", ptn], [PB[po]])
                            bcur = bnext
                            yield
                        attn_norm(po, base, 512, mixT[base:base + 64, 4 + hq // 2, 512 + T * 512: 512 + (T + 1) * 512],
                                  64 if hq % 2 == 0 else 0, [], MIXN)
                        reserved.discard(po)
                close_scope(scG)


        ssm_pass(2, gqa_gen())
        checkpoint()

        scN = open_scope()
        KnaT = sb("KnaT", [128, 2, 2048], BF16)
        kcand = sb("kcand", [128, 2, 512], BF16)
        vsel = sb("vsel", [128, 16, 256], BF16)
        vcand = sb("vcand", [128, 4, 256], BF16)
        vle = sb("vle", [128, 16, 2, 65], BF16)
        vlo = sb("vlo", [128, 16, 2, 128], BF16)
        kctxf = sb("kctxf", [128, 2, 256])
        kctxb = sb("kctxb", [128, 2, 256], BF16)
        KcT = sb("KcT", [128, 2, 256], BF16)
        vcst = sb("vcst", [128, 2, 256], BF16)
        vce = sb("vce", [128, 2, 2, 65], BF16)
        vco = sb("vco", [128, 2, 2, 128], BF16)
        brev = sb("brev", [64, 17, 64])
        BTt = [sb("BT%d" % i, [128, 8, 64]) for i in range(2)]
        namask = sb("namask", [128, 16, 8])
        nacol = sb("nacol", [128, 64])
        jmat = sb("jmat", [64, 64])
        sls = [sb("sl%d" % i, [128, 8, 64]) for i in range(2)]
        plocs = [sb("ploc%d" % i, [128, 512], BF16) for i in range(2)]
        pctxs = [sb("pctx%d" % i, [128, 128], BF16) for i in range(2)]
        dma("sp", namask[:], namask_d, [], ["namask"])
        dma("sp", nacol[:], nacol_d, [], ["nacol"])
        dma("sp", jmat[:], jmat_d, [], ["jmat"])
        dma("sp", KnaT[:, :, 512:1536], fap(kna_in, 0, [[1024, 128], [128 * 1024, 2], [1, 1024]]), list(binn), ["KnaT"])
        dma("sp", vsel[:, 4:12, :], fap(v_in, 0, [[384, 128], [128 * 384, 8], [1, 256]]), list(binn), ["vsel"])
        for (side, ohoff, kdst, vdst, tok_off) in [("prev", 8, 0, 0, 512), ("next", 12, 1536, 12, 0)]:
            for r in range(4):
                dma("sp", kcand[:], fap(kna_out, r * 256 * 1024 + tok_off, [[1024, 128], [128 * 1024, 2], [1, 512]]), ["bout_%s" % kna_in.name], ["kcand"])
                if r == 0:
                    ts("dve", KnaT[:, :, kdst:kdst + 512], kcand[:], oh[:, ohoff:ohoff + 1], None, ALU.mult, None, ["kcand", "oh"], ["KnaT"])
                else:
                    stt("dve", KnaT[:, :, kdst:kdst + 512], kcand[:], oh[:, ohoff + r:ohoff + r + 1], KnaT[:, :, kdst:kdst + 512], ALU.mult, ALU.add,
                        ["kcand", "oh", "KnaT"], ["KnaT"])
                dma("sp", vcand[:], fap(v_out, (r * 1024 + tok_off) * 384, [[384, 128], [128 * 384, 4], [1, 256]]), ["bout_%s" % v_in.name], ["vcand"])
                if r == 0:
                    ts("dve", vsel[:, vdst:vdst + 4, :], vcand[:], oh[:, ohoff:ohoff + 1], None, ALU.mult, None, ["vcand", "oh"], ["vsel"])
                else:
                    stt("dve", vsel[:, vdst:vdst + 4, :], vcand[:], oh[:, ohoff + r:ohoff + r + 1], vsel[:, vdst:vdst + 4, :], ALU.mult, ALU.add,
                        ["vcand", "oh", "vsel"], ["vsel"])
        memset("pool", vle[:], 1.0, ["vle"])
        memset("pool", vlo[:], 0.0, ["vlo"])
        memset("pool", vlo[:, :, :, 0:1], 1.0, ["vlo"])
        cp("pool", vle[:, :, :, 0:64], fap(vsel, 0, [[4096, 128], [256, 16], [128, 2], [1, 64]]), ["vsel", "vle"], ["vle"])
        cp("pool", vlo[:, :, :, 64:128], fap(vsel, 64, [[4096, 128], [256, 16], [128, 2], [1, 64]]), ["vsel", "vlo"], ["vlo"])
        dma("sp", kctxf[:], cnak[l].rearrange("(j p) f -> p j f", p=128), [], ["kctxf"])
        cp("dve", kctxb[:], kctxf[:], ["kctxf"], ["kctxb"])
        b = nextb()
        for j in range(2):
            for c in range(2):
                tr(bank_bf(b)[:, (c * 2 + j) * 128:(c * 2 + j + 1) * 128], kctxb[:, j, c * 128:(c + 1) * 128], ident_b[:], ["kctxb", "ident_b"], [PB[b]])
        cp("dve", KcT[:].rearrange("p c t -> p (c t)"), bank_bf(b)[:, 0:512], [PB[b]], ["KcT"])
        dma("pool", vcst[:], cnav[l].rearrange("(j p) f -> p j f", p=128), [], ["vcst"])
        memset("pool", vce[:], 1.0, ["vce"])
        memset("pool", vco[:], 0.0, ["vco"])
        memset("pool", vco[:, :, :, 0:1], 1.0, ["vco"])
        cp("pool", vce[:, :, :, 0:64], fap(vcst, 0, [[512, 128], [256, 2], [128, 2], [1, 64]]), ["vcst", "vce"], ["vce"])
        cp("pool", vco[:, :, :, 64:128], fap(vcst, 64, [[512, 128], [256, 2], [128, 2], [1, 64]]), ["vcst", "vco"], ["vco"])
        for h in range(4):
            base = (h % 2) * 64
            dma("sp", brev[:], fap(nab.tensor, l * NABLEN + 128 + (h * 15 - 2) * 31 - 48, [[1, 64], [31, 17], [1, 64]]), [], ["brev"])
            for par in range(2):
                b = nextb()
                for i in range(8):
                    jj = 2 * i + 1 - par
                    mm(bank(b)[:, i * 64:(i + 1) * 64], brev[:, jj:jj + 2, :].rearrange("p a k -> p (a k)"), jmat[:], True, True,
                       ["brev", "jmat"], [PB[b]])
                tt("dve", BTt[par][:], bank(b).rearrange("p (i q) -> p i q", q=64), fap(nacol, 0, [[64, 128], [0, 8], [1, 64]]), ALU.add,
                   [PB[b], "nacol"], ["BT%d" % par])
            def na_S(r):
                wbase = r - (r % 2)
                q_ap = qnaT[base:base + 64, h // 2, 512 + r * 64: 512 + (r + 1) * 64]
                bS = nextb()
                reserved.add(bS)
                for i in range(8):
                    mm(bank(bS)[:, i * 64:(i + 1) * 64], KnaT[base:base + 64, h // 2, (wbase + 2 * i) * 64:(wbase + 2 * i + 2) * 64], q_ap,
                       True, True, ["KnaT", "qnaT"], [PB[bS]], sig=(i == 7))
                bC = nextb()
                reserved.add(bC)
                for j in range(2):
                    mm(bank(bC)[:, j * 64:(j + 1) * 64], KcT[base:base + 64, h // 2, j * 128:(j + 1) * 128], q_ap, True, True,
                       ["KcT", "qnaT"], [PB[bC]], sig=(j == 1))
                return bS, bC
            nxt = na_S(0)
            po = None
            for r in range(16):
                rg, rr = r // 8, r % 8
                if rr == 0:
                    po = nextb()
                    reserved.add(po)
                bS, bC = nxt
                nxt = na_S(r + 1) if r < 15 else None
                par = r % 2
                slt, plt, pct = sls[par], plocs[par], pctxs[par]
                nSL, nPL, nPC = "sl%d" % par, "ploc%d" % par, "pctx%d" % par
                stt("dve", slt[:], bank(bS).rearrange("p (i q) -> p i q", q=64), 0.125, BTt[par][:], ALU.mult, ALU.add,
                    [PB[bS], "BT%d" % par], [nSL])
                reserved.discard(bS)
                tt("pool", slt[:], slt[:], fap(namask, r * 8, [[128, 128], [1, 8], [0, 64]]), ALU.add, [nSL, "namask"], [nSL])
                act(plt[:], slt[:].rearrange("p i q -> p (i q)"), AF.Exp, [nSL], [nPL])
                act(pct[:], bank(bC)[:, 0:128], AF.Exp, [PB[bC]], [nPC], scale=0.125)
                reserved.discard(bC)
                oc = po
                for i in range(8):
                    ch = r // 2 + i
                    if h % 2 == 0:
                        mm(bank(oc)[0:65, rr * 64:(rr + 1) * 64], vle[:, ch, h // 2, :], plt[:, i * 64:(i + 1) * 64], i == 0, False,
                           ["vle", nPL], [PB[oc]])
                    else:
                        mm(bank(oc)[:, rr * 64:(rr + 1) * 64], vlo[:, ch, h // 2, :], plt[:, i * 64:(i + 1) * 64], i == 0, False,
                           ["vlo", nPL], [PB[oc]])
                for j in range(2):
                    if h % 2 == 0:
                        mm(bank(oc)[0:65, rr * 64:(rr + 1) * 64], vce[:, j, h // 2, :], pct[:, j * 64:(j + 1) * 64], False, j == 1,
                           ["vce", nPC], [PB[oc]])
                    else:
                        mm(bank(oc)[:, rr * 64:(rr + 1) * 64], vco[:, j, h // 2, :], pct[:, j * 64:(j + 1) * 64], False, j == 1,
                           ["vco", nPC], [PB[oc]])
                if rr == 7:
                    attn_norm(po, base, 512, mixT[base:base + 64, 2 + h // 2, 512 + rg * 512: 512 + (rg + 1) * 512],
                              64 if h % 2 == 0 else 0, [], MIXN)
                    reserved.discard(po)
        close_scope(scN)
        checkpoint()

        scA5 = open_scope()
        wout = sb("wout", [128, 8, D], BF16)
        lnrow = [sb("lnrow%d" % i, [128, D]) for i in range(2)]
        hold["xnf"] = sb("xnf", [128, D])
        hold["tmpf"] = sb("tmpf", [128, D])
        dma("pool", wout[:], w_out[l].rearrange("(k p) n -> p k n", p=128), [], ["wout"])
        dma("sp", lnrow[0][:], fap(ln1g.tensor, l * D, [[0, 128], [1, D]]), [], ["lnrow0"])
        dma("sp", lnrow[1][:], fap(ln1b.tensor, l * D, [[0, 128], [1, D]]), [], ["lnrow1"])
        for ti in range(NT):
            bl = []
            for half in range(2):
                b = nextb()
                bl.append(b)
                for k in range(8):
                    mm(bank(b), mixT[:, k, ti * 128:(ti + 1) * 128], wout[:, k, half * 512:(half + 1) * 512], k == 0, k == 7,
                       MIXN + ["wout", ACTT[ti]], [PB[b]])
            post_ln(ti, 0, lnrow[0], lnrow[1], bl)
            ln_to_T(ti, 1, 3)
        close_scope(scA5)
        close_scope(scPA)
        checkpoint()

        scF = open_scope()
        w1a = [sb("w1a%d" % i, [128, 8, 256], BF16) for i in range(2)]
        w1g = [sb("w1g%d" % i, [128, 8, 256], BF16) for i in range(2)]
        w2h = sb("w2h", [128, 11, D], BF16)
        uff = sb("uff", [128, 11, NTOK], BF16)
        sa = sb("sa", [128, 512])
        lnrow = [sb("lnrow%d" % i, [128, D]) for i in range(2)]
        xnf = sb("xnf", [128, D])
        tmpf = sb("tmpf", [128, D])
        dma("sp", lnrow[0][:], fap(ln2g.tensor, l * D, [[0, 128], [1, D]]), [], ["lnrow0"])
        dma("sp", lnrow[1][:], fap(ln2b.tensor, l * D, [[0, 128], [1, D]]), [], ["lnrow1"])
        for fh in range(2):
            ch0 = fh * 11
            dma("pool", w2h[:], w_f2[l, ch0 * 128:(ch0 + 11) * 128, :].rearrange("(k p) n -> p k n", p=128), [], ["w2h"])
            for gq in range(6):
                ncq = 2 if gq < 5 else 1
                wa, wg = w1a[gq % 2], w1g[gq % 2]
                wan, wgn = "w1a%d" % (gq % 2), "w1g%d" % (gq % 2)
                f0 = (ch0 + gq * 2) * 128
                dma("pool", wa[:, :, 0:ncq * 128], w_f1[l, :, f0:f0 + ncq * 128].rearrange("(k p) n -> p k n", p=128), [], [wan])
                dma("pool", wg[:, :, 0:ncq * 128], w_f1[l, :, DFF + f0:DFF + f0 + ncq * 128].rearrange("(k p) n -> p k n", p=128), [], [wgn])
                for T in range(3):
                    rdT = ACTT[T * 4:(T + 1) * 4]
                    for c in range(ncq):
                        ba = nextb()
                        bg = nextb()
                        for k in range(8):
                            mm(bank(ba), wa[:, k, c * 128:(c + 1) * 128], actT[:, k, T * 512:(T + 1) * 512], k == 0, k == 7, [wan] + rdT, [PB[ba]])
                        for k in range(8):
                            mm(bank(bg), wg[:, k, c * 128:(c + 1) * 128], actT[:, k, T * 512:(T + 1) * 512], k == 0, k == 7, [wgn] + rdT, [PB[bg]])
                        act(sa[:], bank(ba), AF.Silu, [PB[ba]], ["sa"])
                        tt("dve", uff[:, gq * 2 + c, T * 512:(T + 1) * 512], bank(bg), sa[:], ALU.mult, [PB[bg], "sa"], ["uff"])
            for ti in range(NT):
                xr = "xres%d" % ti
                cond = cond_of(ti)
                bl = [nextb(), nextb()]
                for c in range(11):
                    for half in range(2):
                        mm(bank(bl[half]), uff[:, c, ti * 128:(ti + 1) * 128], w2h[:, c, half * 512:(half + 1) * 512],
                           c == 0, c == 10, ["uff", "w2h"], [PB[bl[half]]])
                if fh == 0:
                    gt = gate[1][cond]
                    for half in range(2):
                        tt("dve", tmpf[:, half * 512:(half + 1) * 512], bank(bl[half]), gt[:, half * 512:(half + 1) * 512], ALU.mult,
                           [PB[bl[half]], "gate1_%d" % cond], ["tmpf"])
                    stt("dve", xres[:, ti, :], xres[:, ti, :], ALPHA, tmpf[:], ALU.mult, ALU.add, [xr, "tmpf"], [xr])
                else:
                    gt = gate[1][cond]
                    for half in range(2):
                        tt("dve", tmpf[:, half * 512:(half + 1) * 512], bank(bl[half]), gt[:, half * 512:(half + 1) * 512], ALU.mult,
                           [PB[bl[half]], "gate1_%d" % cond], ["tmpf"])
                    tt("dve", xres[:, ti, :], xres[:, ti, :], tmpf[:], ALU.add, [xr, "tmpf"], [xr])
                    layernorm_stats(xres[:, ti, :], [xr])
                    act(xnf[:], xres[:, ti, :], AF.Identity, [xr, "rstd", "nmr"], ["xnf"], scale=rstd[:], bias=nmr[:])
                    tt("pool", xnf[:], xnf[:], lnrow[0][:], ALU.mult, ["xnf", "lnrow0"], ["xnf"])
                    tt("pool", xres[:, ti, :], xnf[:], lnrow[1][:], ALU.add, ["xnf", "lnrow1"], [xr])
        close_scope(scF)
        checkpoint()
    except _Stop:
        S.barrier()
        while len(cur) > 1:
            cur.pop().close()

    for ti in range(NT):
        dst = yp[ti * 128:(ti + 1) * 128, :] if ti < 4 else ys[(ti - 4) * 128:(ti - 3) * 128, :]
        dma("sp", dst, xres[:, ti, :], ["xres%d" % ti], [oname()])
    S.wait_all("sp", outs)
    with nc.Block() as block:
        S.run(block)
    es.close()
    return nc


_CACHE = {}


def _consts():
    ident = np.eye(128, dtype=np.float32)
    sp1 = np.tile(np.arange(1, 1025, dtype=np.float32)[None, :], (128, 1))
    rowmask = np.zeros((128, 8), np.float32)
    colmask = np.zeros((128, 8, 128), np.float32)
    for g8 in range(8):
        rowmask[g8 * 16:(g8 + 1) * 16, g8] = 1.0
        colmask[:, g8, g8 * 16:(g8 + 1) * 16] = 1.0
    swap = np.zeros((128, 128), np.float32)
    for p in range(64):
        swap[p, 64 + p] = 1.0
        swap[64 + p, p] = 1.0
    sgn = np.ones((128, 1), np.float32)
    sgn[0:64] = -1.0
    jmat = np.zeros((64, 64), np.float32)
    for q in range(64):
        jmat[63 - q, q] = 1.0
    kc = np.arange(64)[:, None]
    qc = np.arange(64)[None, :]
    cs = np.clip(qc - 8, 0, 48)
    ok = (kc >= cs) & (kc < cs + 16)
    nacol1 = np.where(ok, 0.0, NEG).astype(np.float32)
    nacol = np.concatenate([nacol1, nacol1], axis=0)
    return dict(c_ident=ident, c_sp1=sp1, c_rowmask=rowmask, c_colmask=colmask, c_swap=swap, c_sgn=sgn, jmat=jmat, nacol=nacol)


def _core_consts(q):
    nf = 16
    inv = (np.float32(10000.0) ** (-np.arange(nf, dtype=np.float32) / np.float32(nf))).astype(np.float32)
    t = q * 1024 + np.arange(1024)
    rows = (t // 64).astype(np.float32)
    cols = (t % 64).astype(np.float32)
    ang = np.stack([rows[:, None] * inv[None, :], cols[:, None] * inv[None, :]], axis=1).astype(np.float32)
    cosv = np.cos(ang.astype(np.float64)).astype(np.float32)
    sinv = np.sin(ang.astype(np.float64)).astype(np.float32)
    cos_full = np.stack([cosv, cosv], axis=2)
    sin_sgn = np.stack([-sinv, sinv], axis=2)
    ropec = np.tile(cos_full.reshape(1024, 1, 64), (1, 10, 1)).reshape(1024, 640).astype(np.float32)
    ropes = np.tile(sin_sgn.reshape(1024, 1, 64), (1, 10, 1)).reshape(1024, 640).astype(np.float32)
    R0 = 16 * q
    namask = np.full((128, 16, 8), NEG, np.float32)
    for r in range(16):
        rg = R0 + r
        par = r % 2
        rs = min(max(rg - 4, 0), 56)
        for i in range(8):
            for half in range(2):
                kr = rg - 8 - par + 2 * i + half
                if rs <= kr < rs + 8:
                    namask[half * 64:(half + 1) * 64, r, i] = 0.0
    oh = np.zeros((128, 16), np.float32)
    oh[:, 0 + q] = 1.0
    oh[:, 4 + (3 - q)] = 1.0
    if q > 0:
        oh[:, 8 + (q - 1)] = 1.0
    if q < 3:
        oh[:, 12 + (q + 1)] = 1.0
    return dict(ropec=ropec, ropes=ropes, namask=namask, oh=oh)


def kernel(x_prompt, x_sample, c, cache_na_k, cache_na_v, cache_gqa_k, cache_gqa_v, state_ssm_re, state_ssm_im,
           c_ctx, w_ada, b_ada, w_in, w_out, q_norm_g, k_norm_g, na_bias, ssm_lam_re, ssm_lam_im, ssm_log_dt,
           ssm_b_re, ssm_b_im, ssm_c_re, ssm_c_im, ssm_d, w_ssm_glu, ln1_g, ln1_b, ln2_g, ln2_b,
           w_ffn_in, w_ffn_out):
    f = lambda a: np.ascontiguousarray(np.asarray(a, dtype=np.float32))
    if "nc" not in _CACHE:
        import os
        _CACHE["nc"] = build_program(int(os.environ.get("KSTOP", "99")))
    nc = _CACHE["nc"]
    consts = _consts()
    x_prompt = f(x_prompt); x_sample = f(x_sample); c = f(c); c_ctx = f(c_ctx)
    b_adaT = f(np.transpose(f(b_ada).reshape(DEPTH, 48, 128), (0, 2, 1)))
    g10 = f(np.concatenate([np.tile(f(q_norm_g), (1, 8)), np.tile(f(k_norm_g), (1, 2))], axis=1))

    def pdup(a):
        t = np.transpose(f(a).reshape(DEPTH, 32, 64), (0, 2, 1))
        return f(np.concatenate([t, t], axis=1))
    lamre = pdup(ssm_lam_re); lamim = pdup(ssm_lam_im)
    logdt = f(f(ssm_log_dt).reshape(DEPTH, 32))

    def bl(a):
        return f(np.transpose(f(a), (0, 3, 1, 2, 4)).reshape(DEPTH, 64, 512))

    def cl(a):
        t = f(a).reshape(DEPTH, 2, 2, 8, 16, 64)
        return f(np.transpose(t, (0, 3, 4, 1, 2, 5)).reshape(DEPTH, 128, 4, 64))
    ssmd = f(np.transpose(f(ssm_d).reshape(DEPTH, 2, 128), (0, 2, 1)))
    nab = np.zeros((DEPTH, NABLEN), np.float32)
    nab[:, 128:128 + 4 * 15 * 31] = f(na_bias).reshape(DEPTH, -1)
    common = dict(w_ada=f(w_ada), b_adaT=b_adaT, b_ada=f(b_ada), w_in=f(w_in), w_out=f(w_out), g10=g10,
                  lamre=lamre, lamim=lamim, logdt=logdt, bre=bl(ssm_b_re), bim=bl(ssm_b_im), cre=cl(ssm_c_re), cim=cl(ssm_c_im),
                  ssmd=ssmd, wglu=f(w_ssm_glu), ln1g=f(ln1_g), ln1b=f(ln1_b), ln2g=f(ln2_g), ln2b=f(ln2_b),
                  w_f1=f(w_ffn_in), w_f2=f(w_ffn_out), nab=nab)
    common.update(consts)
    cache_na_k = f(cache_na_k); cache_na_v = f(cache_na_v); cache_gqa_k = f(cache_gqa_k); cache_gqa_v = f(cache_gqa_v)
    state_ssm_re = f(state_ssm_re); state_ssm_im = f(state_ssm_im)
    cc = [_core_consts(q) for q in range(4)]
    in_maps = []
    for core in range(8):
        bs = core // 4
        q = core % 4
        cond = np.stack([c_ctx, c[bs]], axis=0)
        condT = f(np.transpose(cond.reshape(2, 8, 128), (2, 1, 0)))
        m = dict(common)
        m.update(cc[q])
        m["xp"] = f(x_prompt[2 * core:2 * core + 2].reshape(NPT, D))
        m["xs"] = f(x_sample[bs, q * 1024:(q + 1) * 1024])
        m["condT"] = condT
        m["cnak"] = f(cache_na_k[bs].reshape(DEPTH, 256, 256))
        m["cnav"] = f(cache_na_v[bs].reshape(DEPTH, 256, 256))
        m["cgk"] = f(cache_gqa_k[bs].reshape(DEPTH, 256, 128))
        m["cgv"] = f(cache_gqa_v[bs].reshape(DEPTH, 256, 128))
        sre_t = np.transpose(state_ssm_re[bs].reshape(DEPTH, 32, 64), (0, 2, 1))
        sim_t = np.transpose(state_ssm_im[bs].reshape(DEPTH, 32, 64), (0, 2, 1))
        m["h0"] = f(np.concatenate([sre_t, sim_t], axis=1))
        in_maps.append(m)
    res = run_bass_kernel_spmd(nc, in_maps, core_ids=list(range(8)))
    r = res.results
    y_prompt = np.concatenate([r[i]["yp"].reshape(2, SEQ, D) for i in range(8)], axis=0)
    y_sample = np.stack([np.concatenate([r[b * 4 + q]["ys"] for q in range(4)], axis=0) for b in range(2)], axis=0)
    nak = np.concatenate([r[i]["o_nak"].reshape(2, DEPTH, SEQ, 4, 64) for i in range(8)], axis=0)
    nav = np.concatenate([r[i]["o_nav"].reshape(2, DEPTH, SEQ, 4, 64) for i in range(8)], axis=0)
    gk = np.concatenate([r[i]["o_gk"].reshape(2, DEPTH, SEQ, 2, 64) for i in range(8)], axis=0)
    gv = np.concatenate([r[i]["o_gv"].reshape(2, DEPTH, SEQ, 2, 64) for i in range(8)], axis=0)
    sre = np.concatenate([r[i]["o_sre"].reshape(2, DEPTH, 2, 16, 64) for i in range(8)], axis=0)
    sim = np.concatenate([r[i]["o_sim"].reshape(2, DEPTH, 2, 16, 64) for i in range(8)], axis=0)
    return (y_prompt.astype(np.float32), y_sample.astype(np.float32), nak, nav, gk, gv, sre, sim)
```

```python
import math
import numpy as np
import concourse.bass as bass
import concourse.mybir as mybir
from concourse.bass_utils import run_bass_kernel_spmd
from contextlib import ExitStack

F32 = mybir.dt.float32
BF16 = mybir.dt.bfloat16
AF = mybir.ActivationFunctionType
ALU = mybir.AluOpType
AX = mybir.AxisListType

D = 1024
DEPTH = 2
SEQ = 256
NPT = 512
IN_W = 1792
DFF = 2816
ALPHA = (2 * DEPTH) ** 0.25
LN_EPS = 1e-6
RMS_EPS = 1e-6
TWO_PI = 2.0 * math.pi
MAGIC = 12582912.0
N_DMA_SEMS = 40
N_SW_SEMS = 12
SAME_ENGINE_SYNC = True


class Res:
    __slots__ = ("name", "w", "r")

    def __init__(self, name):
        self.name = name
        self.w = None
        self.r = []


class Sched:
    def __init__(self, nc, es):
        self.nc = nc
        self.eng = {"pe": nc.tensor, "dve": nc.vector, "act": nc.scalar, "pool": nc.gpsimd, "sp": nc.sync}
        self.sem = {}
        self.cnt = {}
        self.prog = {k: [] for k in self.eng}
        self.seen = {k: {} for k in self.eng}
        for k in self.eng:
            self.sem[k] = es.enter_context(nc.semaphore("s_" + k))
            self.cnt[k] = 0
        self.dsem = [es.enter_context(nc.semaphore("d%d" % i)) for i in range(N_DMA_SEMS)]
        self.dcnt = [0] * N_DMA_SEMS
        self.csem = [es.enter_context(nc.semaphore("c%d" % i)) for i in range(10)]
        self.cnext = 0
        self.dnext = 0
        self.dnext_sw = 0
        self.rs = {}

    def R(self, name):
        r = self.rs.get(name)
        if r is None:
            r = Res(name)
            self.rs[name] = r
        return r

    def _semobj(self, key):
        if isinstance(key, str):
            return self.sem[key]
        if key >= 1000:
            return self.csem[key - 1000]
        return self.dsem[key]

    def coll(self, fn, reads=(), writes=()):
        reads = [self.R(x) if isinstance(x, str) else x for x in reads]
        writes = [self.R(x) if isinstance(x, str) else x for x in writes]
        waits = self._deps("pool", reads, writes)
        k = 1000 + self.cnext
        self.cnext += 1
        tok = (k, 1)
        self.prog["pool"].append((waits, fn, k))
        self._commit(tok, reads, writes)

    def _deps(self, e, reads, writes):
        need = {}

        def add(tok):
            if tok is None:
                return
            k, v = tok
            if k == e and (e == "pe" or not SAME_ENGINE_SYNC):
                return
            if need.get(k, 0) < v:
                need[k] = v
        for r in reads:
            add(r.w)
        for w in writes:
            add(w.w)
            for t in w.r:
                add(t)
        waits = []
        seen = self.seen[e]
        for k, v in need.items():
            if seen.get(k, 0) >= v:
                continue
            seen[k] = v
            waits.append((k, v))
        return waits

    def _commit(self, tok, reads, writes):
        for r in reads:
            if r not in writes:
                r.r.append(tok)
        for w in writes:
            w.w = tok
            w.r = []

    def op(self, e, fn, reads=(), writes=(), sig=True):
        reads = [self.R(x) if isinstance(x, str) else x for x in reads]
        writes = [self.R(x) if isinstance(x, str) else x for x in writes]
        for r in reads:
            if r.name.startswith("psb") and r not in writes:
                writes.append(r)
        waits = self._deps(e, reads, writes)
        if sig:
            self.cnt[e] += 1
            tok = (e, self.cnt[e])
            self.prog[e].append((waits, fn, None))
        else:
            tok = (e, self.cnt[e] + 1)
            self.prog[e].append((waits, fn, "nosig"))
        self._commit(tok, reads, writes)

    def dma(self, q, fn, reads=(), writes=()):
        reads = [self.R(x) if isinstance(x, str) else x for x in reads]
        writes = [self.R(x) if isinstance(x, str) else x for x in writes]
        if q == "pool":
            k = self.dnext_sw
            self.dnext_sw = (self.dnext_sw + 1) % N_SW_SEMS
        else:
            k = N_SW_SEMS + self.dnext
            self.dnext = (self.dnext + 1) % (N_DMA_SEMS - N_SW_SEMS)
        waits = self._deps(q, reads, writes)
        prev = self.dcnt[k]
        if prev > 0 and self.seen[q].get(k, 0) < prev:
            self.seen[q][k] = prev
            waits.append((k, prev))
        self.dcnt[k] += 16
        tok = (k, self.dcnt[k])
        self.prog[q].append((waits, fn, k))
        self._commit(tok, reads, writes)

    def barrier(self):
        for e in self.eng:
            waits = []
            for k in self.eng:
                if k != e and self.cnt[k] > self.seen[e].get(k, 0):
                    self.seen[e][k] = self.cnt[k]
                    waits.append((k, self.cnt[k]))
            for k in range(N_DMA_SEMS):
                if self.dcnt[k] > self.seen[e].get(k, 0):
                    self.seen[e][k] = self.dcnt[k]
                    waits.append((k, self.dcnt[k]))
            self.prog[e].append((waits, None, None))

    def wait_all(self, e, resources):
        resources = [self.R(x) if isinstance(x, str) else x for x in resources]
        waits = self._deps(e, resources, ())
        self.prog[e].append((waits, None, None))

    def run(self, block):
        sch = self

        def mk(ename):
            def body(engine):
                for waits, fn, dk in sch.prog[ename]:
                    for k, v in waits:
                        engine.wait_ge(sch._semobj(k), v)
                    if fn is None:
                        continue
                    inst = fn(engine)
                    if dk is None:
                        inst.then_inc(sch.sem[ename], 1)
                    elif dk == "nosig":
                        pass
                    elif dk >= 1000:
                        inst.then_inc(sch.csem[dk - 1000], 1)
                    else:
                        inst.then_inc(sch.dsem[dk], 16)
            return body
        block.tensor(mk("pe"))
        block.vector(mk("dve"))
        block.scalar(mk("act"))
        block.gpsimd(mk("pool"))
        block.sync(mk("sp"))


def fap(t, off, dims):
    return bass.AP(t, off, [list(d) for d in dims])


class _Stop(Exception):
    pass


NT = 12
NTOK = 1536
BROWS = 896
NEG = -30000.0
NABLEN = 128 + 4 * 15 * 31 + 256


def build_program(kstop=99):
    nc = bass.Bass("TRN2", target_bir_lowering=False)
    stage = {"n": 0}

    def checkpoint():
        stage["n"] += 1
        if stage["n"] >= kstop:
            raise _Stop()

    def din(name, shape, dt=F32):
        return nc.dram_tensor(name, list(shape), dt, kind="ExternalInput").ap()

    def dout(name, shape):
        return nc.dram_tensor(name, list(shape), F32, kind="ExternalOutput").ap()

    xp = din("xp", [NPT, D])
    xs = din("xs", [1024, D])
    condT = din("condT", [128, 8, 2])
    w_ada = din("w_ada", [DEPTH, D, 6 * D])
    b_adaT = din("b_adaT", [DEPTH, 128, 48])
    b_ada = din("b_ada", [DEPTH, 6 * D])
    w_in = din("w_in", [DEPTH, D, IN_W])
    w_out = din("w_out", [DEPTH, D, D])
    g10 = din("g10", [DEPTH, 640])
    lamre = din("lamre", [DEPTH, 128, 32])
    lamim = din("lamim", [DEPTH, 128, 32])
    logdt = din("logdt", [DEPTH, 32])
    bre = din("bre", [DEPTH, 64, 512])
    bim = din("bim", [DEPTH, 64, 512])
    cre = din("cre", [DEPTH, 128, 4, 64])
    cim = din("cim", [DEPTH, 128, 4, 64])
    ssmd = din("ssmd", [DEPTH, 128, 2])
    wglu = din("wglu", [DEPTH, 256, 256])
    ln1g = din("ln1g", [DEPTH, D])
    ln1b = din("ln1b", [DEPTH, D])
    ln2g = din("ln2g", [DEPTH, D])
    ln2b = din("ln2b", [DEPTH, D])
    w_f1 = din("w_f1", [DEPTH, D, 2 * DFF])
    w_f2 = din("w_f2", [DEPTH, DFF, D])
    c_ident = din("c_ident", [128, 128])
    c_sp1 = din("c_sp1", [128, 1024])
    c_rowmask = din("c_rowmask", [128, 8])
    c_colmask = din("c_colmask", [128, 8, 128])
    c_swap = din("c_swap", [128, 128])
    c_sgn = din("c_sgn", [128, 1])
    ropec = din("ropec", [1024, 640])
    ropes = din("ropes", [1024, 640])
    namask_d = din("namask", [128, 16, 8])
    nacol_d = din("nacol", [128, 64])
    jmat_d = din("jmat", [64, 64])
    oh_d = din("oh", [128, 16])
    h0_d = din("h0", [DEPTH, 128, 32])
    cnak = din("cnak", [DEPTH, 256, 256])
    cnav = din("cnav", [DEPTH, 256, 256])
    cgk = din("cgk", [DEPTH, 256, 128])
    cgv = din("cgv", [DEPTH, 256, 128])
    nab = din("nab", [DEPTH, NABLEN])

    yp = dout("yp", [NPT, D])
    ys = dout("ys", [1024, D])
    o_nak = dout("o_nak", [2, DEPTH, SEQ, 256])
    o_nav = dout("o_nav", [2, DEPTH, SEQ, 256])
    o_gk = dout("o_gk", [2, DEPTH, SEQ, 128])
    o_gv = dout("o_gv", [2, DEPTH, SEQ, 128])
    o_sre = dout("o_sre", [2, DEPTH, 32, 64])
    o_sim = dout("o_sim", [2, DEPTH, 32, 64])

    kna_in_t = [nc.dram_tensor("kna_in%d" % l, [256, 1024], BF16) for l in range(DEPTH)]
    kna_out_t = [nc.dram_tensor("kna_out%d" % l, [1024, 1024], BF16) for l in range(DEPTH)]
    kg_in_t = [nc.dram_tensor("kg_in%d" % l, [256, 1024], BF16) for l in range(DEPTH)]
    kg_out_t = [nc.dram_tensor("kg_out%d" % l, [1024, 1024], BF16) for l in range(DEPTH)]
    v_in_t = [nc.dram_tensor("v_in%d" % l, [1024, 384], BF16) for l in range(DEPTH)]
    v_out_t = [nc.dram_tensor("v_out%d" % l, [4096, 384], BF16) for l in range(DEPTH)]
    sgin_t = [nc.dram_tensor("sg_in%d" % l, [128, 32], F32) for l in range(DEPTH)]
    sgout_t = [nc.dram_tensor("sg_out%d" % l, [512, 32], F32) for l in range(DEPTH)]

    es = ExitStack()
    S = Sched(nc, es)
    cur = [es]
    uniq = {"n": 0}

    def sb(name, shape, dt=F32):
        uniq["n"] += 1
        return cur[-1].enter_context(nc.sbuf_tensor("%s_%d" % (name, uniq["n"]), list(shape), dt))

    def open_scope():
        sc = ExitStack()
        cur.append(sc)
        return sc

    def close_scope(sc):
        assert cur[-1] is sc
        S.barrier()
        cur.pop()
        sc.close()

    def mm(out, lhsT, rhs, start, stop, rd, wr, sig=None):
        if sig is None:
            sig = bool(stop)
        S.op("pe", lambda e: e.matmul(out, lhsT=lhsT, rhs=rhs, start=start, stop=stop), rd, wr, sig=sig)

    def tr(out, in_, ident, rd, wr):
        S.op("pe", lambda e: e.transpose(out, in_, ident), rd, wr)

    def act(out, in_, func, rd, wr, scale=1.0, bias=None):
        if bias is None:
            S.op("act", lambda e: e.activation(out=out, in_=in_, func=func, scale=scale), rd, wr)
        else:
            S.op("act", lambda e: e.activation(out=out, in_=in_, func=func, scale=scale, bias=bias), rd, wr)

    def tt(eng, out, in0, in1, op, rd, wr):
        S.op(eng, lambda e: e.tensor_tensor(out=out, in0=in0, in1=in1, op=op), rd, wr)

    def ts(eng, out, in0, s1, s2, op0, op1, rd, wr):
        if op1 is None:
            S.op(eng, lambda e: e.tensor_scalar(out=out, in0=in0, scalar1=s1, scalar2=None, op0=op0), rd, wr)
        else:
            S.op(eng, lambda e: e.tensor_scalar(out=out, in0=in0, scalar1=s1, scalar2=s2, op0=op0, op1=op1), rd, wr)

    def stt(eng, out, in0, scalar, in1, op0, op1, rd, wr):
        S.op("dve", lambda e: e.scalar_tensor_tensor(out=out, in0=in0, scalar=scalar, in1=in1, op0=op0, op1=op1), rd, wr)

    def cp(eng, out, in_, rd, wr):
        if eng == "act":
            act(out, in_, AF.Identity, rd, wr)
        else:
            S.op(eng, lambda e: e.tensor_copy(out=out, in_=in_), rd, wr)

    def treduce(out, in_, rd, wr):
        S.op("dve", lambda e: e.tensor_reduce(out=out, in_=in_, op=ALU.add, axis=AX.X), rd, wr)

    def recip(out, in_, rd, wr):
        S.op("dve", lambda e: e.reciprocal(out=out, in_=in_), rd, wr)

    def scan(out, d0, d1, init, rd, wr):
        S.op("dve", lambda e: e.tensor_tensor_scan(out=out, data0=d0, data1=d1, initial=init, op0=ALU.mult, op1=ALU.add), rd, wr)

    def dma(q, out, in_, rd, wr):
        S.dma(q, lambda e: e.dma_start(out=out, in_=in_), rd, wr)

    outs = []

    def oname():
        n = "out%d" % len(outs)
        outs.append(n)
        return n

    def memset(eng, out, val, wr):
        S.op(eng, lambda e: e.memset(out, val), (), wr)

    ps_all = es.enter_context(nc.psum_tensor("ps_all", [128, 4096], F32))

    def bank(b):
        return ps_all[:, b * 512:(b + 1) * 512]

    def bank_bf(b):
        return ps_all[:, b * 512:(b + 1) * 512].bitcast(BF16)

    PB = ["psb%d" % b for b in range(8)]
    rot = {"i": 0}
    reserved = set()

    def nextb():
        while True:
            b = rot["i"]
            rot["i"] = (b + 1) % 8
            if b not in reserved:
                return b

    ident_f = sb("ident_f", [128, 128])
    ident_b = sb("ident_b", [128, 128], BF16)
    ones_f = sb("ones_f", [128, 128])
    sp1 = sb("sp1", [128, 1024])
    rowmask = sb("rowmask", [128, 8])
    nrowmask = sb("nrowmask", [128, 8])
    colmask = sb("colmask", [128, 8, 128], BF16)
    swapm = sb("swapm", [128, 128])
    sgn = sb("sgn", [128, 1])
    halfpi = sb("halfpi", [128, 1])
    oh = sb("oh", [128, 16])
    dma("sp", ident_f[:], c_ident, [], ["ident_f"])
    cp("dve", ident_b[:], ident_f[:], ["ident_f"], ["ident_b"])
    memset("dve", ones_f[:], 1.0, ["ones_f"])
    memset("dve", halfpi[:], math.pi / 2.0, ["halfpi"])
    dma("sp", sp1[:], c_sp1, [], ["sp1"])
    dma("sp", rowmask[:], c_rowmask, [], ["rowmask"])
    ts("dve", nrowmask[:], rowmask[:], -1.0, None, ALU.mult, None, ["rowmask"], ["nrowmask"])
    _sc0 = ExitStack()
    cur.append(_sc0)
    colmask_f = sb("colmask_f", [128, 8, 128])
    dma("sp", colmask_f[:], c_colmask, [], ["colmask_f"])
    cp("dve", colmask[:], colmask_f[:], ["colmask_f"], ["colmask"])
    S.barrier()
    cur.pop()
    _sc0.close()
    dma("sp", swapm[:], c_swap, [], ["swapm"])
    dma("sp", sgn[:], c_sgn, [], ["sgn"])
    dma("sp", oh[:], oh_d, [], ["oh"])

    xres = sb("xres", [128, NT, D])
    for ti in range(NT):
        src = xp[ti * 128:(ti + 1) * 128, :] if ti < 4 else xs[(ti - 4) * 128:(ti - 3) * 128, :]
        dma("sp", xres[:, ti, :], src, [], ["xres%d" % ti])

    def cond_of(ti):
        return 0 if ti < 4 else 1

    sc_f = sb("sc_f", [128, 8, 2])
    sc_b = sb("sc_b", [128, 8, 2], BF16)
    mT = sb("mT", [128, 48, 2])
    scp1 = sb("scp1", [128, 16, 2])
    bT = sb("bT", [128, 48])
    gate = [[sb("gate%d_%d" % (i, c), [128, D]) for c in range(2)] for i in range(2)]
    xn = [sb("xn%d" % i, [128, D], BF16) for i in range(2)]
    stats = sb("stats", [128, 2, 6])
    mv = sb("mv", [128, 2])
    rstd = sb("rstd", [128, 1])
    nmr = sb("nmr", [128, 1])
    hold = {}
    actT = sb("actT", [128, 8, NTOK], BF16)
    pT = [sb("pT%d" % i, [128, 512], BF16) for i in range(3)]
    rinv = sb("rinv", [128, 512])
    bcs = sb("bcs", [128, 512])
    lre = sb("lre", [128, 32]); lim = sb("lim", [128, 32]); ldt = sb("ldt", [128, 32])
    s_a = sb("s_a", [128, 32]); s_rho = sb("s_rho", [128, 32]); s_turn = sb("s_turn", [128, 32])
    s_t1 = sb("s_t1", [128, 32]); s_t2 = sb("s_t2", [128, 32]); s_t3 = sb("s_t3", [128, 32])
    s_c1 = sb("s_c1", [128, 32]); s_s1 = sb("s_s1", [128, 32]); s_q1 = sb("s_q1", [128, 32]); s_q2 = sb("s_q2", [128, 32])
    s_cT = sb("s_cT", [128, 32]); s_sT = sb("s_sT", [128, 32])
    s_cK = sb("s_cK", [128, 32]); s_sK = sb("s_sK", [128, 32])
    Lrr = sb("Lrr", [128, 32]); Lii = sb("Lii", [128, 32])
    kre = sb("kre", [128, 32]); kim = sb("kim", [128, 32])
    GL = sb("GL", [128, 64]); SWt = sb("SWt", [128, 64]); Sfin = sb("Sfin", [128, 64]); SfT = sb("SfT", [64, 128])
    GLs = sb("GLs", [128, 32]); Sloc = sb("Sloc", [128, 32]); SWs = sb("SWs", [128, 32])
    Sg = sb("Sg", [128, 4, 32]); SJ = sb("SJ", [128, 4, 32]); hin = sb("hin", [128, 4, 32]); hsel = sb("hsel", [128, 32])
    ssmd_t = sb("ssmd_t", [128, 2])

    def sincos(turn, sin_out, cos_out, tmp_r, tmp_a, rd, wrn):
        ts("dve", tmp_r, turn, MAGIC, MAGIC, ALU.add, ALU.subtract, rd, ["sc_tmp_r"])
        tt("dve", tmp_r, turn, tmp_r, ALU.subtract, rd + ["sc_tmp_r"], ["sc_tmp_r"])
        stt("dve", tmp_a, tmp_r, -1.0, tmp_r, ALU.mult, ALU.max, ["sc_tmp_r"], ["sc_tmp_a"])
        act(sin_out, tmp_r, AF.Sin, ["sc_tmp_r"], [wrn + "_s"], scale=TWO_PI)
        act(cos_out, tmp_a, AF.Sin, ["sc_tmp_a", "halfpi"], [wrn + "_c"], scale=-TWO_PI, bias=halfpi[:])

    def layernorm_stats(x_ap, rd):
        S.op("dve", lambda e: e.bn_stats(out=stats[:, 0, :], in_=x_ap[:, 0:512]), rd, ["stats"])
        S.op("dve", lambda e: e.bn_stats(out=stats[:, 1, :], in_=x_ap[:, 512:1024]), rd + ["stats"], ["stats"])
        S.op("dve", lambda e: e.bn_aggr(out=mv[:], in_=stats[:].rearrange("p a b -> p (a b)")), ["stats"], ["mv"])
        ts("dve", rstd[:], mv[:, 1:2], LN_EPS, None, ALU.add, None, ["mv"], ["rstd"])
        act(rstd[:], rstd[:], AF.Sqrt, ["rstd"], ["rstd"])
        recip(rstd[:], rstd[:], ["rstd"], ["rstd"])
        stt("dve", nmr[:], mv[:, 0:1], -1.0, rstd[:], ALU.mult, ALU.mult, ["mv", "rstd"], ["nmr"])

    def ln_to_T(ti, sc_idx, sh_idx):
        cond = cond_of(ti)
        xr = "xres%d" % ti
        x_ap = xres[:, ti, :]
        layernorm_stats(x_ap, [xr])
        xb = xn[ti % 2]
        xbn = "xn%d" % (ti % 2)
        act(xb[:], x_ap, AF.Identity, [xr, "rstd", "nmr"], [xbn], scale=rstd[:], bias=nmr[:])
        b = nextb()
        for k in range(8):
            tr(bank_bf(b)[:, k * 128:(k + 1) * 128], xb[:, k * 128:(k + 1) * 128], ident_b[:], [xbn, "ident_b"], [PB[b]])
        for k in range(8):
            if k % 2 == 1:
                act(actT[:, k, ti * 128:(ti + 1) * 128], bank_bf(b)[:, k * 128:(k + 1) * 128], AF.Identity,
                    [PB[b], "scp1", "mT"], ["actT%d" % ti], scale=scp1[:, sc_idx * 8 + k, cond:cond + 1],
                    bias=mT[:, sh_idx * 8 + k, cond:cond + 1])
            else:
                ts("dve", actT[:, k, ti * 128:(ti + 1) * 128], bank_bf(b)[:, k * 128:(k + 1) * 128],
                   scp1[:, sc_idx * 8 + k, cond:cond + 1], mT[:, sh_idx * 8 + k, cond:cond + 1], ALU.mult, ALU.add,
                   [PB[b], "scp1", "mT"], ["actT%d" % ti])

    def attn_norm(po_b, base, ncols, dst_ap, sumrow, rd_extra, wr):
        S.op("dve", lambda e: e.reciprocal(out=rinv[sumrow:sumrow + 1, 0:ncols], in_=bank(po_b)[sumrow:sumrow + 1, 0:ncols]),
             [PB[po_b]], ["rinv"])
        bb = nextb()
        mm(bank(bb)[:, 0:ncols], ones_f[sumrow:sumrow + 1, 0:128], rinv[sumrow:sumrow + 1, 0:ncols], True, True,
           ["ones_f", "rinv"], [PB[bb]])
        cp("act", bcs[base:base + 64, 0:ncols], bank(bb)[base:base + 64, 0:ncols], [PB[bb]], ["bcs"])
        tt("dve", dst_ap, bank(po_b)[base:base + 64, 0:ncols], bcs[base:base + 64, 0:ncols], ALU.mult,
           [PB[po_b], "bcs"] + rd_extra, wr)

    ACTT = ["actT%d" % ti for ti in range(NT)]

    def post_ln(ti, gi, lnr0, lnr1, bl):
        xr = "xres%d" % ti
        cond = cond_of(ti)
        gt = gate[gi][cond]
        gn = "gate%d_%d" % (gi, cond)
        tmpf, xnf = hold["tmpf"], hold["xnf"]
        for half in range(2):
            tt("dve", tmpf[:, half * 512:(half + 1) * 512], bank(bl[half]), gt[:, half * 512:(half + 1) * 512], ALU.mult,
               [PB[bl[half]], gn], ["tmpf"])
        stt("dve", xres[:, ti, :], xres[:, ti, :], ALPHA, tmpf[:], ALU.mult, ALU.add, [xr, "tmpf"], [xr])
        layernorm_stats(xres[:, ti, :], [xr])
        act(xnf[:], xres[:, ti, :], AF.Identity, [xr, "rstd", "nmr"], ["xnf"], scale=rstd[:], bias=nmr[:])
        tt("pool", xnf[:], xnf[:], lnr0[:], ALU.mult, ["xnf", "lnrow0"], ["xnf"])
        tt("pool", xres[:, ti, :], xnf[:], lnr1[:], ALU.add, ["xnf", "lnrow1"], [xr])

    try:
      for l in range(DEPTH):
        kna_in, kna_out, kg_in, kg_out, v_in, v_out = kna_in_t[l], kna_out_t[l], kg_in_t[l], kg_out_t[l], v_in_t[l], v_out_t[l]
        binn = []

        def bname():
            n = "bin%d_%d" % (l, len(binn))
            binn.append(n)
            return n
        scW = open_scope()
        wada = [sb("wada%d" % i, [128, 8, 1024], BF16) for i in range(2)]
        sc_rep = sb("sc_rep", [128, 8, 2, 128], BF16)
        brow = sb("brow", [128, D])
        if l == 0:
            dma("sp", sc_f[:], condT, [], ["sc_f"])
            act(sc_f[:], sc_f[:], AF.Silu, ["sc_f"], ["sc_f"])
            cp("dve", sc_b[:], sc_f[:], ["sc_f"], ["sc_b"])
        cp("dve", sc_rep[:], fap(sc_f, 0, [[16, 128], [2, 8], [1, 2], [0, 128]]), ["sc_f"], ["sc_rep"])
        dma("sp", bT[:], b_adaT[l], [], ["bT"])
        for piece in range(6):
            wb = wada[piece % 2]
            wn = "wada%d" % (piece % 2)
            dma("pool", wb[:], w_ada[l, :, piece * 1024:(piece + 1) * 1024].rearrange("(k p) n -> p k n", p=128), [], [wn])
            if piece in (2, 5):
                gi = 0 if piece == 2 else 1
                dma("sp", brow[:], fap(b_ada.tensor, l * 6 * D + piece * 1024, [[0, 128], [1, 1024]]), [], ["brow"])
                for cond in range(2):
                    for half in range(2):
                        b = nextb()
                        for k in range(8):
                            mm(bank(b), sc_rep[:, k, cond, :], wb[:, k, half * 512:(half + 1) * 512], k == 0, k == 7,
                               ["sc_rep", wn], [PB[b]])
                        tt("dve", gate[gi][cond][:, half * 512:(half + 1) * 512], bank(b), brow[:, half * 512:(half + 1) * 512], ALU.add,
                           [PB[b], "brow"], ["gate%d_%d" % (gi, cond)])
            else:
                b = nextb()
                for oc in range(8):
                    for k in range(8):
                        mm(bank(b)[:, oc * 2:(oc + 1) * 2], wb[:, k, oc * 128:(oc + 1) * 128], sc_b[:, k, :], k == 0, k == 7,
                           ["sc_b", wn], [PB[b]])
                tt("dve", mT[:, piece * 8:(piece + 1) * 8, :], bank(b)[:, 0:16].rearrange("p (a b) -> p a b", b=2),
                   fap(bT, piece * 8, [[48, 128], [1, 8], [0, 2]]), ALU.add, [PB[b], "bT"], ["mT"])
        ts("dve", scp1[:, 0:8, :], mT[:, 8:16, :], 1.0, None, ALU.add, None, ["mT"], ["scp1"])
        ts("dve", scp1[:, 8:16, :], mT[:, 32:40, :], 1.0, None, ALU.add, None, ["mT"], ["scp1"])
        close_scope(scW)
        checkpoint()

        scPA = open_scope()
        uT = sb("uT", [128, 2, NTOK], BF16)
        qnaT = sb("qnaT", [128, 2, NTOK], BF16)
        QgT = sb("QgT", [128, 4, 1024], BF16)
        wglu_t = sb("wglu_t", [128, 2, 256], BF16)
        scP2 = open_scope()
        kqkT = sb("kqkT", [128, 8, NPT], BF16)
        v_na_e = sb("v_na_e", [128, 4, 2, 65], BF16)
        v_na_o = sb("v_na_o", [128, 4, 2, 128], BF16)
        v_g_e = sb("v_g_e", [128, 4, 2, 65], BF16)
        v_g_o = sb("v_g_o", [128, 4, 2, 128], BF16)
        memset("pool", v_na_e[:], 1.0, ["v_na_e"])
        memset("pool", v_g_e[:], 1.0, ["v_g_e"])
        memset("pool", v_na_o[:], 0.0, ["v_na_o"])
        memset("pool", v_g_o[:], 0.0, ["v_g_o"])
        memset("pool", v_na_o[:, :, :, 0:1], 1.0, ["v_na_o"])
        memset("pool", v_g_o[:, :, :, 0:1], 1.0, ["v_g_o"])
        dma("pool", wglu_t[:], wglu[l].rearrange("(k p) n -> p k n", p=128), [], ["wglu_t"])

        scA1 = open_scope()
        win = sb("win", [128, 8, IN_W], BF16)
        tz = sb("tz", [128, 1280])
        sqt = sb("sqt", [128, 640])
        ss10 = sb("ss10", [128, 10])
        g10t = sb("g10t", [128, 640])
        kqk_b = sb("kqk_b", [128, 1024], BF16)
        rc_t = sb("rc_t", [128, 640]); rs_t = sb("rs_t", [128, 640])
        ktmp = sb("ktmp", [128, 8, 128], BF16)
        vtok = sb("vtok", [128, 384], BF16)
        dma("pool", win[:], w_in[l].rearrange("(k p) n -> p k n", p=128), [], ["win"])
        dma("sp", g10t[:], fap(g10.tensor, l * 640, [[0, 128], [1, 640]]), [], ["g10t"])
        for ti in range(NT):
            ln_to_T(ti, 0, 0)
        checkpoint()
        for T in range(3):
            rdT = ACTT[T * 4:(T + 1) * 4]
            for oc in range(4):
                b = nextb()
                for k in range(8):
                    mm(bank(b), win[:, k, oc * 128:(oc + 1) * 128], actT[:, k, T * 512:(T + 1) * 512], k == 0, k == 7,
                       ["win"] + rdT, [PB[b]])
                if oc < 2:
                    cp("dve", uT[:, oc, T * 512:(T + 1) * 512], bank(b), [PB[b]], ["uT"])
                else:
                    cp("act", qnaT[:, oc - 2, T * 512:(T + 1) * 512], bank(b), [PB[b]], ["qnaT"])
        checkpoint()
        for ti in range(NT):
            is_s = ti >= 4
            bl = []
            for (c0, n) in [(512, 512), (1024, 512), (1536, 256)]:
                b = nextb()
                bl.append(b)
                for k in range(8):
                    mm(bank(b)[:, 0:n], actT[:, k, ti * 128:(ti + 1) * 128], win[:, k, c0:c0 + n], k == 0, k == 7,
                       ["win", ACTT[ti]], [PB[b]])
            cp("act", tz[:, 0:512], bank(bl[0]), [PB[bl[0]]], ["tz"])
            cp("dve", tz[:, 512:1024], bank(bl[1]), [PB[bl[1]]], ["tz"])
            cp("act", tz[:, 1024:1280], bank(bl[2])[:, 0:256], [PB[bl[2]]], ["tz"])
            tt("pool", sqt[:], tz[:, 512:1152], tz[:, 512:1152], ALU.mult, ["tz"], ["sqt"])
            treduce(ss10[:], sqt[:].rearrange("p (h d) -> p h d", d=64), ["sqt"], ["ss10"])
            ts("dve", ss10[:], ss10[:], 1.0 / 64.0, RMS_EPS, ALU.mult, ALU.add, ["ss10"], ["ss10"])
            act(ss10[:], ss10[:], AF.Sqrt, ["ss10"], ["ss10"])
            recip(ss10[:], ss10[:], ["ss10"], ["ss10"])
            tt("dve", tz[:, 512:1152].rearrange("p (h d) -> p h d", d=64), tz[:, 512:1152].rearrange("p (h d) -> p h d", d=64),
               fap(ss10, 0, [[10, 128], [1, 10], [0, 64]]), ALU.mult, ["tz", "ss10"], ["tz"])
            tt("pool", tz[:, 512:1152], tz[:, 512:1152], g10t[:], ALU.mult, ["tz", "g10t"], ["tz"])
            if not is_s:
                sub = ti
                bb_, t0 = sub // 2, (sub % 2) * 128
                dma("sp", o_nak[bb_, l, t0:t0 + 128, :], tz[:, 0:256], ["tz"], [oname()])
                dma("sp", o_nav[bb_, l, t0:t0 + 128, :], tz[:, 256:512], ["tz"], [oname()])
                dma("sp", o_gk[bb_, l, t0:t0 + 128, :], tz[:, 1024:1152], ["tz"], [oname()])
                dma("sp", o_gv[bb_, l, t0:t0 + 128, :], tz[:, 1152:1280], ["tz"], [oname()])
            else:
                ts0 = (ti - 4) * 128
                dma("sp", rc_t[:], ropec[ts0:ts0 + 128, :], [], ["rc_t"])
                dma("sp", rs_t[:], ropes[ts0:ts0 + 128, :], [], ["rs_t"])
                xsw = fap(tz, 512 + 16, [[1280, 128], [32, 20], [-16, 2], [1, 16]])
                tt("pool", sqt[:].rearrange("p (a b c) -> p a b c", b=2, c=16), xsw,
                   rs_t[:].rearrange("p (a b c) -> p a b c", b=2, c=16), ALU.mult, ["tz", "rs_t"], ["sqt"])
                tt("pool", tz[:, 512:1152], tz[:, 512:1152], rc_t[:], ALU.mult, ["tz", "rc_t"], ["tz"])
                tt("pool", tz[:, 512:1152], tz[:, 512:1152], sqt[:], ALU.add, ["tz", "sqt"], ["tz"])
            cp("act", kqk_b[:, 0:256], tz[:, 0:256], ["tz"], ["kqk_b"])
            cp("pool", kqk_b[:, 256:768], tz[:, 512:1024], ["tz"], ["kqk_b"])
            cp("pool", kqk_b[:, 768:1024].rearrange("p (kv r d) -> p kv r d", kv=2, r=2),
               fap(tz, 1024, [[1280, 128], [64, 2], [0, 2], [1, 64]]), ["tz"], ["kqk_b"])
            b = nextb()
            for c in range(8):
                tr(bank_bf(b)[:, c * 128:(c + 1) * 128], kqk_b[:, c * 128:(c + 1) * 128], ident_b[:], ["kqk_b", "ident_b"], [PB[b]])
            if not is_s:
                sub = ti
                cp("act", v_na_e[:, sub, :, 0:64], fap(tz, 256, [[1280, 128], [128, 2], [1, 64]]), ["tz"], ["v_na_e"])
                cp("act", v_na_o[:, sub, :, 64:128], fap(tz, 256 + 64, [[1280, 128], [128, 2], [1, 64]]), ["tz"], ["v_na_o"])
                cp("pool", v_g_e[:, sub, :, 0:64], fap(tz, 1152, [[1280, 128], [64, 2], [1, 64]]), ["tz"], ["v_g_e"])
                cp("pool", v_g_o[:, sub, :, 64:128], fap(tz, 1152, [[1280, 128], [64, 2], [1, 64]]), ["tz"], ["v_g_o"])
                cp("dve", kqkT[:, :, sub * 128:(sub + 1) * 128], bank_bf(b).rearrange("p (c t) -> p c t", t=128), [PB[b]], ["kqkT"])
            else:
                ts0 = (ti - 4) * 128
                cp("dve", ktmp[:], bank_bf(b).rearrange("p (c t) -> p c t", t=128), [PB[b]], ["ktmp"])
                cp("pool", QgT[:, :, ts0:ts0 + 128], ktmp[:, 2:6, :], ["ktmp"], ["QgT"])
                dma("sp", fap(kna_in, ts0, [[1024, 128], [128 * 1024, 2], [1, 128]]), ktmp[:, 0:2, :], ["ktmp"], [bname()])
                dma("sp", fap(kg_in, ts0, [[1024, 128], [128 * 1024, 2], [1, 128]]), ktmp[:, 6:8, :], ["ktmp"], [bname()])
                cp("act", vtok[:, 0:256], tz[:, 256:512], ["tz"], ["vtok"])
                cp("act", vtok[:, 256:384], tz[:, 1152:1280], ["tz"], ["vtok"])
                dma("sp", fap(v_in, ts0 * 384, [[384, 128], [1, 384]]), vtok[:], ["vtok"], [bname()])
        close_scope(scA1)
        checkpoint()

        for (a_, b__) in [(kna_in, kna_out), (kg_in, kg_out), (v_in, v_out)]:
            S.coll((lambda a_=a_, b__=b__: lambda e: e.collective_compute(
                "AllGather", ALU.bypass, replica_groups=[[0, 1, 2, 3], [4, 5, 6, 7]], ins=[a_.ap()], outs=[b__.ap()]))(),
                list(binn), ["bout_%s" % a_.name])
        mixT = actT
        MIXN = ["mix"]

        dma("sp", lre[:], lamre[l], [], ["lre"])
        dma("sp", lim[:], lamim[l], [], ["lim"])
        dma("sp", ldt[:], fap(logdt.tensor, l * 32, [[0, 128], [1, 32]]), [], ["ldt"])
        dma("sp", ssmd_t[:], ssmd[l], [], ["ssmd_t"])
        ts("dve", lre[:], lre[:], -1e-4, None, ALU.min, None, ["lre"], ["lre"])
        act(ldt[:], ldt[:], AF.Exp, ["ldt"], ["ldt"])
        tt("dve", s_a[:], lre[:], ldt[:], ALU.mult, ["lre", "ldt"], ["s_a"])
        act(s_rho[:], s_a[:], AF.Exp, ["s_a"], ["s_rho"])
        tt("dve", s_turn[:], lim[:], ldt[:], ALU.mult, ["lim", "ldt"], ["s_turn"])
        ts("dve", s_turn[:], s_turn[:], 1.0 / TWO_PI, None, ALU.mult, None, ["s_turn"], ["s_turn"])
        sincos(s_turn[:], s_s1[:], s_c1[:], s_q1[:], s_q2[:], ["s_turn"], "sc1")
        ts("dve", s_t3[:], s_turn[:], float(SEQ), None, ALU.mult, None, ["s_turn"], ["s_t3"])
        sincos(s_t3[:], s_sT[:], s_cT[:], s_q1[:], s_q2[:], ["s_t3"], "scT")
        ts("dve", s_t3[:], s_turn[:], 1024.0, None, ALU.mult, None, ["s_turn", "s_t3"], ["s_t3"])
        sincos(s_t3[:], s_sK[:], s_cK[:], s_q1[:], s_q2[:], ["s_t3"], "scK")
        act(s_t1[:], s_a[:], AF.Exp, ["s_a"], ["s_t1"], scale=1024.0)
        tt("dve", Lrr[:], s_t1[:], s_cK[:], ALU.mult, ["s_t1", "scK_c"], ["Lrr"])
        tt("dve", Lii[:], s_t1[:], s_sK[:], ALU.mult, ["s_t1", "scK_s"], ["Lii"])
        ts("dve", Lii[:], Lii[:], sgn[:, 0:1], None, ALU.mult, None, ["Lii", "sgn"], ["Lii"])
        tt("dve", s_t1[:], s_rho[:], s_c1[:], ALU.mult, ["s_rho", "sc1_c", "Lrr", "Lii"], ["s_t1"])
        ts("dve", s_t1[:], s_t1[:], -1.0, None, ALU.add, None, ["s_t1"], ["s_t1"])
        tt("dve", s_t2[:], s_rho[:], s_s1[:], ALU.mult, ["s_rho", "sc1_s"], ["s_t2"])
        tt("dve", s_t3[:], lre[:], lre[:], ALU.mult, ["lre", "scK_s", "scK_c", "sc_tmp_r"], ["s_t3"])
        tt("dve", kre[:], lim[:], lim[:], ALU.mult, ["lim"], ["kre"])
        tt("dve", s_t3[:], s_t3[:], kre[:], ALU.add, ["s_t3", "kre"], ["s_t3"])
        recip(s_t3[:], s_t3[:], ["s_t3"], ["s_t3"])
        tt("dve", kre[:], s_t1[:], lre[:], ALU.mult, ["s_t1", "lre"], ["kre"])
        tt("dve", kim[:], s_t2[:], lim[:], ALU.mult, ["s_t2", "lim"], ["kim"])
        tt("dve", kre[:], kre[:], kim[:], ALU.add, ["kre", "kim"], ["kre"])
        tt("dve", kre[:], kre[:], s_t3[:], ALU.mult, ["kre", "s_t3"], ["kre"])
        tt("dve", kim[:], s_t2[:], lre[:], ALU.mult, ["s_t2", "lre"], ["kim"])
        tt("dve", s_t2[:], s_t1[:], lim[:], ALU.mult, ["s_t1", "lim", "kim"], ["s_t2"])
        tt("dve", kim[:], kim[:], s_t2[:], ALU.subtract, ["kim", "s_t2"], ["kim"])
        tt("dve", kim[:], kim[:], s_t3[:], ALU.mult, ["kim", "s_t3"], ["kim"])

        def ssm_pass(mode, co=None):
            scS = open_scope()
            bb_re = sb("bb_re", [64, 32, 16]); bb_im = sb("bb_im", [64, 32, 16])
            scB = open_scope()
            b_re_t = sb("b_re_t", [64, 32, 16]); b_im_t = sb("b_im_t", [64, 32, 16]); bb_t = sb("bb_t", [64, 32, 16])
            dma("sp", b_re_t[:].rearrange("p a c -> p (a c)"), bre[l], [], ["b_re_t"])
            dma("sp", b_im_t[:].rearrange("p a c -> p (a c)"), bim[l], [], ["b_im_t"])
            kre_b = fap(kre, 0, [[32, 64], [1, 32], [0, 16]])
            kim_b = fap(kim, 0, [[32, 64], [1, 32], [0, 16]])
            tt("pool", bb_re[:], b_re_t[:], kre_b, ALU.mult, ["b_re_t", "kre"], ["bb_re"])
            tt("pool", bb_t[:], b_im_t[:], kim_b, ALU.mult, ["b_im_t", "kim"], ["bb_t"])
            tt("pool", bb_re[:], bb_re[:], bb_t[:], ALU.subtract, ["bb_re", "bb_t"], ["bb_re"])
            tt("pool", bb_im[:], b_im_t[:], kre_b, ALU.mult, ["b_im_t", "kre"], ["bb_im"])
            tt("pool", bb_t[:], b_re_t[:], kim_b, ALU.mult, ["b_re_t", "kim", "bb_re"], ["bb_t"])
            tt("pool", bb_im[:], bb_im[:], bb_t[:], ALU.add, ["bb_im", "bb_t"], ["bb_im"])
            close_scope(scB)
            c_re_t = sb("c_re_t", [128, 4, 64]); c_im_t = sb("c_im_t", [128, 4, 64])
            dma("sp", c_re_t[:], cre[l], [], ["c_re_t"])
            dma("sp", c_im_t[:], cim[l], [], ["c_im_t"])
            csrc1 = sb("csrc1", [128, 128]); csrc2 = sb("csrc2", [128, 128])
            bpad = sb("bpad", [128, 8, 128], BF16); bsw = sb("bsw", [128, 8, 128], BF16)
            c1pad = sb("c1pad", [128, 8, 128], BF16); c2pad = sb("c2pad", [128, 8, 128], BF16)
            ntab = 2 if mode == 1 else 1
            tcoss = [sb("tcos%d" % i, [128, 1024]) for i in range(ntab)]; tsins = [sb("tsin%d" % i, [128, 1024]) for i in range(ntab)]
            r1s = [sb("r1_%d" % i, [128, 512]) for i in range(2)]; r2s = [sb("r2_%d" % i, [128, 512]) for i in range(2)]
            btls = [sb("btl_%d" % i, [128, 512]) for i in range(2)]; gscs = [sb("gsc_%d" % i, [128, 512]) for i in range(2)]
            G1s = [sb("G1_%d" % i, [128, 512], BF16) for i in range(2)]; G2s = [sb("G2_%d" % i, [128, 512], BF16) for i in range(2)]
            r1, r2, btl = r1s[0], r2s[0], btls[0]
            ucnt = {"n": 0}
            pend = {"f": None}
            gl_b = sb("gl_b", [128, 2, 1024], BF16)
            nlen = 1024 if True else 256

            for gc in range(2):
                if mode == 1:
                    accs = {"p": 7}
                else:
                    accs = {"s1": 6, "s2": 7}
                for a in accs.values():
                    reserved.add(a)
                first = {k: True for k in accs}
                for d in range(2):
                    dg0 = d * 16 + gc * 8
                    b = nextb()
                    tr(bank(b)[:, 0:64], bb_re[:, dg0:dg0 + 8, :].rearrange("p a c -> p (a c)"), ident_f[0:64, 0:64], ["bb_re", "ident_f"], [PB[b]])
                    tr(bank(b)[:, 64:128], bb_im[:, dg0:dg0 + 8, :].rearrange("p a c -> p (a c)"), ident_f[0:64, 0:64], ["bb_im", "ident_f"], [PB[b]])
                    full = fap(ps_all, b * 512, [[4096, 128], [0, 8], [1, 128]])
                    tt("dve", bpad[:], full, fap(rowmask, 0, [[8, 128], [1, 8], [0, 128]]), ALU.mult, [PB[b], "rowmask"], ["bpad"])
                    full_im = fap(ps_all, b * 512 + 64, [[4096, 128], [0, 8], [1, 64]])
                    full_re = fap(ps_all, b * 512, [[4096, 128], [0, 8], [1, 64]])
                    tt("dve", bsw[:, :, 0:64], full_im, fap(rowmask, 0, [[8, 128], [1, 8], [0, 64]]), ALU.mult, [PB[b], "rowmask"], ["bsw"])
                    tt("dve", bsw[:, :, 64:128], full_re, fap(nrowmask, 0, [[8, 128], [1, 8], [0, 64]]), ALU.mult, [PB[b], "nrowmask"], ["bsw"])
                    ci = d * 2 + gc
                    cp("pool", csrc1[:, 0:64], c_re_t[:, ci, :], ["c_re_t"], ["csrc1"])
                    ts("pool", csrc1[:, 64:128], c_im_t[:, ci, :], -1.0, None, ALU.mult, None, ["c_im_t"], ["csrc1"])
                    ts("pool", csrc2[:, 0:64], c_im_t[:, ci, :], -1.0, None, ALU.mult, None, ["c_im_t"], ["csrc2"])
                    ts("pool", csrc2[:, 64:128], c_re_t[:, ci, :], -1.0, None, ALU.mult, None, ["c_re_t"], ["csrc2"])
                    b1 = nextb()
                    tr(bank(b1)[:, 0:128], csrc1[:], ident_f[:], ["csrc1", "ident_f"], [PB[b1]])
                    tr(bank(b1)[:, 128:256], csrc2[:], ident_f[:], ["csrc2", "ident_f"], [PB[b1]])
                    tt("dve", c1pad[:], fap(ps_all, b1 * 512, [[4096, 128], [0, 8], [1, 128]]), colmask[:], ALU.mult, [PB[b1], "colmask"], ["c1pad"])
                    tt("dve", c2pad[:], fap(ps_all, b1 * 512 + 128, [[4096, 128], [0, 8], [1, 128]]), colmask[:], ALU.mult, [PB[b1], "colmask"], ["c2pad"])
                    for g8 in range(8):
                        dg = dg0 + g8
                        tb = dg % ntab
                        if ntab == 1 and pend["f"] is not None:
                            pend["f"]()
                            pend["f"] = None
                        tcos, tsin = tcoss[tb], tsins[tb]
                        nTC, nTS = "tcos%d" % tb, "tsin%d" % tb
                        ts("dve", tsin[:, 0:nlen], sp1[:, 0:nlen], s_turn[:, dg:dg + 1], None, ALU.mult, None, ["sp1", "s_turn"], [nTS])
                        ts("dve", tcos[:, 0:nlen], tsin[:, 0:nlen], MAGIC, MAGIC, ALU.add, ALU.subtract, [nTS], [nTC])
                        tt("pool", tsin[:, 0:nlen], tsin[:, 0:nlen], tcos[:, 0:nlen], ALU.subtract, [nTS, nTC], [nTS])
                        stt("dve", tcos[:, 0:nlen], tsin[:, 0:nlen], -1.0, tsin[:, 0:nlen], ALU.mult, ALU.max, [nTS], [nTC])
                        act(tsin[:, 0:nlen], tsin[:, 0:nlen], AF.Sin, [nTS], [nTS], scale=TWO_PI)
                        act(tcos[:, 0:nlen], tcos[:, 0:nlen], AF.Sin, [nTC, "halfpi"], [nTC], scale=-TWO_PI, bias=halfpi[:])
                        if mode == 1:
                            units = [("s", 512, [(0, 512, 0)], None), ("s", 1024, [(0, 512, 512)], None),
                                     ("p", 0, [(0, 256, 0), (256, 256, 0)], "p")]
                        else:
                            units = [("s", 512, [(0, 512, 0)], "s1"), ("s", 1024, [(0, 512, 512)], "s2")]
                        if d == 1:
                            if mode == 1:
                                units = [("s", 1024, [(0, 512, 0)], None), ("s", 512, [(0, 512, 512)], None),
                                         ("p", 0, [(0, 256, 0), (256, 256, 0)], "p")]
                            else:
                                units = [("s", 1024, [(0, 512, 0)], "s2"), ("s", 512, [(0, 512, 512)], "s1")]
                        prev_last = None
                        for (kind, c0, segs, acck) in units:
                            ub = ucnt["n"] % 2
                            ucnt["n"] += 1
                            r1, r2, btl, gsc, G1, G2 = r1s[ub], r2s[ub], btls[ub], gscs[ub], G1s[ub], G2s[ub]
                            nR1, nR2, nBT, nGS, nG1, nG2 = "r1_%d" % ub, "r2_%d" % ub, "btl_%d" % ub, "gsc_%d" % ub, "G1_%d" % ub, "G2_%d" % ub
                            bu = nextb()
                            bw = nextb()
                            mm(bank(bu), bpad[:, g8, :], uT[:, gc, c0:c0 + 512], True, True, ["bpad", "uT"], [PB[bu]])
                            mm(bank(bw), bsw[:, g8, :], uT[:, gc, c0:c0 + 512], True, True, ["bsw", "uT"], [PB[bw]])
                            for (off, ln_, toff) in segs:
                                if d == 0:
                                    bu_v = fap(ps_all, bu * 512 + off, [[4096, 128], [1, ln_]])
                                    bw_v = fap(ps_all, bw * 512 + off, [[4096, 128], [1, ln_]])
                                else:
                                    bu_v = fap(ps_all, bu * 512 + off + ln_ - 1, [[4096, 128], [-1, ln_]])
                                    bw_v = fap(ps_all, bw * 512 + off + ln_ - 1, [[4096, 128], [-1, ln_]])
                                tt("dve", r1[:, off:off + ln_], bu_v, tcos[:, toff:toff + ln_], ALU.mult, [PB[bu], nTC], [nR1])
                                tt("dve", r2[:, off:off + ln_], bw_v, tsin[:, toff:toff + ln_], ALU.mult, [PB[bw], nTS], [nR2])
                            tt("dve" if mode == 2 else "pool", btl[:], r1[:], r2[:], ALU.add, [nR1, nR2], [nBT])

                            def stageB(kind=kind, segs=segs, acck=acck, btl=btl, gsc=gsc, G1=G1, G2=G2, nBT=nBT, nGS=nGS, nG1=nG1, nG2=nG2,
                                       dg=dg, d=d, g8=g8, tcos=tcos, tsin=tsin, nTC=nTC, nTS=nTS):
                                for si, (off, ln_, toff) in enumerate(segs):
                                    if kind == "p":
                                        init = 0.0
                                        rdi = []
                                    elif toff == 0:
                                        if mode == 1:
                                            init = 0.0
                                            rdi = []
                                        else:
                                            init = hsel[:, dg:dg + 1]
                                            rdi = ["hsel"]
                                    else:
                                        init = GLs[:, dg:dg + 1]
                                        rdi = ["GLs"]
                                    scan(gsc[:, off:off + ln_], fap(s_rho, dg, [[32, 128], [0, ln_]]), btl[:, off:off + ln_], init,
                                         [nBT, "s_rho"] + rdi, [nGS])
                                if kind == "p":
                                    cp("act", GL[:, dg * 2:dg * 2 + 2], fap(gsc, 255, [[512, 128], [256, 2]]), [nGS], ["GL"])
                                else:
                                    cp("act", GLs[:, dg:dg + 1], gsc[:, 511:512], [nGS], ["GLs"])
                                if acck is not None:
                                    for (off, ln_, toff) in segs:
                                        if d == 0:
                                            g1o = G1[:, off:off + ln_]
                                            g2o = G2[:, off:off + ln_]
                                        else:
                                            g1o = fap(G1, off + ln_ - 1, [[512, 128], [-1, ln_]])
                                            g2o = fap(G2, off + ln_ - 1, [[512, 128], [-1, ln_]])
                                        tt("dve", g1o, gsc[:, off:off + ln_], tcos[:, toff:toff + ln_], ALU.mult, [nGS, nTC], [nG1])
                                        tt("pool", g2o, gsc[:, off:off + ln_], tsin[:, toff:toff + ln_], ALU.mult, [nGS, nTS], [nG2])
                                    acc = accs[acck]
                                    lastmm = (d == 1 and g8 == 7)
                                    mm(bank(acc), c1pad[:, g8, :], G1[:], first[acck], False, ["c1pad", nG1], [PB[acc]])
                                    first[acck] = False
                                    mm(bank(acc), c2pad[:, g8, :], G2[:], False, lastmm, ["c2pad", nG2], [PB[acc]])
                            if pend["f"] is not None:
                                pend["f"]()
                            pend["f"] = stageB
                            if co is not None:
                                for _ in range(9):
                                    next(co, None)
                    if pend["f"] is not None:
                        pend["f"]()
                        pend["f"] = None
                for acck, acc in accs.items():
                    if acck == "p":
                        c0, gcol = 0, 0
                    elif acck == "s1":
                        c0, gcol = 512, 0
                    else:
                        c0, gcol = 1024, 512
                    yt, y2, y3 = r1s[0], r2s[0], btls[0]
                    stt("dve", yt[:], uT[:, gc, c0:c0 + 512], ssmd_t[:, gc:gc + 1], bank(acc), ALU.mult, ALU.add, [PB[acc], "uT", "ssmd_t"], ["r1_0"])
                    tt("pool", y2[:], yt[:], yt[:], ALU.mult, ["r1_0"], ["r2_0"])
                    ts("pool", y2[:], y2[:], 0.044715, 1.0, ALU.mult, ALU.add, ["r2_0"], ["r2_0"])
                    tt("pool", y2[:], y2[:], yt[:], ALU.mult, ["r2_0", "r1_0"], ["r2_0"])
                    act(y3[:], y2[:], AF.Tanh, ["r2_0"], ["btl_0"], scale=math.sqrt(2.0 / math.pi))
                    ts("pool", y3[:], y3[:], 1.0, 0.5, ALU.add, ALU.mult, ["btl_0"], ["btl_0"])
                    tt("pool", gl_b[:, gc, gcol:gcol + 512], y3[:], yt[:], ALU.mult, ["btl_0", "r1_0"], ["gl_b"])
                for a in accs.values():
                    reserved.discard(a)
            if co is not None:
                for _ in co:
                    pass
            cols = [(0, 0)] if mode == 1 else [(512, 0), (1024, 512)]
            for (c0, gcol) in cols:
                for oc in range(2):
                    b = nextb()
                    for k in range(2):
                        mm(bank(b), wglu_t[:, k, oc * 128:(oc + 1) * 128], gl_b[:, k, gcol:gcol + 512], k == 0, k == 1, ["wglu_t", "gl_b"], [PB[b]])
                    act(r1s[0][:], bank(b), AF.Sigmoid, [PB[b]], ["r1_0"])
                    tt("dve", mixT[:, oc, c0:c0 + 512], r1s[0][:], gl_b[:, oc, gcol:gcol + 512], ALU.mult, ["r1_0", "gl_b"], MIXN)
            close_scope(scS)

        ssm_pass(1)
        b = nextb()
        mm(bank(b)[:, 0:64], swapm[:], GL[:], True, True, ["swapm", "GL"], [PB[b]])
        cp("act", SWt[:], bank(b)[:, 0:64], [PB[b]], ["SWt"])
        ts("dve", s_t1[:], s_sT[:], sgn[:, 0:1], None, ALU.mult, None, ["scT_s", "sgn"], ["s_t1"])
        tt("dve", Sfin[:].rearrange("p (a b) -> p a b", b=2), GL[:].rearrange("p (a b) -> p a b", b=2),
           fap(s_cT, 0, [[32, 128], [1, 32], [0, 2]]), ALU.mult, ["GL", "scT_c"], ["Sfin"])
        tt("dve", SWt[:].rearrange("p (a b) -> p a b", b=2), SWt[:].rearrange("p (a b) -> p a b", b=2),
           fap(s_t1, 0, [[32, 128], [1, 32], [0, 2]]), ALU.mult, ["SWt", "s_t1"], ["SWt"])
        tt("dve", Sfin[:], Sfin[:], SWt[:], ALU.add, ["Sfin", "SWt"], ["Sfin"])
        b = nextb()
        tr(bank(b)[0:64, 0:128], Sfin[:], ident_f[:], ["Sfin", "ident_f"], [PB[b]])
        cp("act", SfT[:], bank(b)[0:64, 0:128], [PB[b]], ["SfT"])
        for bq in range(2):
            dma("sp", o_sre[bq, l], fap(SfT, bq * 128, [[256, 32], [1, 64]]), ["SfT"], [oname()])
            dma("sp", o_sim[bq, l], fap(SfT, bq * 128 + 64, [[256, 32], [1, 64]]), ["SfT"], [oname()])
        b = nextb()
        mm(bank(b)[:, 0:32], swapm[:], GLs[:], True, True, ["swapm", "GLs"], [PB[b]])
        cp("act", SWs[:], bank(b)[:, 0:32], [PB[b]], ["SWs"])
        ts("dve", s_t2[:], s_sK[:], sgn[:, 0:1], None, ALU.mult, None, ["scK_s", "sgn"], ["s_t2"])
        tt("dve", Sloc[:], GLs[:], s_cK[:], ALU.mult, ["GLs", "scK_c"], ["Sloc"])
        tt("dve", SWs[:], SWs[:], s_t2[:], ALU.mult, ["SWs", "s_t2"], ["SWs"])
        tt("dve", Sloc[:], Sloc[:], SWs[:], ALU.add, ["Sloc", "SWs"], ["Sloc"])
        dma("sp", sgin_t[l].ap(), Sloc[:], ["Sloc"], ["sgin"])
        checkpoint()

        S.coll(lambda e, a=sgin_t[l], b_=sgout_t[l]: e.collective_compute(
            "AllGather", ALU.bypass, replica_groups=[[0, 1, 2, 3], [4, 5, 6, 7]], ins=[a.ap()], outs=[b_.ap()]),
            ["sgin"], ["sgout"])
        checkpoint()

        for bq in range(2):
            tok0 = bq * 256
            for h in range(4):
                base = (h % 2) * 64
                b = nextb()
                for j in range(2):
                    mm(bank(b)[:, j * 256:(j + 1) * 256], kqkT[base:base + 64, h // 2, tok0 + j * 128: tok0 + (j + 1) * 128],
                       qnaT[base:base + 64, h // 2, tok0:tok0 + 256], True, True, ["kqkT", "qnaT"], [PB[b]])
                pt = pT[h % 2]
                act(pt[:], bank(b), AF.Exp, [PB[b]], ["pT%d" % (h % 2)], scale=0.125)
                po = nextb()
                for j in range(2):
                    if h % 2 == 0:
                        mm(bank(po)[0:65, 0:256], v_na_e[:, bq * 2 + j, h // 2, :], pt[:, j * 256:(j + 1) * 256], j == 0, j == 1,
                           ["v_na_e", "pT%d" % (h % 2)], [PB[po]])
                    else:
                        mm(bank(po)[:, 0:256], v_na_o[:, bq * 2 + j, h // 2, :], pt[:, j * 256:(j + 1) * 256], j == 0, j == 1,
                           ["v_na_o", "pT%d" % (h % 2)], [PB[po]])
                attn_norm(po, base, 256, mixT[base:base + 64, 2 + h // 2, tok0:tok0 + 256], 64 if h % 2 == 0 else 0, [], MIXN)
            for hq in range(8):
                base = (hq % 2) * 64
                kv = hq // 4
                b = nextb()
                for j in range(2):
                    mm(bank(b)[:, j * 256:(j + 1) * 256], kqkT[base:base + 64, 6 + kv, tok0 + j * 128: tok0 + (j + 1) * 128],
                       kqkT[base:base + 64, 2 + hq // 2, tok0:tok0 + 256], True, True, ["kqkT"], [PB[b]])
                pt = pT[hq % 2]
                act(pt[:], bank(b), AF.Exp, [PB[b]], ["pT%d" % (hq % 2)], scale=0.125)
                po = nextb()
                for j in range(2):
                    if hq % 2 == 0:
                        mm(bank(po)[0:65, 0:256], v_g_e[:, bq * 2 + j, kv, :], pt[:, j * 256:(j + 1) * 256], j == 0, j == 1,
                           ["v_g_e", "pT%d" % (hq % 2)], [PB[po]])
                    else:
                        mm(bank(po)[:, 0:256], v_g_o[:, bq * 2 + j, kv, :], pt[:, j * 256:(j + 1) * 256], j == 0, j == 1,
                           ["v_g_o", "pT%d" % (hq % 2)], [PB[po]])
                attn_norm(po, base, 256, mixT[base:base + 64, 4 + hq // 2, tok0:tok0 + 256], 64 if hq % 2 == 0 else 0, [], MIXN)
        checkpoint()


        close_scope(scP2)
        dma("sp", Sg[:], sgout_t[l].ap().rearrange("(r p) c -> p r c", p=128), ["sgout"], ["Sg"])
        for j in range(4):
            cp("dve", SJ[:, j, 0:16], Sg[:, j, 0:16], ["Sg"], ["SJ"])
            cp("dve", SJ[:, j, 16:32], Sg[:, 3 - j, 16:32], ["Sg"], ["SJ"])
        dma("sp", hin[:, 0, :], h0_d[l], [], ["hin"])
        for j in range(3):
            b = nextb()
            mm(bank(b)[:, 0:32], swapm[:], hin[:, j, :], True, True, ["swapm", "hin"], [PB[b]])
            tt("dve", SWs[:], bank(b)[:, 0:32], Lii[:], ALU.mult, [PB[b], "Lii"], ["SWs"])
            tt("dve", hin[:, j + 1, :], hin[:, j, :], Lrr[:], ALU.mult, ["hin", "Lrr"], ["hin"])
            tt("dve", hin[:, j + 1, :], hin[:, j + 1, :], SWs[:], ALU.add, ["hin", "SWs"], ["hin"])
            tt("dve", hin[:, j + 1, :], hin[:, j + 1, :], SJ[:, j, :], ALU.add, ["hin", "SJ"], ["hin"])
        for (c0, oo) in [(0, 0), (16, 4)]:
            ts("dve", hsel[:, c0:c0 + 16], hin[:, 0, c0:c0 + 16], oh[:, oo:oo + 1], None, ALU.mult, None, ["hin", "oh"], ["hsel"])
            for j in range(1, 4):
                stt("dve", hsel[:, c0:c0 + 16], hin[:, j, c0:c0 + 16], oh[:, oo + j:oo + j + 1], hsel[:, c0:c0 + 16], ALU.mult, ALU.add,
                    ["hin", "oh", "hsel"], ["hsel"])
        def gqa_gen():
            for kv in range(2):
                scG = open_scope()
                KgT = sb("KgT", [128, 4352], BF16)
                vge = sb("vge", [128, 34, 65], BF16)
                vgo = sb("vgo", [128, 34, 128], BF16)
                ckf = sb("ckf", [128, 2, 128])
                ckb = sb("ckb", [128, 2, 128], BF16)
                memset("pool", vge[:], 1.0, ["vge"])
                memset("pool", vgo[:], 0.0, ["vgo"])
                memset("pool", vgo[:, :, 0:1], 1.0, ["vgo"])
                for r in range(4):
                    dma("sp", KgT[:, r * 1024:(r + 1) * 1024], kg_out.ap()[r * 256 + kv * 128: r * 256 + (kv + 1) * 128, :], ["bout_%s" % kg_in.name], ["KgT"])
                    vsrc = fap(v_out, r * 1024 * 384 + 256 + kv * 64, [[384, 128], [128 * 384, 8], [1, 64]])
                    dma("sp", vge[:, r * 8:(r + 1) * 8, 0:64], vsrc, ["bout_%s" % v_in.name, "vge"], ["vge"])
                    dma("sp", vgo[:, r * 8:(r + 1) * 8, 64:128], vsrc, ["bout_%s" % v_in.name, "vgo"], ["vgo"])
                for j in range(2):
                    for rep in range(2):
                        dma("sp", ckf[:, j, rep * 64:(rep + 1) * 64], fap(cgk.tensor, l * 256 * 128 + j * 128 * 128 + kv * 64, [[128, 128], [1, 64]]), [], ["ckf"])
                cp("dve", ckb[:], ckf[:], ["ckf"], ["ckb"])
                b = nextb()
                for j in range(2):
                    tr(bank_bf(b)[:, j * 128:(j + 1) * 128], ckb[:, j, :], ident_b[:], ["ckb", "ident_b"], [PB[b]])
                cp("dve", KgT[:, 4096:4352], bank_bf(b)[:, 0:256], [PB[b]], ["KgT"])
                csrc = fap(cgv.tensor, l * 256 * 128 + kv * 64, [[128, 128], [128 * 128, 2], [1, 64]])
                dma("pool", vge[:, 32:34, 0:64], csrc, ["vge"], ["vge"])
                dma("pool", vgo[:, 32:34, 64:128], csrc, ["vgo"], ["vgo"])
                for T in range(2):
                    for hh in range(4):
                        hq = kv * 4 + hh
                        base = (hq % 2) * 64
                        po = nextb()
                        reserved.add(po)
                        def s_mm(c):
                            b = nextb()
                            mm(bank(b), KgT[base:base + 64, c * 128:(c + 1) * 128], QgT[base:base + 64, hq // 2, T * 512:(T + 1) * 512],
                               True, True, ["KgT", "QgT"], [PB[b]])
                            reserved.add(b)
                            return b
                        bcur = s_mm(0)
                        for c in range(34):
                            bnext = s_mm(c + 1) if c < 33 else None
                            pt = pT[c % 3]
                            ptn = "pT%d" % (c % 3)
                            act(pt[:], bank(bcur), AF.Exp, [PB[bcur]], [ptn], scale=0.125)
                            reserved.discard(bcur)
                            if hq % 2 == 0:
                                mm(bank(po)[0:65, :], vge[:, c, :], pt[:], c == 0, c == 33, ["vge", ptn], [PB[po]])
                            else:
                                mm(bank(po)[:, :], vgo[:, c, :], pt[:], c == 0, c == 33, ["vgo", ptn], [PB[po]])
                            bcur = bnext
                            yield
                        attn_norm(po, base, 512, mixT[base:base + 64, 4 + hq // 2, 512 + T * 512: 512 + (T + 1) * 512],
                                  64 if hq % 2 == 0 else 0, [], MIXN)
                        reserved.discard(po)
                close_scope(scG)


        ssm_pass(2, gqa_gen())
        checkpoint()

        scN = open_scope()
        KnaT = sb("KnaT", [128, 2, 2048], BF16)
        kcand = sb("kcand", [128, 2, 512], BF16)
        vsel = sb("vsel", [128, 16, 256], BF16)
        vcand = sb("vcand", [128, 4, 256], BF16)
        vle = sb("vle", [128, 16, 2, 65], BF16)
        vlo = sb("vlo", [128, 16, 2, 128], BF16)
        kctxf = sb("kctxf", [128, 2, 256])
        kctxb = sb("kctxb", [128, 2, 256], BF16)
        KcT = sb("KcT", [128, 2, 256], BF16)
        vcst = sb("vcst", [128, 2, 256], BF16)
        vce = sb("vce", [128, 2, 2, 65], BF16)
        vco = sb("vco", [128, 2, 2, 128], BF16)
        brev = sb("brev", [64, 17, 64])
        BTt = [sb("BT%d" % i, [128, 8, 64]) for i in range(2)]
        namask = sb("namask", [128, 16, 8])
        nacol = sb("nacol", [128, 64])
        jmat = sb("jmat", [64, 64])
        sls = [sb("sl%d" % i, [128, 8, 64]) for i in range(2)]
        plocs = [sb("ploc%d" % i, [128, 512], BF16) for i in range(2)]
        pctxs = [sb("pctx%d" % i, [128, 128], BF16) for i in range(2)]
        dma("sp", namask[:], namask_d, [], ["namask"])
        dma("sp", nacol[:], nacol_d, [], ["nacol"])
        dma("sp", jmat[:], jmat_d, [], ["jmat"])
        dma("sp", KnaT[:, :, 512:1536], fap(kna_in, 0, [[1024, 128], [128 * 1024, 2], [1, 1024]]), list(binn), ["KnaT"])
        dma("sp", vsel[:, 4:12, :], fap(v_in, 0, [[384, 128], [128 * 384, 8], [1, 256]]), list(binn), ["vsel"])
        for (side, ohoff, kdst, vdst, tok_off) in [("prev", 8, 0, 0, 512), ("next", 12, 1536, 12, 0)]:
            for r in range(4):
                dma("sp", kcand[:], fap(kna_out, r * 256 * 1024 + tok_off, [[1024, 128], [128 * 1024, 2], [1, 512]]), ["bout_%s" % kna_in.name], ["kcand"])
                if r == 0:
                    ts("dve", KnaT[:, :, kdst:kdst + 512], kcand[:], oh[:, ohoff:ohoff + 1], None, ALU.mult, None, ["kcand", "oh"], ["KnaT"])
                else:
                    stt("dve", KnaT[:, :, kdst:kdst + 512], kcand[:], oh[:, ohoff + r:ohoff + r + 1], KnaT[:, :, kdst:kdst + 512], ALU.mult, ALU.add,
                        ["kcand", "oh", "KnaT"], ["KnaT"])
                dma("sp", vcand[:], fap(v_out, (r * 1024 + tok_off) * 384, [[384, 128], [128 * 384, 4], [1, 256]]), ["bout_%s" % v_in.name], ["vcand"])
                if r == 0:
                    ts("dve", vsel[:, vdst:vdst + 4, :], vcand[:], oh[:, ohoff:ohoff + 1], None, ALU.mult, None, ["vcand", "oh"], ["vsel"])
                else:
                    stt("dve", vsel[:, vdst:vdst + 4, :], vcand[:], oh[:, ohoff + r:ohoff + r + 1], vsel[:, vdst:vdst + 4, :], ALU.mult, ALU.add,
                        ["vcand", "oh", "vsel"], ["vsel"])
        memset("pool", vle[:], 1.0, ["vle"])
        memset("pool", vlo[:], 0.0, ["vlo"])
        memset("pool", vlo[:, :, :, 0:1], 1.0, ["vlo"])
        cp("pool", vle[:, :, :, 0:64], fap(vsel, 0, [[4096, 128], [256, 16], [128, 2], [1, 64]]), ["vsel", "vle"], ["vle"])
        cp("pool", vlo[:, :, :, 64:128], fap(vsel, 64, [[4096, 128], [256, 16], [128, 2], [1, 64]]), ["vsel", "vlo"], ["vlo"])
        dma("sp", kctxf[:], cnak[l].rearrange("(j p) f -> p j f", p=128), [], ["kctxf"])
        cp("dve", kctxb[:], kctxf[:], ["kctxf"], ["kctxb"])
        b = nextb()
        for j in range(2):
            for c in range(2):
                tr(bank_bf(b)[:, (c * 2 + j) * 128:(c * 2 + j + 1) * 128], kctxb[:, j, c * 128:(c + 1) * 128], ident_b[:], ["kctxb", "ident_b"], [PB[b]])
        cp("dve", KcT[:].rearrange("p c t -> p (c t)"), bank_bf(b)[:, 0:512], [PB[b]], ["KcT"])
        dma("pool", vcst[:], cnav[l].rearrange("(j p) f -> p j f", p=128), [], ["vcst"])
        memset("pool", vce[:], 1.0, ["vce"])
        memset("pool", vco[:], 0.0, ["vco"])
        memset("pool", vco[:, :, :, 0:1], 1.0, ["vco"])
        cp("pool", vce[:, :, :, 0:64], fap(vcst, 0, [[512, 128], [256, 2], [128, 2], [1, 64]]), ["vcst", "vce"], ["vce"])
        cp("pool", vco[:, :, :, 64:128], fap(vcst, 64, [[512, 128], [256, 2], [128, 2], [1, 64]]), ["vcst", "vco"], ["vco"])
        for h in range(4):
            base = (h % 2) * 64
            dma("sp", brev[:], fap(nab.tensor, l * NABLEN + 128 + (h * 15 - 2) * 31 - 48, [[1, 64], [31, 17], [1, 64]]), [], ["brev"])
            for par in range(2):
                b = nextb()
                for i in range(8):
                    jj = 2 * i + 1 - par
                    mm(bank(b)[:, i * 64:(i + 1) * 64], brev[:, jj:jj + 2, :].rearrange("p a k -> p (a k)"), jmat[:], True, True,
                       ["brev", "jmat"], [PB[b]])
                tt("dve", BTt[par][:], bank(b).rearrange("p (i q) -> p i q", q=64), fap(nacol, 0, [[64, 128], [0, 8], [1, 64]]), ALU.add,
                   [PB[b], "nacol"], ["BT%d" % par])
            def na_S(r):
                wbase = r - (r % 2)
                q_ap = qnaT[base:base + 64, h // 2, 512 + r * 64: 512 + (r + 1) * 64]
                bS = nextb()
                reserved.add(bS)
                for i in range(8):
                    mm(bank(bS)[:, i * 64:(i + 1) * 64], KnaT[base:base + 64, h // 2, (wbase + 2 * i) * 64:(wbase + 2 * i + 2) * 64], q_ap,
                       True, True, ["KnaT", "qnaT"], [PB[bS]], sig=(i == 7))
                bC = nextb()
                reserved.add(bC)
                for j in range(2):
                    mm(bank(bC)[:, j * 64:(j + 1) * 64], KcT[base:base + 64, h // 2, j * 128:(j + 1) * 128], q_ap, True, True,
                       ["KcT", "qnaT"], [PB[bC]], sig=(j == 1))
                return bS, bC
            nxt = na_S(0)
            po = None
            for r in range(16):
                rg, rr = r // 8, r % 8
                if rr == 0:
                    po = nextb()
                    reserved.add(po)
                bS, bC = nxt
                nxt = na_S(r + 1) if r < 15 else None
                par = r % 2
                slt, plt, pct = sls[par], plocs[par], pctxs[par]
                nSL, nPL, nPC = "sl%d" % par, "ploc%d" % par, "pctx%d" % par
                stt("dve", slt[:], bank(bS).rearrange("p (i q) -> p i q", q=64), 0.125, BTt[par][:], ALU.mult, ALU.add,
                    [PB[bS], "BT%d" % par], [nSL])
                reserved.discard(bS)
                tt("dve", slt[:], slt[:], fap(namask, r * 8, [[128, 128], [1, 8], [0, 64]]), ALU.add, [nSL, "namask"], [nSL])
                act(plt[:], slt[:].rearrange("p i q -> p (i q)"), AF.Exp, [nSL], [nPL])
                act(pct[:], bank(bC)[:, 0:128], AF.Exp, [PB[bC]], [nPC], scale=0.125)
                reserved.discard(bC)
                oc = po
                for i in range(8):
                    ch = r // 2 + i
                    if h % 2 == 0:
                        mm(bank(oc)[0:65, rr * 64:(rr + 1) * 64], vle[:, ch, h // 2, :], plt[:, i * 64:(i + 1) * 64], i == 0, False,
                           ["vle", nPL], [PB[oc]])
                    else:
                        mm(bank(oc)[:, rr * 64:(rr + 1) * 64], vlo[:, ch, h // 2, :], plt[:, i * 64:(i + 1) * 64], i == 0, False,
                           ["vlo", nPL], [PB[oc]])
                for j in range(2):
                    if h % 2 == 0:
                        mm(bank(oc)[0:65, rr * 64:(rr + 1) * 64], vce[:, j, h // 2, :], pct[:, j * 64:(j + 1) * 64], False, j == 1,
                           ["vce", nPC], [PB[oc]])
                    else:
                        mm(bank(oc)[:, rr * 64:(rr + 1) * 64], vco[:, j, h // 2, :], pct[:, j * 64:(j + 1) * 64], False, j == 1,
                           ["vco", nPC], [PB[oc]])
                if rr == 7:
                    attn_norm(po, base, 512, mixT[base:base + 64, 2 + h // 2, 512 + rg * 512: 512 + (rg + 1) * 512],
                              64 if h % 2 == 0 else 0, [], MIXN)
                    reserved.discard(po)
        close_scope(scN)
        checkpoint()

        scA5 = open_scope()
        wout = sb("wout", [128, 8, D], BF16)
        lnrow = [sb("lnrow%d" % i, [128, D]) for i in range(2)]
        hold["xnf"] = sb("xnf", [128, D])
        hold["tmpf"] = sb("tmpf", [128, D])
        dma("pool", wout[:], w_out[l].rearrange("(k p) n -> p k n", p=128), [], ["wout"])
        dma("sp", lnrow[0][:], fap(ln1g.tensor, l * D, [[0, 128], [1, D]]), [], ["lnrow0"])
        dma("sp", lnrow[1][:], fap(ln1b.tensor, l * D, [[0, 128], [1, D]]), [], ["lnrow1"])
        for ti in range(NT):
            bl = []
            for half in range(2):
                b = nextb()
                bl.append(b)
                for k in range(8):
                    mm(bank(b), mixT[:, k, ti * 128:(ti + 1) * 128], wout[:, k, half * 512:(half + 1) * 512], k == 0, k == 7,
                       MIXN + ["wout", ACTT[ti]], [PB[b]])
            post_ln(ti, 0, lnrow[0], lnrow[1], bl)
            ln_to_T(ti, 1, 3)
        close_scope(scA5)
        close_scope(scPA)
        checkpoint()

        scF = open_scope()
        w1a = [sb("w1a%d" % i, [128, 8, 256], BF16) for i in range(2)]
        w1g = [sb("w1g%d" % i, [128, 8, 256], BF16) for i in range(2)]
        w2h = sb("w2h", [128, 11, D], BF16)
        uff = sb("uff", [128, 11, NTOK], BF16)
        sa = sb("sa", [128, 512])
        lnrow = [sb("lnrow%d" % i, [128, D]) for i in range(2)]
        xnf = sb("xnf", [128, D])
        tmpf = sb("tmpf", [128, D])
        dma("sp", lnrow[0][:], fap(ln2g.tensor, l * D, [[0, 128], [1, D]]), [], ["lnrow0"])
        dma("sp", lnrow[1][:], fap(ln2b.tensor, l * D, [[0, 128], [1, D]]), [], ["lnrow1"])
        for fh in range(2):
            ch0 = fh * 11
            dma("pool", w2h[:], w_f2[l, ch0 * 128:(ch0 + 11) * 128, :].rearrange("(k p) n -> p k n", p=128), [], ["w2h"])
            for gq in range(6):
                ncq = 2 if gq < 5 else 1
                wa, wg = w1a[gq % 2], w1g[gq % 2]
                wan, wgn = "w1a%d" % (gq % 2), "w1g%d" % (gq % 2)
                f0 = (ch0 + gq * 2) * 128
                dma("pool", wa[:, :, 0:ncq * 128], w_f1[l, :, f0:f0 + ncq * 128].rearrange("(k p) n -> p k n", p=128), [], [wan])
                dma("pool", wg[:, :, 0:ncq * 128], w_f1[l, :, DFF + f0:DFF + f0 + ncq * 128].rearrange("(k p) n -> p k n", p=128), [], [wgn])
                for T in range(3):
                    rdT = ACTT[T * 4:(T + 1) * 4]
                    for c in range(ncq):
                        ba = nextb()
                        bg = nextb()
                        for k in range(8):
                            mm(bank(ba), wa[:, k, c * 128:(c + 1) * 128], actT[:, k, T * 512:(T + 1) * 512], k == 0, k == 7, [wan] + rdT, [PB[ba]])
                        for k in range(8):
                            mm(bank(bg), wg[:, k, c * 128:(c + 1) * 128], actT[:, k, T * 512:(T + 1) * 512], k == 0, k == 7, [wgn] + rdT, [PB[bg]])
                        act(sa[:], bank(ba), AF.Silu, [PB[ba]], ["sa"])
                        tt("dve", uff[:, gq * 2 + c, T * 512:(T + 1) * 512], bank(bg), sa[:], ALU.mult, [PB[bg], "sa"], ["uff"])
            for ti in range(NT):
                xr = "xres%d" % ti
                cond = cond_of(ti)
                bl = [nextb(), nextb()]
                for c in range(11):
                    for half in range(2):
                        mm(bank(bl[half]), uff[:, c, ti * 128:(ti + 1) * 128], w2h[:, c, half * 512:(half + 1) * 512],
                           c == 0, c == 10, ["uff", "w2h"], [PB[bl[half]]])
                if fh == 0:
                    gt = gate[1][cond]
                    for half in range(2):
                        tt("dve", tmpf[:, half * 512:(half + 1) * 512], bank(bl[half]), gt[:, half * 512:(half + 1) * 512], ALU.mult,
                           [PB[bl[half]], "gate1_%d" % cond], ["tmpf"])
                    stt("dve", xres[:, ti, :], xres[:, ti, :], ALPHA, tmpf[:], ALU.mult, ALU.add, [xr, "tmpf"], [xr])
                else:
                    gt = gate[1][cond]
                    for half in range(2):
                        tt("dve", tmpf[:, half * 512:(half + 1) * 512], bank(bl[half]), gt[:, half * 512:(half + 1) * 512], ALU.mult,
                           [PB[bl[half]], "gate1_%d" % cond], ["tmpf"])
                    tt("dve", xres[:, ti, :], xres[:, ti, :], tmpf[:], ALU.add, [xr, "tmpf"], [xr])
                    layernorm_stats(xres[:, ti, :], [xr])
                    act(xnf[:], xres[:, ti, :], AF.Identity, [xr, "rstd", "nmr"], ["xnf"], scale=rstd[:], bias=nmr[:])
                    tt("pool", xnf[:], xnf[:], lnrow[0][:], ALU.mult, ["xnf", "lnrow0"], ["xnf"])
                    tt("pool", xres[:, ti, :], xnf[:], lnrow[1][:], ALU.add, ["xnf", "lnrow1"], [xr])
        close_scope(scF)
        checkpoint()
    except _Stop:
        S.barrier()
        while len(cur) > 1:
            cur.pop().close()

    for ti in range(NT):
        dst = yp[ti * 128:(ti + 1) * 128, :] if ti < 4 else ys[(ti - 4) * 128:(ti - 3) * 128, :]
        dma("sp", dst, xres[:, ti, :], ["xres%d" % ti], [oname()])
    S.wait_all("sp", outs)
    with nc.Block() as block:
        S.run(block)
    es.close()
    return nc


_CACHE = {}


def _consts():
    ident = np.eye(128, dtype=np.float32)
    sp1 = np.tile(np.arange(1, 1025, dtype=np.float32)[None, :], (128, 1))
    rowmask = np.zeros((128, 8), np.float32)
    colmask = np.zeros((128, 8, 128), np.float32)
    for g8 in range(8):
        rowmask[g8 * 16:(g8 + 1) * 16, g8] = 1.0
        colmask[:, g8, g8 * 16:(g8 + 1) * 16] = 1.0
    swap = np.zeros((128, 128), np.float32)
    for p in range(64):
        swap[p, 64 + p] = 1.0
        swap[64 + p, p] = 1.0
    sgn = np.ones((128, 1), np.float32)
    sgn[0:64] = -1.0
    jmat = np.zeros((64, 64), np.float32)
    for q in range(64):
        jmat[63 - q, q] = 1.0
    kc = np.arange(64)[:, None]
    qc = np.arange(64)[None, :]
    cs = np.clip(qc - 8, 0, 48)
    ok = (kc >= cs) & (kc < cs + 16)
    nacol1 = np.where(ok, 0.0, NEG).astype(np.float32)
    nacol = np.concatenate([nacol1, nacol1], axis=0)
    return dict(c_ident=ident, c_sp1=sp1, c_rowmask=rowmask, c_colmask=colmask, c_swap=swap, c_sgn=sgn, jmat=jmat, nacol=nacol)


def _core_consts(q):
    nf = 16
    inv = (np.float32(10000.0) ** (-np.arange(nf, dtype=np.float32) / np.float32(nf))).astype(np.float32)
    t = q * 1024 + np.arange(1024)
    rows = (t // 64).astype(np.float32)
    cols = (t % 64).astype(np.float32)
    ang = np.stack([rows[:, None] * inv[None, :], cols[:, None] * inv[None, :]], axis=1).astype(np.float32)
    cosv = np.cos(ang.astype(np.float64)).astype(np.float32)
    sinv = np.sin(ang.astype(np.float64)).astype(np.float32)
    cos_full = np.stack([cosv, cosv], axis=2)
    sin_sgn = np.stack([-sinv, sinv], axis=2)
    ropec = np.tile(cos_full.reshape(1024, 1, 64), (1, 10, 1)).reshape(1024, 640).astype(np.float32)
    ropes = np.tile(sin_sgn.reshape(1024, 1, 64), (1, 10, 1)).reshape(1024, 640).astype(np.float32)
    R0 = 16 * q
    namask = np.full((128, 16, 8), NEG, np.float32)
    for r in range(16):
        rg = R0 + r
        par = r % 2
        rs = min(max(rg - 4, 0), 56)
        for i in range(8):
            for half in range(2):
                kr = rg - 8 - par + 2 * i + half
                if rs <= kr < rs + 8:
                    namask[half * 64:(half + 1) * 64, r, i] = 0.0
    oh = np.zeros((128, 16), np.float32)
    oh[:, 0 + q] = 1.0
    oh[:, 4 + (3 - q)] = 1.0
    if q > 0:
        oh[:, 8 + (q - 1)] = 1.0
    if q < 3:
        oh[:, 12 + (q + 1)] = 1.0
    return dict(ropec=ropec, ropes=ropes, namask=namask, oh=oh)


def kernel(x_prompt, x_sample, c, cache_na_k, cache_na_v, cache_gqa_k, cache_gqa_v, state_ssm_re, state_ssm_im,
           c_ctx, w_ada, b_ada, w_in, w_out, q_norm_g, k_norm_g, na_bias, ssm_lam_re, ssm_lam_im, ssm_log_dt,
           ssm_b_re, ssm_b_im, ssm_c_re, ssm_c_im, ssm_d, w_ssm_glu, ln1_g, ln1_b, ln2_g, ln2_b,
           w_ffn_in, w_ffn_out):
    f = lambda a: np.ascontiguousarray(np.asarray(a, dtype=np.float32))
    if "nc" not in _CACHE:
        import os
        _CACHE["nc"] = build_program(int(os.environ.get("KSTOP", "99")))
    nc = _CACHE["nc"]
    consts = _consts()
    x_prompt = f(x_prompt); x_sample = f(x_sample); c = f(c); c_ctx = f(c_ctx)
    b_adaT = f(np.transpose(f(b_ada).reshape(DEPTH, 48, 128), (0, 2, 1)))
    g10 = f(np.concatenate([np.tile(f(q_norm_g), (1, 8)), np.tile(f(k_norm_g), (1, 2))], axis=1))

    def pdup(a):
        t = np.transpose(f(a).reshape(DEPTH, 32, 64), (0, 2, 1))
        return f(np.concatenate([t, t], axis=1))
    lamre = pdup(ssm_lam_re); lamim = pdup(ssm_lam_im)
    logdt = f(f(ssm_log_dt).reshape(DEPTH, 32))

    def bl(a):
        return f(np.transpose(f(a), (0, 3, 1, 2, 4)).reshape(DEPTH, 64, 512))

    def cl(a):
        t = f(a).reshape(DEPTH, 2, 2, 8, 16, 64)
        return f(np.transpose(t, (0, 3, 4, 1, 2, 5)).reshape(DEPTH, 128, 4, 64))
    ssmd = f(np.transpose(f(ssm_d).reshape(DEPTH, 2, 128), (0, 2, 1)))
    nab = np.zeros((DEPTH, NABLEN), np.float32)
    nab[:, 128:128 + 4 * 15 * 31] = f(na_bias).reshape(DEPTH, -1)
    common = dict(w_ada=f(w_ada), b_adaT=b_adaT, b_ada=f(b_ada), w_in=f(w_in), w_out=f(w_out), g10=g10,
                  lamre=lamre, lamim=lamim, logdt=logdt, bre=bl(ssm_b_re), bim=bl(ssm_b_im), cre=cl(ssm_c_re), cim=cl(ssm_c_im),
                  ssmd=ssmd, wglu=f(w_ssm_glu), ln1g=f(ln1_g), ln1b=f(ln1_b), ln2g=f(ln2_g), ln2b=f(ln2_b),
                  w_f1=f(w_ffn_in), w_f2=f(w_ffn_out), nab=nab)
    common.update(consts)
    cache_na_k = f(cache_na_k); cache_na_v = f(cache_na_v); cache_gqa_k = f(cache_gqa_k); cache_gqa_v = f(cache_gqa_v)
    state_ssm_re = f(state_ssm_re); state_ssm_im = f(state_ssm_im)
    cc = [_core_consts(q) for q in range(4)]
    in_maps = []
    for core in range(8):
        bs = core // 4
        q = core % 4
        cond = np.stack([c_ctx, c[bs]], axis=0)
        condT = f(np.transpose(cond.reshape(2, 8, 128), (2, 1, 0)))
        m = dict(common)
        m.update(cc[q])
        m["xp"] = f(x_prompt[2 * core:2 * core + 2].reshape(NPT, D))
        m["xs"] = f(x_sample[bs, q * 1024:(q + 1) * 1024])
        m["condT"] = condT
        m["cnak"] = f(cache_na_k[bs].reshape(DEPTH, 256, 256))
        m["cnav"] = f(cache_na_v[bs].reshape(DEPTH, 256, 256))
        m["cgk"] = f(cache_gqa_k[bs].reshape(DEPTH, 256, 128))
        m["cgv"] = f(cache_gqa_v[bs].reshape(DEPTH, 256, 128))
        sre_t = np.transpose(state_ssm_re[bs].reshape(DEPTH, 32, 64), (0, 2, 1))
        sim_t = np.transpose(state_ssm_im[bs].reshape(DEPTH, 32, 64), (0, 2, 1))
        m["h0"] = f(np.concatenate([sre_t, sim_t], axis=1))
        in_maps.append(m)
    res = run_bass_kernel_spmd(nc, in_maps, core_ids=list(range(8)))
    r = res.results
    y_prompt = np.concatenate([r[i]["yp"].reshape(2, SEQ, D) for i in range(8)], axis=0)
    y_sample = np.stack([np.concatenate([r[b * 4 + q]["ys"] for q in range(4)], axis=0) for b in range(2)], axis=0)
    nak = np.concatenate([r[i]["o_nak"].reshape(2, DEPTH, SEQ, 4, 64) for i in range(8)], axis=0)
    nav = np.concatenate([r[i]["o_nav"].reshape(2, DEPTH, SEQ, 4, 64) for i in range(8)], axis=0)
    gk = np.concatenate([r[i]["o_gk"].reshape(2, DEPTH, SEQ, 2, 64) for i in range(8)], axis=0)
    gv = np.concatenate([r[i]["o_gv"].reshape(2, DEPTH, SEQ, 2, 64) for i in range(8)], axis=0)
    sre = np.concatenate([r[i]["o_sre"].reshape(2, DEPTH, 2, 16, 64) for i in range(8)], axis=0)
    sim = np.concatenate([r[i]["o_sim"].reshape(2, DEPTH, 2, 16, 64) for i in range(8)], axis=0)
    return (y_prompt.astype(np.float32), y_sample.astype(np.float32), nak, nav, gk, gv, sre, sim)
```
